# Optimizing a Trainium2 kernel written in Bass

```python
import jax
import jax.numpy as jnp
from jax import lax
import numpy as np

D_MODEL = 2048
BATCH = 2
SEQ = 8192
DEPTH = 4

GRID_W = 64
CTX_LEN = 256
N_MIXERS = 3
N_MOD = 6
D_FF = ((8 * D_MODEL + 3 * 256 - 1) // (3 * 256)) * 256
NORM_EPS = 1e-6

LRU_WIDTH = D_MODEL
LRU_BLOCKS = 16
LRU_BLOCK = LRU_WIDTH // LRU_BLOCKS
LRU_CONV = 4
LRU_C = 8.0

GLA_HEADS = 4
GLA_DK = D_MODEL // 2
GLA_DV = D_MODEL
GLA_HK = GLA_DK // GLA_HEADS
GLA_HV = GLA_DV // GLA_HEADS
GLA_RANK = 16
GLA_TAU = 16.0
GLA_CHUNK = 64

RWKV_N = 64
RWKV_H = D_MODEL // RWKV_N
RWKV_MIX = 6
RWKV_DECAY_LORA = max(32, int(round(1.8 * D_MODEL ** 0.5 / 32)) * 32)
RWKV_AAA_LORA = max(32, int(round(1.8 * D_MODEL ** 0.5 / 32)) * 32)
RWKV_GATE_LORA = max(32, int(round(0.6 * D_MODEL ** 0.8 / 32)) * 32)
RWKV_GN_EPS = 64e-5

kernel_name = 'hybrid_rglru_gla_rwkv7_flow_trunk'


def rmsnorm(x, g):
    xf = x.astype(jnp.float32)
    y = xf * lax.rsqrt(jnp.mean(xf * xf, axis=-1, keepdims=True) + NORM_EPS)
    return (y * g.astype(jnp.float32)).astype(x.dtype)


def modulate(u, shift, scale):
    return u * (1.0 + scale) + shift


def to_col_major(u, rows):
    b, n, d = u.shape
    return u.reshape(b, rows, GRID_W, d).transpose(0, 2, 1, 3).reshape(b, n, d)


def to_row_major(u, rows):
    b, n, d = u.shape
    return u.reshape(b, GRID_W, rows, d).transpose(0, 2, 1, 3).reshape(b, n, d)


def flip_seq(ts, axis):
    return tuple(jnp.flip(t, axis) for t in ts)


def swiglu(u, w_in, w_out):
    gate, up = jnp.split(u @ w_in, 2, axis=-1)
    return (jax.nn.silu(gate) * up) @ w_out


def depthwise_conv_centred(u, w, b):
    k = w.shape[0]
    left = k // 2
    y = lax.conv_general_dilated(
        u, w[:, None, :].astype(u.dtype), window_strides=(1,),
        padding=[(left, k - 1 - left)], dimension_numbers=('NWC', 'WIO', 'NWC'),
        feature_group_count=u.shape[-1])
    return y + b


def split_heads(t, n_heads):
    b, n, f = t.shape
    return t.reshape(b, n, n_heads, f // n_heads).transpose(0, 2, 1, 3).astype(jnp.float32)


def merge_heads(t):
    b, h, n, dh = t.shape
    return t.transpose(0, 2, 1, 3).reshape(b, n, h * dh)


def linear_scan(a, b, h0):
    b = b.at[:, 0].add(a[:, 0] * h0)

    def combine(e1, e2):
        a1, b1 = e1
        a2, b2 = e2
        return a1 * a2, a2 * b1 + b2

    _, h = lax.associative_scan(combine, (a, b), axis=1)
    return h, h[:, -1]


def lru_coeffs(xb, gate_w, gate_b, log_lambda):
    b, n, w = xb.shape
    xf = xb.astype(jnp.float32)
    xblk = xf.reshape(b, n, LRU_BLOCKS, LRU_BLOCK)
    gates = jnp.einsum('blnc,gncd->gblnd', xblk, gate_w.astype(jnp.float32)).reshape(2, b, n, w)
    gates = gates + gate_b.astype(jnp.float32)[:, None, None, :]
    r = jax.nn.sigmoid(gates[0])
    i = jax.nn.sigmoid(gates[1])
    log_a = -LRU_C * r * jax.nn.softplus(-log_lambda.astype(jnp.float32))
    a = jnp.exp(log_a)
    bt = jnp.sqrt(-jnp.expm1(2.0 * log_a)) * (i * xf)
    return a, bt


def rglru_mixer(u_ctx, u_lat, w_in, b_in, conv_w, conv_b, gate_w, gate_b, log_lambda,
                w_out, b_out, ctx_out):
    def project(u, with_gate):
        if with_gate:
            z = u @ w_in + b_in
            gate = jax.nn.gelu(z[..., :LRU_WIDTH], approximate=True)
            rec = z[..., LRU_WIDTH:]
        else:
            gate = None
            rec = u @ w_in[:, LRU_WIDTH:] + b_in[LRU_WIDTH:]
        return gate, depthwise_conv_centred(rec, conv_w, conv_b)

    gate_c, x_c = project(u_ctx, ctx_out)
    gate_l, x_l = project(u_lat, True)
    h_c, h_l = [], []
    for d in range(2):
        coef_c = lru_coeffs(x_c, gate_w[d], gate_b[d], log_lambda[d])
        coef_l = lru_coeffs(x_l, gate_w[d], gate_b[d], log_lambda[d])
        if d == 1:
            coef_c, coef_l = flip_seq(coef_c, 1), flip_seq(coef_l, 1)
        hc, hc_last = linear_scan(coef_c[0], coef_c[1], jnp.zeros_like(coef_c[0][:, 0]))
        hl, _ = linear_scan(coef_l[0], coef_l[1], hc_last)
        h_l.append(jnp.flip(hl, 1) if d == 1 else hl)
        if ctx_out:
            h_c.append(jnp.flip(hc, 1) if d == 1 else hc)
    y_lat = ((h_l[0] + h_l[1]).astype(u_lat.dtype) * gate_l) @ w_out + b_out
    y_ctx = None
    if ctx_out:
        y_ctx = ((h_c[0] + h_c[1]).astype(u_ctx.dtype) * gate_c) @ w_out + b_out
    return y_ctx, y_lat


def gla_chunked(q, k, v, g, s0):
    b, h, n, dk = q.shape
    dv = v.shape[-1]
    nc = n // GLA_CHUNK
    mask = jnp.tril(jnp.ones((GLA_CHUNK, GLA_CHUNK), dtype=bool))

    def chunks(t):
        return jnp.moveaxis(t.reshape(b, h, nc, GLA_CHUNK, t.shape[-1]), 2, 0)

    def step(s, xs):
        qc, kc, vc, gc = xs
        bcum = jnp.cumsum(gc, axis=-2)
        q_e = qc * jnp.exp(bcum)
        k_e = kc * jnp.exp(-bcum)
        att = jnp.where(mask, jnp.einsum('bhik,bhjk->bhij', q_e, k_e), 0.0)
        o = jnp.einsum('bhij,bhjv->bhiv', att, vc) + jnp.einsum('bhik,bhkv->bhiv', q_e, s)
        b_last = bcum[:, :, -1, :]
        k_rem = kc * jnp.exp(b_last[:, :, None, :] - bcum)
        s = jnp.exp(b_last)[..., None] * s + jnp.einsum('bhjk,bhjv->bhkv', k_rem, vc)
        return s, o

    s_last, o = lax.scan(step, s0, (chunks(q), chunks(k), chunks(v), chunks(g)))
    return jnp.moveaxis(o, 0, 2).reshape(b, h, n, dv), s_last


def gla_mixer(u_ctx, u_lat, w_in, b_r, gate_w1, gate_w2, gate_b, norm_g, w_out, ctx_out):
    def project(u):
        q, k, v, r = jnp.split(u @ w_in, [GLA_DK, 2 * GLA_DK, 2 * GLA_DK + GLA_DV], axis=-1)
        g = [split_heads(jax.nn.log_sigmoid(
                ((u @ gate_w1[d]) @ gate_w2[d] + gate_b[d]).astype(jnp.float32)) / GLA_TAU,
                GLA_HEADS) for d in range(2)]
        qkv = (split_heads(q, GLA_HEADS) * GLA_HK ** -0.5,
               split_heads(k, GLA_HEADS), split_heads(v, GLA_HEADS))
        return qkv, g, r

    qkv_c, g_c, r_c = project(u_ctx)
    qkv_l, g_l, r_l = project(u_lat)
    s0 = jnp.zeros((u_lat.shape[0], GLA_HEADS, GLA_HK, GLA_HV), jnp.float32)
    o_c, o_l = [], []
    for d in range(2):
        tc = (qkv_c[0], qkv_c[1], qkv_c[2], g_c[d])
        tl = (qkv_l[0], qkv_l[1], qkv_l[2], g_l[d])
        if d == 1:
            tc, tl = flip_seq(tc, 2), flip_seq(tl, 2)
        oc, sc = gla_chunked(tc[0], tc[1], tc[2], tc[3], s0)
        ol, _ = gla_chunked(tl[0], tl[1], tl[2], tl[3], sc)
        o_l.append(jnp.flip(ol, 2) if d == 1 else ol)
        if ctx_out:
            o_c.append(jnp.flip(oc, 2) if d == 1 else oc)

    def readout(o, r):
        o = o * lax.rsqrt(jnp.mean(o * o, axis=-1, keepdims=True) + NORM_EPS)
        o = merge_heads(o).astype(r.dtype) * norm_g
        return (o * jax.nn.silu(r + b_r)) @ w_out

    y_lat = readout(o_l[0] + o_l[1], r_l)
    y_ctx = readout(o_c[0] + o_c[1], r_c) if ctx_out else None
    return y_ctx, y_lat


def centred_shift(u):
    up = jnp.pad(u, ((0, 0), (1, 1), (0, 0)))
    return 0.5 * (up[:, :-2] + up[:, 2:])


def rwkv_scan(r, w, k, v, aa, bb, s0):
    def step(s, xs):
        r_t, w_t, k_t, v_t, a_t, b_t = xs
        sa = jnp.einsum('bhvk,bhk->bhv', s, a_t)
        s = s * w_t[:, :, None, :] + sa[..., None] * b_t[:, :, None, :] + v_t[..., None] * k_t[:, :, None, :]
        return s, jnp.einsum('bhvk,bhk->bhv', s, r_t)

    xs = tuple(jnp.moveaxis(t, 1, 0) for t in (r, w, k, v, aa, bb))
    s_last, y = lax.scan(step, s0, xs)
    return jnp.moveaxis(y, 0, 1), s_last


def rwkv7_mixer(u_ctx, u_lat, mu, w_rkv, w0, w1, w2, a0, a1, a2, g1, g2, k_k, k_a, r_k,
                ln_w, ln_b, w_out, ctx_out):
    kk_scale = k_k.astype(jnp.float32).reshape(RWKV_H, RWKV_N)
    ka_scale = k_a.astype(jnp.float32).reshape(RWKV_H, RWKV_N)
    rk = r_k.astype(jnp.float32)

    def prep(u):
        b, n, _ = u.shape
        dx = centred_shift(u) - u
        xr, xw, xk, xv, xa, xg = [u + dx * mu[m] for m in range(RWKV_MIX)]

        def heads(t):
            return t.astype(jnp.float32).reshape(b, n, RWKV_H, RWKV_N)

        r = heads(xr @ w_rkv[0])
        k = heads(xk @ w_rkv[1])
        v = heads(xv @ w_rkv[2])
        kk = k * kk_scale
        kk = kk * lax.rsqrt(jnp.maximum(jnp.sum(kk * kk, axis=-1, keepdims=True), 1e-24))
        dirs = []
        for d in range(2):
            w_raw = -jax.nn.softplus(-(w0[d] + jnp.tanh(xw @ w1[d]) @ w2[d]).astype(jnp.float32)) - 0.5
            decay = heads(jnp.exp(-jnp.exp(w_raw)))
            a = heads(jax.nn.sigmoid((a0[d] + (xa @ a1[d]) @ a2[d]).astype(jnp.float32)))
            k_d = k * (1.0 + (a - 1.0) * ka_scale)
            dirs.append((decay, k_d, a))
        return r, v, kk, dirs, xg

    r_c, v_c, kk_c, dirs_c, xg_c = prep(u_ctx)
    r_l, v_l, kk_l, dirs_l, xg_l = prep(u_lat)
    s0 = jnp.zeros((u_lat.shape[0], RWKV_H, RWKV_N, RWKV_N), jnp.float32)
    y_c, y_l = [], []
    for d in range(2):
        dc, kc_d, ac = dirs_c[d]
        dl, kl_d, al = dirs_l[d]
        tc = (r_c, dc, kc_d, v_c, -kk_c, kk_c * ac)
        tl = (r_l, dl, kl_d, v_l, -kk_l, kk_l * al)
        if d == 1:
            tc, tl = flip_seq(tc, 1), flip_seq(tl, 1)
        oc, sc = rwkv_scan(tc[0], tc[1], tc[2], tc[3], tc[4], tc[5], s0)
        ol, _ = rwkv_scan(tl[0], tl[1], tl[2], tl[3], tl[4], tl[5], sc)
        y_l.append(jnp.flip(ol, 1) if d == 1 else ol)
        if ctx_out:
            y_c.append(jnp.flip(oc, 1) if d == 1 else oc)

    def readout(ys, r, v, dirs, xg):
        b, n, _, _ = r.shape
        y = ys[0] + ys[1]
        mean = jnp.mean(y, axis=-1, keepdims=True)
        var = jnp.mean(jnp.square(y - mean), axis=-1, keepdims=True)
        y = ((y - mean) * lax.rsqrt(var + RWKV_GN_EPS)).reshape(b, n, D_MODEL)
        y = y * ln_w.astype(jnp.float32) + ln_b.astype(jnp.float32)
        bonus = (jnp.sum(r * dirs[0][1] * rk, axis=-1, keepdims=True)
                 + jnp.sum(r * dirs[1][1] * rk, axis=-1, keepdims=True)) * v
        y = (y + bonus.reshape(b, n, D_MODEL)).astype(xg.dtype)
        g = jax.nn.sigmoid(xg @ g1) @ g2
        return (y * g) @ w_out

    y_lat = readout(y_l, r_l, v_l, dirs_l, xg_l)
    y_ctx = readout(y_c, r_c, v_c, dirs_c, xg_c) if ctx_out else None
    return y_ctx, y_lat


def ffn_sublayer(h, mods, norm_g, w_in, w_out):
    u = modulate(rmsnorm(h, norm_g[2]), mods[3], mods[4])
    return h + mods[5] * rmsnorm(swiglu(u, w_in, w_out), norm_g[3])


def setup_inputs(seed: int = 0) -> dict:
    key = jax.random.key(seed)
    keys = iter(jax.random.split(key, 48))
    f32 = jnp.float32
    D = D_MODEL
    n_lru = len(range(0, DEPTH, N_MIXERS))
    n_gla = len(range(1, DEPTH, N_MIXERS))
    n_rwkv = len(range(2, DEPTH, N_MIXERS))

    def nrm(shape, scale):
        return scale * jax.random.normal(next(keys), shape, f32)

    def unif(shape, lo, hi):
        return jax.random.uniform(next(keys), shape, f32, lo, hi)

    lam = unif((n_lru, 2, LRU_WIDTH), 0.9, 0.999) ** (1.0 / LRU_C)
    return {
        'x': nrm((BATCH, SEQ, D), 1.0),
        'c': nrm((BATCH, D), 1.0),
        'ctx': nrm((BATCH, CTX_LEN, D), 1.0),
        'c_ctx': nrm((D,), 1.0),
        'ada_w': nrm((DEPTH, D, N_MOD * D), 0.5 * D ** -0.5),
        'ada_b': nrm((DEPTH, N_MOD * D), 0.02),
        'norm_g': 1.0 + nrm((DEPTH, 4, D), 0.05),
        'ffn_w_in': nrm((DEPTH, D, 2 * D_FF), D ** -0.5),
        'ffn_w_out': nrm((DEPTH, D_FF, D), D_FF ** -0.5),
        'lru_w_in': nrm((n_lru, D, 2 * LRU_WIDTH), D ** -0.5),
        'lru_b_in': nrm((n_lru, 2 * LRU_WIDTH), 0.02),
        'lru_conv_w': nrm((n_lru, LRU_CONV, LRU_WIDTH), LRU_CONV ** -0.5),
        'lru_conv_b': nrm((n_lru, LRU_WIDTH), 0.02),
        'lru_gate_w': nrm((n_lru, 2, 2, LRU_BLOCKS, LRU_BLOCK, LRU_BLOCK), LRU_BLOCK ** -0.5),
        'lru_gate_b': nrm((n_lru, 2, 2, LRU_WIDTH), 0.1),
        'lru_log_lambda': jnp.log(lam) - jnp.log1p(-lam),
        'lru_w_out': nrm((n_lru, LRU_WIDTH, D), LRU_WIDTH ** -0.5),
        'lru_b_out': nrm((n_lru, D), 0.02),
        'gla_w_in': nrm((n_gla, D, 2 * GLA_DK + 2 * GLA_DV), D ** -0.5),
        'gla_b_r': nrm((n_gla, GLA_DV), 0.02),
        'gla_gate_w1': nrm((n_gla, 2, D, GLA_RANK), D ** -0.5),
        'gla_gate_w2': nrm((n_gla, 2, GLA_RANK, GLA_DK), GLA_RANK ** -0.5),
        'gla_gate_b': unif((n_gla, 2, GLA_DK), 0.0, 4.0),
        'gla_norm_g': 1.0 + nrm((n_gla, GLA_DV), 0.05),
        'gla_w_out': nrm((n_gla, GLA_DV, D), GLA_DV ** -0.5),
        'rwkv_mu': unif((n_rwkv, RWKV_MIX, D), 0.0, 1.0),
        'rwkv_w_rkv': nrm((n_rwkv, 3, D, D), D ** -0.5),
        'rwkv_w0': unif((n_rwkv, 2, D), -6.5, -1.5),
        'rwkv_w1': nrm((n_rwkv, 2, D, RWKV_DECAY_LORA), D ** -0.5),
        'rwkv_w2': nrm((n_rwkv, 2, RWKV_DECAY_LORA, D), 0.1 * RWKV_DECAY_LORA ** -0.5),
        'rwkv_a0': nrm((n_rwkv, 2, D), 0.1),
        'rwkv_a1': nrm((n_rwkv, 2, D, RWKV_AAA_LORA), D ** -0.5),
        'rwkv_a2': nrm((n_rwkv, 2, RWKV_AAA_LORA, D), RWKV_AAA_LORA ** -0.5),
        'rwkv_g1': nrm((n_rwkv, D, RWKV_GATE_LORA), D ** -0.5),
        'rwkv_g2': nrm((n_rwkv, RWKV_GATE_LORA, D), RWKV_GATE_LORA ** -0.5),
        'rwkv_k_k': 0.85 + nrm((n_rwkv, D), 0.02),
        'rwkv_k_a': 1.0 + nrm((n_rwkv, D), 0.02),
        'rwkv_r_k': nrm((n_rwkv, RWKV_H, RWKV_N), 0.1),
        'rwkv_ln_w': 1.0 + nrm((n_rwkv, D), 0.05),
        'rwkv_ln_b': nrm((n_rwkv, D), 0.02),
        'rwkv_w_out': nrm((n_rwkv, D, D), D ** -0.5),
    }


def reference(x, c, ctx, c_ctx, ada_w, ada_b, norm_g, ffn_w_in, ffn_w_out,
              lru_w_in, lru_b_in, lru_conv_w, lru_conv_b, lru_gate_w, lru_gate_b,
              lru_log_lambda, lru_w_out, lru_b_out,
              gla_w_in, gla_b_r, gla_gate_w1, gla_gate_w2, gla_gate_b, gla_norm_g, gla_w_out,
              rwkv_mu, rwkv_w_rkv, rwkv_w0, rwkv_w1, rwkv_w2, rwkv_a0, rwkv_a1, rwkv_a2,
              rwkv_g1, rwkv_g2, rwkv_k_k, rwkv_k_a, rwkv_r_k, rwkv_ln_w, rwkv_ln_b, rwkv_w_out):
    rows = x.shape[1] // GRID_W
    cond_lat = jax.nn.silu(c)[:, None, :]
    cond_ctx = jax.nn.silu(c_ctx)[None, None, :]
    h_lat, h_ctx = x, ctx
    for i in range(DEPTH):
        ctx_out = i < DEPTH - 1
        kind, j = i % N_MIXERS, i // N_MIXERS
        mods_lat = jnp.split(cond_lat @ ada_w[i] + ada_b[i], N_MOD, axis=-1)
        mods_ctx = jnp.split(cond_ctx @ ada_w[i] + ada_b[i], N_MOD, axis=-1)
        u_lat = modulate(rmsnorm(h_lat, norm_g[i, 0]), mods_lat[0], mods_lat[1])
        u_ctx = modulate(rmsnorm(h_ctx, norm_g[i, 0]), mods_ctx[0], mods_ctx[1])
        col_major = i % 2 == 1
        if col_major:
            u_lat = to_col_major(u_lat, rows)
        if kind == 0:
            y_ctx, y_lat = rglru_mixer(u_ctx, u_lat, lru_w_in[j], lru_b_in[j], lru_conv_w[j],
                                       lru_conv_b[j], lru_gate_w[j], lru_gate_b[j],
                                       lru_log_lambda[j], lru_w_out[j], lru_b_out[j], ctx_out)
        elif kind == 1:
            y_ctx, y_lat = gla_mixer(u_ctx, u_lat, gla_w_in[j], gla_b_r[j], gla_gate_w1[j],
                                     gla_gate_w2[j], gla_gate_b[j], gla_norm_g[j],
                                     gla_w_out[j], ctx_out)
        else:
            y_ctx, y_lat = rwkv7_mixer(u_ctx, u_lat, rwkv_mu[j], rwkv_w_rkv[j], rwkv_w0[j],
                                       rwkv_w1[j], rwkv_w2[j], rwkv_a0[j], rwkv_a1[j],
                                       rwkv_a2[j], rwkv_g1[j], rwkv_g2[j], rwkv_k_k[j],
                                       rwkv_k_a[j], rwkv_r_k[j], rwkv_ln_w[j], rwkv_ln_b[j],
                                       rwkv_w_out[j], ctx_out)
        if col_major:
            y_lat = to_row_major(y_lat, rows)
        h_lat = h_lat + mods_lat[2] * rmsnorm(y_lat, norm_g[i, 1])
        h_lat = ffn_sublayer(h_lat, mods_lat, norm_g[i], ffn_w_in[i], ffn_w_out[i])
        if ctx_out:
            h_ctx = h_ctx + mods_ctx[2] * rmsnorm(y_ctx, norm_g[i, 1])
            h_ctx = ffn_sublayer(h_ctx, mods_ctx, norm_g[i], ffn_w_in[i], ffn_w_out[i])
    return h_lat
```

```python
import numpy as np
from contextlib import ExitStack
import concourse.bass as bass
import concourse.mybir as mybir
from concourse.bass_utils import run_bass_kernel_spmd

F32 = mybir.dt.float32
BF16 = mybir.dt.bfloat16
AF = mybir.ActivationFunctionType
ALU = mybir.AluOpType
AX = mybir.AxisListType

NCORES = 8
NTL = 2
LATC = 8192
CTXC = 256
TCP = 8704
TILES_A = [(512 * i, 512, 1) for i in range(16)] + [(8192, 512, 0)]


class T:
    def __init__(self, t, name=""):
        self.t = t
        self.name = name
        self.w = None
        self.r = {}
        self.psum = False

    def __getitem__(self, idx):
        return self.t[idx]


class KB:
    NSP = 14
    NPOOLQ = 8

    def __init__(self, same_engine_sync=True):
        self.nc = bass.Bass("TRN2", target_bir_lowering=False)
        self.es = ExitStack()
        nc = self.nc
        self.eng = dict(pe=nc.tensor, act=nc.scalar, dve=nc.vector, pool=nc.gpsimd, sp=nc.sync)
        self.sem = {}
        self.cnt = {}
        for e in self.eng:
            self.sem[e] = self.es.enter_context(nc.semaphore("s_" + e))
            self.cnt[e] = 0
        self.dq = {}
        for q, n in (("sp", self.NSP), ("pool", self.NPOOLQ), ("act", 4)):
            lst = []
            for i in range(n):
                k = f"d_{q}{i}"
                self.sem[k] = self.es.enter_context(nc.semaphore(k))
                self.cnt[k] = 0
                lst.append(k)
            self.dq[q] = [lst, 0]
        self.seen = {e: {} for e in self.eng}
        self.cur = {e: e for e in self.eng}
        self.nep = {e: 0 for e in self.eng}
        self.own = {e: e for e in self.eng}
        self.same_engine_sync = same_engine_sync
        self.n_inst = 0
        self.uid = 0

    def sb(self, shape, dt=F32, name=None):
        self.uid += 1
        name = name or f"sb{self.uid}"
        t = self.es.enter_context(self.nc.sbuf_tensor(name, list(shape), dt))
        return T(t, name)

    def ps(self, shape=(128, 512), dt=F32, name=None):
        self.uid += 1
        name = name or f"ps{self.uid}"
        t = self.es.enter_context(self.nc.psum_tensor(name, list(shape), dt))
        tt = T(t, name)
        tt.psum = True
        return tt

    def dram(self, name, shape, dt=F32, kind="Internal"):
        t = self.nc.dram_tensor(name, list(shape), dt, kind=kind)
        return T(t.ap(), name)

    def _wait(self, e, dep):
        k, v = dep
        if self.own.get(k) == e and (e == "pe" or not self.same_engine_sync):
            return
        if self.seen[e].get(k, 0) >= v:
            return
        self.eng[e].wait_ge(self.sem[k], v)
        self.seen[e][k] = v

    def _deps(self, e, reads, writes):
        for t in reads:
            if t.w is not None:
                self._wait(e, t.w)
            if t.psum:
                for k, v in list(t.r.items()):
                    if self.own.get(k) != e:
                        self._wait(e, (k, v))
        for t in writes:
            if t.w is not None:
                self._wait(e, t.w)
            for k, v in t.r.items():
                self._wait(e, (k, v))

    def _mark(self, mark, reads, writes):
        k, v = mark
        for t in reads:
            if t.r.get(k, 0) < v:
                t.r[k] = v
        for t in writes:
            t.w = mark
            t.r = {}

    EPOCH = 12000

    def op(self, e, fn, reads=(), writes=()):
        self._deps(e, reads, writes)
        inst = fn(self.eng[e])
        k = self.cur[e]
        if self.cnt[k] >= self.EPOCH:
            self.nep[e] += 1
            k = f"{e}#{self.nep[e]}"
            self.sem[k] = self.es.enter_context(self.nc.semaphore("s_" + k.replace("#", "_")))
            self.cnt[k] = 0
            self.own[k] = e
            self.cur[e] = k
        self.cnt[k] += 1
        inst.then_inc(self.sem[k], 1)
        self._mark((k, self.cnt[k]), reads, writes)
        self.n_inst += 1
        return inst

    def dma(self, q, out, in_, reads=(), writes=(), **kw):
        lst, i = self.dq[q]
        k = lst[i % len(lst)]
        self.dq[q][1] = i + 1
        if self.cnt[k] > 0:
            self._wait(q, (k, self.cnt[k]))
        self._deps(q, reads, writes)
        inst = self.eng[q].dma_start(out=out, in_=in_, **kw)
        self.cnt[k] += 16
        inst.then_inc(self.sem[k], 16)
        self._mark((k, self.cnt[k]), reads, writes)
        self.n_inst += 1
        return inst

    def finish(self, outs):
        for t in outs:
            if t.w is not None:
                self._wait("sp", t.w)
        for e in ("pe", "act", "dve", "pool"):
            k = self.cur[e]
            if self.cnt[k]:
                self._wait("sp", (k, self.cnt[k]))
        for q in self.dq:
            for k in self.dq[q][0]:
                if self.cnt[k]:
                    self._wait("sp", (k, self.cnt[k]))

    def close(self):
        self.es.close()


def run(kb, in_maps):
    res = run_bass_kernel_spmd(kb.nc, in_maps, core_ids=list(range(len(in_maps))))
    return res.results


def build_ada():
    kb = KB()
    NCH = 48
    cT = kb.dram("cT", [128, 16, 3], F32, kind="ExternalInput")
    w = kb.dram("w", [NCH // 4, 128, 4 * 16 * 128], F32, kind="ExternalInput")
    bia = kb.dram("bia", [128, NCH], F32, kind="ExternalInput")
    out = kb.dram("out", [128, NCH, 3], F32, kind="ExternalOutput")
    c32 = kb.sb([128, 16, 3]); cb = kb.sb([128, 16, 3], BF16)
    bs = kb.sb([128, NCH]); osb = kb.sb([128, NCH, 3])
    wts = [kb.sb([128, 4, 16, 128], BF16) for _ in range(3)]
    pss = [kb.ps([128, 512]) for _ in range(2)]
    kb.dma("sp", c32[:], cT[:], reads=[cT], writes=[c32])
    kb.dma("sp", bs[:], bia[:], reads=[bia], writes=[bs])
    kb.op("act", lambda e: e.activation(out=cb[:], in_=c32[:], func=AF.Silu), reads=[c32], writes=[cb])
    for g in range(NCH // 4):
        wt = wts[g % 3]
        kb.dma("pool", wt[:].rearrange("p a k n -> p (a k n)"), w[g], reads=[w], writes=[wt])
        for a in range(4):
            j = g * 4 + a
            ps = pss[j % 2]
            for kc in range(16):
                kb.op("pe", lambda e: e.matmul(ps[:, 0:3], wt[:, a, kc, :], cb[:, kc, :], start=(kc == 0), stop=(kc == 15)),
                      reads=[wt, cb], writes=[ps])
            kb.op("act", lambda e: e.activation(out=osb[:, j, :], in_=ps[:, 0:3], func=AF.Identity, bias=bs[:, j:j + 1]),
                  reads=[ps, bs], writes=[osb])
    kb.dma("sp", out[:], osb[:], reads=[osb], writes=[out])
    kb.finish([out])
    kb.close()
    return kb

def run_ada(inp):
    cT = np.stack([inp["c"][0], inp["c"][1], inp["c_ctx"]], axis=1)
    cT = np.ascontiguousarray(cT.reshape(16, 128, 3).transpose(1, 0, 2))
    aw = inp["ada_w"].reshape(4, 16, 128, 96, 128)
    aw = aw.transpose(0, 3, 2, 1, 4).reshape(384, 128, 16 * 128)
    ab = inp["ada_b"].reshape(384, 128)
    kb = build_ada()
    maps = []
    for c in range(NCORES):
        wc = aw[c * 48:(c + 1) * 48].reshape(12, 4, 128, 2048).transpose(0, 2, 1, 3).reshape(12, 128, 4 * 2048)
        maps.append({"cT": cT, "w": np.ascontiguousarray(wc), "bia": np.ascontiguousarray(ab[c * 48:(c + 1) * 48].T)})
    res = run(kb, maps)
    o = np.concatenate([r["out"].transpose(1, 0, 2) for r in res], axis=0)
    return o.reshape(4, 12288, 3)


D = 2048
TC = TCP
TILES = TILES_A
EPS = 1e-6


def norm_rstd(kb, ps_ss, TT, rs_t, tmp_t, eps=EPS, n=D):
    kb.op("act", lambda e: e.activation(out=tmp_t[:, :TT], in_=ps_ss[:, :TT], func=AF.Sqrt, scale=1.0 / n, bias=eps_ap(kb, eps)),
          reads=[ps_ss], writes=[tmp_t])
    kb.op("dve", lambda e: e.reciprocal(out=rs_t[:, :TT], in_=tmp_t[:, :TT]), reads=[tmp_t], writes=[rs_t])


_eps_cache = {}
def eps_ap(kb, val):
    key = (id(kb), val)
    if key not in _eps_cache:
        t = kb.sb([128, 1], F32)
        kb.op("pool", lambda e: e.memset(t[:], val), writes=[t])
        _eps_cache[key] = t
    t = _eps_cache[key]
    return t[:, 0:1]


def build_c(has_p2=True):
    kb = KB()
    h = kb.dram("h", [128, 16, TC], kind="ExternalInput")
    p1 = kb.dram("p1", [128, 16, TC], kind="ExternalInput")
    p2 = kb.dram("p2", [128, 16, TC], kind="ExternalInput") if has_p2 else None
    mods = kb.dram("mods", [128, 96, 2], kind="ExternalInput")
    ng = kb.dram("ng", [128, 4, 16], kind="ExternalInput")
    bout = kb.dram("bout", [128, 16], kind="ExternalInput")
    wout = kb.dram("wout", [16, 128, 16 * 128], kind="ExternalInput")
    wg = kb.dram("wg", [44, 128, 16 * 128], kind="ExternalInput")
    wu = kb.dram("wu", [44, 128, 16 * 128], kind="ExternalInput")
    wo = kb.dram("wo", [16, 2, 128, 22 * 128], kind="ExternalInput")
    h2 = kb.dram("h2", [128, 16, TC], kind="ExternalOutput")

    mods_s = kb.sb([128, 96, 2]); ng_s = kb.sb([128, 4, 16]); bout_s = kb.sb([128, 16])
    CG1 = kb.sb([128, 2, 16]); SC2 = kb.sb([128, 2, 16]); SH2 = kb.sb([128, 2, 16]); CG3 = kb.sb([128, 2, 16])
    ones = kb.sb([128, 128], BF16)
    kb.op("pool", lambda e: e.memset(ones[:], 1.0), writes=[ones])
    epsT = eps_ap(kb, EPS)
    kb.dma("sp", mods_s[:], mods[:], reads=[mods], writes=[mods_s])
    kb.dma("sp", ng_s[:], ng[:], reads=[ng], writes=[ng_s])
    kb.dma("sp", bout_s[:], bout[:], reads=[bout], writes=[bout_s])
    for ms in range(2):
        def m(i):
            return mods_s[:, i * 16:(i + 1) * 16, ms]
        kb.op("dve", lambda e: e.tensor_tensor(out=CG1[:, ms, :], in0=m(2), in1=ng_s[:, 1, :], op=ALU.mult), reads=[mods_s, ng_s], writes=[CG1])
        kb.op("dve", lambda e: e.scalar_tensor_tensor(out=SC2[:, ms, :], in0=m(4), scalar=1.0, in1=ng_s[:, 2, :], op0=ALU.add, op1=ALU.mult), reads=[mods_s, ng_s], writes=[SC2])
        kb.op("dve", lambda e: e.tensor_copy(out=SH2[:, ms, :], in_=m(3)), reads=[mods_s], writes=[SH2])
        kb.op("dve", lambda e: e.tensor_tensor(out=CG3[:, ms, :], in0=m(5), in1=ng_s[:, 3, :], op=ALU.mult), reads=[mods_s, ng_s], writes=[CG3])

    hs = kb.sb([128, 16, 512]); y = kb.sb([128, 16, 512]); pb = kb.sb([128, 16, 512], BF16)
    hid = kb.sb([128, 22, 512], BF16)
    sa = [kb.sb([128, 512]) for _ in range(2)]; sb_ = [kb.sb([128, 512]) for _ in range(2)]
    sq = [kb.sb([128, 512], BF16) for _ in range(2)]
    tmp = [kb.sb([128, 512]) for _ in range(2)]
    rs = kb.sb([128, 512]); rtmp = kb.sb([128, 512])
    wt = [kb.sb([128, 16, 128], BF16) for _ in range(2)]
    wgt = [kb.sb([128, 16, 128], BF16) for _ in range(2)]; wut = [kb.sb([128, 16, 128], BF16) for _ in range(2)]
    wot = [kb.sb([128, 22, 128], BF16) for _ in range(2)]
    ps_mm = [kb.ps() for _ in range(2)]; ps_g = [kb.ps() for _ in range(2)]; ps_u = [kb.ps() for _ in range(2)]
    ps_ss = kb.ps()
    ctr = dict(sq=0, tmp=0, mm=0, w=0, gu=0, wo=0, io=0)

    def sumsq_step(src_ap, src_t, TT, first, last):
        s = sq[ctr["sq"] % 2]; ctr["sq"] += 1
        kb.op("act", lambda e: e.activation(out=s[:, :TT], in_=src_ap, func=AF.Square), reads=[src_t], writes=[s])
        kb.op("pe", lambda e: e.matmul(ps_ss[:, :TT], ones[:], s[:, :TT], start=first, stop=last), reads=[ones, s], writes=[ps_ss])

    for (t0, TT, ms) in TILES:
        kb.dma("sp", hs[:, :, :TT], h[:, :, t0:t0 + TT], reads=[h], writes=[hs])
        for kc in range(16):
            a = sa[ctr["io"] % 2]; b = sb_[ctr["io"] % 2]; ctr["io"] += 1
            kb.dma("sp", a[:, :TT], p1[:, kc, t0:t0 + TT], reads=[p1], writes=[a])
            if has_p2:
                kb.dma("sp", b[:, :TT], p2[:, kc, t0:t0 + TT], reads=[p2], writes=[b])
                kb.op("dve", lambda e: e.tensor_tensor(out=pb[:, kc, :TT], in0=a[:, :TT], in1=b[:, :TT], op=ALU.mult), reads=[a, b], writes=[pb])
            else:
                kb.op("dve", lambda e: e.tensor_copy(out=pb[:, kc, :TT], in_=a[:, :TT]), reads=[a], writes=[pb])
        for n in range(16):
            w_ = wt[ctr["w"] % 2]; ctr["w"] += 1
            kb.dma("pool", w_[:].rearrange("p k n -> p (k n)"), wout[n], reads=[wout], writes=[w_])
            ps = ps_mm[ctr["mm"] % 2]; ctr["mm"] += 1
            for kc in range(16):
                kb.op("pe", lambda e: e.matmul(ps[:, :TT], w_[:, kc, :], pb[:, kc, :TT], start=(kc == 0), stop=(kc == 15)), reads=[w_, pb], writes=[ps])
            kb.op("act", lambda e: e.activation(out=y[:, n, :TT], in_=ps[:, :TT], func=AF.Identity, bias=bout_s[:, n:n + 1]), reads=[ps, bout_s], writes=[y])
            sumsq_step(y[:, n, :TT], y, TT, n == 0, n == 15)
        norm_rstd(kb, ps_ss, TT, rs, rtmp)
        for kc in range(16):
            t_ = tmp[ctr["tmp"] % 2]; ctr["tmp"] += 1
            kb.op("dve", lambda e: e.tensor_tensor(out=t_[:, :TT], in0=y[:, kc, :TT], in1=rs[:, :TT], op=ALU.mult), reads=[y, rs], writes=[t_])
            kb.op("dve", lambda e: e.scalar_tensor_tensor(out=hs[:, kc, :TT], in0=t_[:, :TT], scalar=CG1[:, ms, kc:kc + 1], in1=hs[:, kc, :TT], op0=ALU.mult, op1=ALU.add),
                  reads=[t_, CG1, hs], writes=[hs])
            sumsq_step(hs[:, kc, :TT], hs, TT, kc == 0, kc == 15)
        norm_rstd(kb, ps_ss, TT, rs, rtmp)
        for kc in range(16):
            t_ = tmp[ctr["tmp"] % 2]; ctr["tmp"] += 1
            kb.op("dve", lambda e: e.tensor_tensor(out=t_[:, :TT], in0=hs[:, kc, :TT], in1=rs[:, :TT], op=ALU.mult), reads=[hs, rs], writes=[t_])
            kb.op("act", lambda e: e.activation(out=pb[:, kc, :TT], in_=t_[:, :TT], func=AF.Identity, scale=SC2[:, ms, kc:kc + 1], bias=SH2[:, ms, kc:kc + 1]),
                  reads=[t_, SC2, SH2], writes=[pb])
        for half in range(2):
            for jj in range(22):
                j = half * 22 + jj
                g_ = wgt[ctr["gu"] % 2]; u_ = wut[ctr["gu"] % 2]
                pg = ps_g[ctr["gu"] % 2]; pu = ps_u[ctr["gu"] % 2]; ctr["gu"] += 1
                kb.dma("pool", g_[:].rearrange("p k n -> p (k n)"), wg[j], reads=[wg], writes=[g_])
                kb.dma("pool", u_[:].rearrange("p k n -> p (k n)"), wu[j], reads=[wu], writes=[u_])
                for kc in range(16):
                    kb.op("pe", lambda e: e.matmul(pg[:, :TT], g_[:, kc, :], pb[:, kc, :TT], start=(kc == 0), stop=(kc == 15)), reads=[g_, pb], writes=[pg])
                for kc in range(16):
                    kb.op("pe", lambda e: e.matmul(pu[:, :TT], u_[:, kc, :], pb[:, kc, :TT], start=(kc == 0), stop=(kc == 15)), reads=[u_, pb], writes=[pu])
                t_ = tmp[ctr["tmp"] % 2]; ctr["tmp"] += 1
                kb.op("act", lambda e: e.activation(out=t_[:, :TT], in_=pg[:, :TT], func=AF.Silu), reads=[pg], writes=[t_])
                kb.op("dve", lambda e: e.tensor_tensor(out=hid[:, jj, :TT], in0=t_[:, :TT], in1=pu[:, :TT], op=ALU.mult), reads=[t_, pu], writes=[hid])
            for n in range(16):
                w_ = wot[ctr["wo"] % 2]; ctr["wo"] += 1
                kb.dma("pool", w_[:].rearrange("p k n -> p (k n)"), wo[n, half], reads=[wo], writes=[w_])
                ps = ps_mm[ctr["mm"] % 2]; ctr["mm"] += 1
                for jj in range(22):
                    kb.op("pe", lambda e: e.matmul(ps[:, :TT], w_[:, jj, :], hid[:, jj, :TT], start=(jj == 0), stop=(jj == 21)), reads=[w_, hid], writes=[ps])
                if half == 0:
                    kb.op("act", lambda e: e.activation(out=y[:, n, :TT], in_=ps[:, :TT], func=AF.Identity), reads=[ps], writes=[y])
                else:
                    kb.op("dve", lambda e: e.tensor_tensor(out=y[:, n, :TT], in0=y[:, n, :TT], in1=ps[:, :TT], op=ALU.add), reads=[ps, y], writes=[y])
                    sumsq_step(y[:, n, :TT], y, TT, n == 0, n == 15)
        norm_rstd(kb, ps_ss, TT, rs, rtmp)
        for kc in range(16):
            t_ = tmp[ctr["tmp"] % 2]; ctr["tmp"] += 1
            kb.op("dve", lambda e: e.tensor_tensor(out=t_[:, :TT], in0=y[:, kc, :TT], in1=rs[:, :TT], op=ALU.mult), reads=[y, rs], writes=[t_])
            kb.op("dve", lambda e: e.scalar_tensor_tensor(out=hs[:, kc, :TT], in0=t_[:, :TT], scalar=CG3[:, ms, kc:kc + 1], in1=hs[:, kc, :TT], op0=ALU.mult, op1=ALU.add),
                  reads=[t_, CG3, hs], writes=[hs])
        kb.dma("sp", h2[:, :, t0:t0 + TT], hs[:, :, :TT], reads=[hs], writes=[h2])
    kb.finish([h2])
    kb.close()
    return kb


def fm(xT):
    n = xT.shape[0] // 128
    return np.ascontiguousarray(xT.reshape(n, 128, -1).transpose(1, 0, 2))


def prep_c_weights(w_out, ffn_w_in, ffn_w_out):
    wout = np.ascontiguousarray(w_out.reshape(16, 128, 16, 128).transpose(2, 1, 0, 3).reshape(16, 128, 2048))
    wi = ffn_w_in.reshape(16, 128, 88, 128).transpose(2, 1, 0, 3).reshape(88, 128, 2048)
    wg = np.ascontiguousarray(wi[:44]); wu = np.ascontiguousarray(wi[44:])
    wo = ffn_w_out.reshape(2, 22, 128, 16, 128).transpose(3, 0, 2, 1, 4).reshape(16, 2, 128, 22 * 128)
    return dict(wout=wout, wg=wg, wu=wu, wo=np.ascontiguousarray(wo))


def premod_setup(kb, mods, ng, mi_shift, mi_scale, gi):
    mods_s = kb.sb([128, 96, 2]); ng_s = kb.sb([128, 4, 16])
    kb.dma("sp", mods_s[:], mods[:], reads=[mods], writes=[mods_s])
    kb.dma("sp", ng_s[:], ng[:], reads=[ng], writes=[ng_s])
    SC = kb.sb([128, 2, 16]); SH = kb.sb([128, 2, 16])
    for ms in range(2):
        kb.op("dve", lambda e: e.scalar_tensor_tensor(out=SC[:, ms, :], in0=mods_s[:, mi_scale * 16:(mi_scale + 1) * 16, ms], scalar=1.0, in1=ng_s[:, gi, :], op0=ALU.add, op1=ALU.mult),
              reads=[mods_s, ng_s], writes=[SC])
        kb.op("dve", lambda e: e.tensor_copy(out=SH[:, ms, :], in_=mods_s[:, mi_shift * 16:(mi_shift + 1) * 16, ms]), reads=[mods_s], writes=[SH])
    return SC, SH


class PreMod:
    def __init__(self, kb, h, mods, ng, out_dt=BF16, hs_w=512):
        self.kb = kb; self.h = h
        self.SC, self.SH = premod_setup(kb, mods, ng, 0, 1, 0)
        self.hs = kb.sb([128, 16, hs_w]); self.u = kb.sb([128, 16, 512], out_dt)
        self.sq = [kb.sb([128, 512], BF16) for _ in range(2)]
        self.tmp = [kb.sb([128, 512]) for _ in range(2)]
        self.rs = kb.sb([128, 512]); self.rtmp = kb.sb([128, 512])
        self.ones = kb.sb([128, 128], BF16)
        kb.op("pool", lambda e: e.memset(self.ones[:], 1.0), writes=[self.ones])
        self.ps_ss = kb.ps()
        self.c = 0

    def tile(self, t0, TT, ms):
        kb = self.kb; hs = self.hs
        kb.dma("sp", hs[:, :, :TT], self.h[:, :, t0:t0 + TT], reads=[self.h], writes=[hs])
        for kc in range(16):
            s = self.sq[self.c % 2]; self.c += 1
            kb.op("act", lambda e: e.activation(out=s[:, :TT], in_=hs[:, kc, :TT], func=AF.Square), reads=[hs], writes=[s])
            kb.op("pe", lambda e: e.matmul(self.ps_ss[:, :TT], self.ones[:], s[:, :TT], start=(kc == 0), stop=(kc == 15)), reads=[self.ones, s], writes=[self.ps_ss])
        norm_rstd(kb, self.ps_ss, TT, self.rs, self.rtmp)
        for kc in range(16):
            t_ = self.tmp[self.c % 2]; self.c += 1
            kb.op("dve", lambda e: e.tensor_tensor(out=t_[:, :TT], in0=hs[:, kc, :TT], in1=self.rs[:, :TT], op=ALU.mult), reads=[hs, self.rs], writes=[t_])
            kb.op("act", lambda e: e.activation(out=self.u[:, kc, :TT], in_=t_[:, :TT], func=AF.Identity, scale=self.SC[:, ms, kc:kc + 1], bias=self.SH[:, ms, kc:kc + 1]),
                  reads=[t_, self.SC, self.SH], writes=[self.u])
        return self.u


def build_lru_a():
    kb = KB()
    h = kb.dram("h", [128, 16, TCP], kind="ExternalInput")
    mods = kb.dram("mods", [128, 96, 2], kind="ExternalInput")
    ng = kb.dram("ng", [128, 4, 16], kind="ExternalInput")
    w = kb.dram("w", [32, 128, 16 * 128], kind="ExternalInput")
    bin_ = kb.dram("bin", [128, 32], kind="ExternalInput")
    gate = kb.dram("gate", [128, 16, TCP], kind="ExternalOutput")
    rec = kb.dram("rec", [128, 16, TCP], kind="ExternalOutput")
    pm = PreMod(kb, h, mods, ng)
    b_s = kb.sb([128, 32])
    kb.dma("sp", b_s[:], bin_[:], reads=[bin_], writes=[b_s])
    wt = [kb.sb([128, 16, 128], BF16) for _ in range(2)]
    pss = [kb.ps() for _ in range(2)]
    z = [kb.sb([128, 512]) for _ in range(2)]; z2 = [kb.sb([128, 512]) for _ in range(2)]
    o = [kb.sb([128, 512]) for _ in range(3)]
    i = 0
    for (t0, TT, ms) in TILES_A:
        u = pm.tile(t0, TT, ms)
        for n in range(32):
            w_ = wt[i % 2]; ps = pss[i % 2]; zz = z[i % 2]; zq = z2[i % 2]; oo = o[i % 3]; i += 1
            kb.dma("pool", w_[:].rearrange("p k n -> p (k n)"), w[n], reads=[w], writes=[w_])
            for kc in range(16):
                kb.op("pe", lambda e: e.matmul(ps[:, :TT], w_[:, kc, :], u[:, kc, :TT], start=(kc == 0), stop=(kc == 15)), reads=[w_, u], writes=[ps])
            if n < 16:
                kb.op("act", lambda e: e.activation(out=zz[:, :TT], in_=ps[:, :TT], func=AF.Identity, bias=b_s[:, n:n + 1]), reads=[ps, b_s], writes=[zz])
                kb.op("act", lambda e: e.activation(out=zq[:, :TT], in_=ps[:, :TT], func=AF.Square, bias=b_s[:, n:n + 1]), reads=[ps, b_s], writes=[zq])
                kb.op("dve", lambda e: e.tensor_scalar(out=zq[:, :TT], in0=zq[:, :TT], scalar1=0.044715, scalar2=1.0, op0=ALU.mult, op1=ALU.add), reads=[zq], writes=[zq])
                kb.op("dve", lambda e: e.tensor_tensor(out=zq[:, :TT], in0=zq[:, :TT], in1=zz[:, :TT], op=ALU.mult), reads=[zq, zz], writes=[zq])
                kb.op("act", lambda e: e.activation(out=zq[:, :TT], in_=zq[:, :TT], func=AF.Sigmoid, scale=1.5957691216057308), reads=[zq], writes=[zq])
                kb.op("dve", lambda e: e.tensor_tensor(out=oo[:, :TT], in0=zq[:, :TT], in1=zz[:, :TT], op=ALU.mult), reads=[zq, zz], writes=[oo])
                kb.dma("sp", gate[:, n, t0:t0 + TT], oo[:, :TT], reads=[oo], writes=[gate])
            else:
                kb.op("act", lambda e: e.activation(out=oo[:, :TT], in_=ps[:, :TT], func=AF.Identity, bias=b_s[:, n:n + 1]), reads=[ps, b_s], writes=[oo])
                kb.dma("sp", rec[:, n - 16, t0:t0 + TT], oo[:, :TT], reads=[oo], writes=[rec])
    kb.finish([gate, rec])
    kb.close()
    return kb


NS = 8448
CTXN = 256
OFF_C = 2
OFF_L = 261
BUFW = 8454


def build_lru_b():
    kb = KB()
    rec = kb.dram("rec", [2, 2, 128, NS], kind="ExternalInput")
    cw = kb.dram("cw", [128, 2, 4], kind="ExternalInput")
    cb = kb.dram("cb", [128, 2], kind="ExternalInput")
    gw = kb.dram("gw", [128, 2, 2, 2, 128], kind="ExternalInput")
    gb = kb.dram("gb", [128, 2, 2, 2], kind="ExternalInput")
    ll = kb.dram("ll", [128, 2, 2], kind="ExternalInput")
    hsum = kb.dram("hsum", [2, 2, 128, NS], kind="ExternalOutput")
    cw_s = kb.sb([128, 2, 4]); cb_s = kb.sb([128, 2]); gw_s = kb.sb([128, 2, 2, 2, 128]); gb_s = kb.sb([128, 2, 2, 2]); ll_s = kb.sb([128, 2, 2])
    for s, d in ((cw_s, cw), (cb_s, cb), (gw_s, gw), (gb_s, gb), (ll_s, ll)):
        kb.dma("sp", s[:], d[:], reads=[d], writes=[s])
    one_ap = eps_ap(kb, 1.0)
    cl = kb.sb([128, 2, 2]); cl_t = kb.sb([128, 2, 2])
    kb.op("act", lambda e: e.activation(out=cl_t[:], in_=ll_s[:], func=AF.Exp, scale=-1.0), reads=[ll_s], writes=[cl_t])
    kb.op("act", lambda e: e.activation(out=cl_t[:], in_=cl_t[:], func=AF.Ln, bias=one_ap), reads=[cl_t], writes=[cl_t])
    kb.op("dve", lambda e: e.tensor_scalar(out=cl[:], in0=cl_t[:], scalar1=-8.0, scalar2=None, op0=ALU.mult), reads=[cl_t], writes=[cl])

    buf = kb.sb([128, BUFW]); x = kb.sb([128, NS]); a = kb.sb([128, NS]); bt = kb.sb([128, NS]); o0 = kb.sb([128, NS])
    o1 = buf
    pss = [kb.ps() for _ in range(4)]
    tr = [kb.sb([128, 512]) for _ in range(2)]; ti = [kb.sb([128, 512]) for _ in range(2)]; tq = [kb.sb([128, 512]) for _ in range(2)]
    ttiles = [(0, 256)] + [(256 + 512 * i, 512) for i in range(16)]
    segs = [(0, CTXN, OFF_C), (CTXN, NS - CTXN, OFF_L)]
    pieces = [(0, 256)] + [(256 + 2048 * i, 2048) for i in range(4)]
    c = 0
    for blk in range(2):
        for b in range(2):
            kb.op("pool", lambda e: e.memset(buf[:, 0:OFF_C], 0.0), writes=[buf])
            kb.op("pool", lambda e: e.memset(buf[:, OFF_C + CTXN:OFF_L], 0.0), writes=[buf])
            kb.op("pool", lambda e: e.memset(buf[:, OFF_L + NS - CTXN:BUFW], 0.0), writes=[buf])
            kb.dma("sp", buf[:, OFF_C:OFF_C + CTXN], rec[blk, b, :, 0:CTXN], reads=[rec], writes=[buf])
            kb.dma("sp", buf[:, OFF_L:OFF_L + NS - CTXN], rec[blk, b, :, CTXN:NS], reads=[rec], writes=[buf])
            for (s0, sl, off) in segs:
                kb.op("act", lambda e: e.activation(out=x[:, s0:s0 + sl], in_=buf[:, off - 2:off - 2 + sl], func=AF.Identity, scale=cw_s[:, blk, 0:1], bias=cb_s[:, blk:blk + 1]),
                      reads=[buf, cw_s, cb_s], writes=[x])
                for j in range(1, 4):
                    kb.op("dve", lambda e: e.scalar_tensor_tensor(out=x[:, s0:s0 + sl], in0=buf[:, off - 2 + j:off - 2 + j + sl], scalar=cw_s[:, blk, j:j + 1], in1=x[:, s0:s0 + sl], op0=ALU.mult, op1=ALU.add),
                          reads=[buf, cw_s, x], writes=[x])
            for d in range(2):
                for (t0, TT) in ttiles:
                    pr = pss[c % 2]; pi = pss[2 + c % 2]; r_ = tr[c % 2]; i_ = ti[c % 2]; q_ = tq[c % 2]; c += 1
                    kb.op("pe", lambda e: e.matmul(pr[:, :TT], gw_s[:, blk, d, 0, :], x[:, t0:t0 + TT], start=True, stop=True), reads=[gw_s, x], writes=[pr])
                    kb.op("pe", lambda e: e.matmul(pi[:, :TT], gw_s[:, blk, d, 1, :], x[:, t0:t0 + TT], start=True, stop=True), reads=[gw_s, x], writes=[pi])
                    kb.op("act", lambda e: e.activation(out=r_[:, :TT], in_=pr[:, :TT], func=AF.Sigmoid, bias=gb_s[:, blk, d, 0:1]), reads=[pr, gb_s], writes=[r_])
                    kb.op("act", lambda e: e.activation(out=i_[:, :TT], in_=pi[:, :TT], func=AF.Sigmoid, bias=gb_s[:, blk, d, 1:2]), reads=[pi, gb_s], writes=[i_])
                    kb.op("act", lambda e: e.activation(out=a[:, t0:t0 + TT], in_=r_[:, :TT], func=AF.Exp, scale=cl[:, blk, d:d + 1]), reads=[r_, cl], writes=[a])
                    kb.op("act", lambda e: e.activation(out=q_[:, :TT], in_=a[:, t0:t0 + TT], func=AF.Square), reads=[a], writes=[q_])
                    kb.op("dve", lambda e: e.tensor_scalar(out=q_[:, :TT], in0=q_[:, :TT], scalar1=-1.0, scalar2=1.0, op0=ALU.mult, op1=ALU.add), reads=[q_], writes=[q_])
                    kb.op("act", lambda e: e.activation(out=q_[:, :TT], in_=q_[:, :TT], func=AF.Sqrt), reads=[q_], writes=[q_])
                    kb.op("dve", lambda e: e.tensor_tensor(out=i_[:, :TT], in0=i_[:, :TT], in1=x[:, t0:t0 + TT], op=ALU.mult), reads=[i_, x], writes=[i_])
                    kb.op("dve", lambda e: e.tensor_tensor(out=bt[:, t0:t0 + TT], in0=i_[:, :TT], in1=q_[:, :TT], op=ALU.mult), reads=[i_, q_], writes=[bt])
                if d == 0:
                    prev = None
                    for (p0, pl) in pieces:
                        init = 0.0 if prev is None else o0[:, prev - 1:prev]
                        kb.op("dve", lambda e: e.tensor_tensor_scan(out=o0[:, p0:p0 + pl], data0=a[:, p0:p0 + pl], data1=bt[:, p0:p0 + pl], initial=init, op0=ALU.mult, op1=ALU.add),
                              reads=[a, bt, o0], writes=[o0])
                        prev = p0 + pl
                else:
                    prev = None
                    for (p0, pl) in [pieces[0]] + pieces[:0:-1]:
                        init = 0.0 if prev is None else o1[:, prev:prev + 1]
                        kb.op("dve", lambda e: e.tensor_tensor_scan(out=o1[:, p0:p0 + pl][:, ::-1], data0=a[:, p0:p0 + pl][:, ::-1], data1=bt[:, p0:p0 + pl][:, ::-1], initial=init, op0=ALU.mult, op1=ALU.add),
                              reads=[a, bt, o1], writes=[o1])
                        prev = p0
            for (p0, pl) in pieces:
                kb.op("pool", lambda e: e.tensor_tensor(out=o0[:, p0:p0 + pl], in0=o0[:, p0:p0 + pl], in1=o1[:, p0:p0 + pl], op=ALU.add), reads=[o0, o1], writes=[o0])
            kb.dma("sp", hsum[blk, b], o0[:, :], reads=[o0], writes=[hsum])
    kb.finish([hsum])
    kb.close()
    return kb


def build_gla_a():
    kb = KB()
    h = kb.dram("h", [128, 16, TCP], kind="ExternalInput")
    mods = kb.dram("mods", [128, 96, 2], kind="ExternalInput")
    ng = kb.dram("ng", [128, 4, 16], kind="ExternalInput")
    w = kb.dram("w", [48, 128, 16 * 128], kind="ExternalInput")
    w1 = kb.dram("w1", [128, 16 * 32], kind="ExternalInput")
    br = kb.dram("br", [128, 16], kind="ExternalInput")
    qkv = kb.dram("qkv", [128, 32, TCP], kind="ExternalOutput")
    sr = kb.dram("sr", [128, 16, TCP], kind="ExternalOutput")
    gl = kb.dram("gl", [32, TCP], kind="ExternalOutput")
    pm = PreMod(kb, h, mods, ng)
    br_s = kb.sb([128, 16]); w1_s = kb.sb([128, 16, 32], BF16)
    kb.dma("sp", br_s[:], br[:], reads=[br], writes=[br_s])
    kb.dma("pool", w1_s[:].rearrange("p k n -> p (k n)"), w1[:], reads=[w1], writes=[w1_s])
    wt = [kb.sb([128, 16, 128], BF16) for _ in range(2)]
    pss = [kb.ps() for _ in range(2)]
    o = [kb.sb([128, 512]) for _ in range(3)]
    i = 0
    for (t0, TT, ms) in TILES_A:
        u = pm.tile(t0, TT, ms)
        ps = pss[i % 2]; oo = o[i % 3]; i += 1
        for kc in range(16):
            kb.op("pe", lambda e: e.matmul(ps[0:32, :TT], w1_s[:, kc, :], u[:, kc, :TT], start=(kc == 0), stop=(kc == 15)), reads=[w1_s, u], writes=[ps])
        kb.op("act", lambda e: e.activation(out=oo[0:32, :TT], in_=ps[0:32, :TT], func=AF.Identity), reads=[ps], writes=[oo])
        kb.dma("sp", gl[:, t0:t0 + TT], oo[0:32, :TT], reads=[oo], writes=[gl])
        for n in range(48):
            w_ = wt[i % 2]; ps = pss[i % 2]; oo = o[i % 3]; i += 1
            kb.dma("pool", w_[:].rearrange("p k n -> p (k n)"), w[n], reads=[w], writes=[w_])
            for kc in range(16):
                kb.op("pe", lambda e: e.matmul(ps[:, :TT], w_[:, kc, :], u[:, kc, :TT], start=(kc == 0), stop=(kc == 15)), reads=[w_, u], writes=[ps])
            if n < 32:
                sc = 1.0 / 16.0 if n < 8 else 1.0
                kb.op("act", lambda e: e.activation(out=oo[:, :TT], in_=ps[:, :TT], func=AF.Identity, scale=sc), reads=[ps], writes=[oo])
                kb.dma("sp", qkv[:, n, t0:t0 + TT], oo[:, :TT], reads=[oo], writes=[qkv])
            else:
                kb.op("act", lambda e: e.activation(out=oo[:, :TT], in_=ps[:, :TT], func=AF.Silu, bias=br_s[:, n - 32:n - 31]), reads=[ps, br_s], writes=[oo])
                kb.dma("sp", sr[:, n - 32, t0:t0 + TT], oo[:, :TT], reads=[oo], writes=[sr])
    kb.finish([qkv, sr, gl])
    kb.close()
    return kb


NCH = NS // 128


def build_gla_b():
    kb = KB()
    qT = kb.dram("qT", [128, 2, NS], kind="ExternalInput")
    kT = kb.dram("kT", [128, 2, NS], kind="ExternalInput")
    ktok = kb.dram("ktok", [NS, 256], kind="ExternalInput")
    vtok = kb.dram("vtok", [NS, 512], kind="ExternalInput")
    glT = kb.dram("glT", [16, 2, NS], kind="ExternalInput")
    w2 = kb.dram("w2", [16, 2, 256], kind="ExternalInput")
    gbF = kb.dram("gbF", [128, 2, 2], kind="ExternalInput")
    gbT = kb.dram("gbT", [128, 2, 256], kind="ExternalInput")
    ngT = kb.dram("ngT", [128, 512], kind="ExternalInput")
    masks = kb.dram("masks", [128, 2, 2, 128], kind="ExternalInput")
    rmask = kb.dram("rmask", [128, 512], kind="ExternalInput")
    o0 = kb.dram("o0", [NS, 512])
    p1 = kb.dram("p1", [NS, 512], kind="ExternalOutput")

    w2_s = kb.sb([16, 2, 256]); gbF_s = kb.sb([128, 2, 2]); gbT_s = kb.sb([128, 2, 256]); ngT_s = kb.sb([128, 512])
    masks_s = kb.sb([128, 2, 2, 128]); rmask_s = kb.sb([128, 512])
    for s, d in ((w2_s, w2), (gbF_s, gbF), (gbT_s, gbT), (ngT_s, ngT), (masks_s, masks), (rmask_s, rmask)):
        kb.dma("sp", s[:], d[:], reads=[d], writes=[s])
    S = [kb.sb([128, 512]) for _ in range(2)]; Sb = [kb.sb([128, 512], BF16) for _ in range(2)]
    q_g = kb.sb([128, 2, 512]); k_g = kb.sb([128, 2, 512]); gl_g = kb.sb([16, 512])
    gF = kb.sb([128, 2, 512]); bc = kb.sb([128, 2, 512]); Eq = kb.sb([128, 2, 512]); Ek = kb.sb([128, 2, 512])
    qe = kb.sb([128, 2, 512], BF16); ke = kb.sb([128, 2, 512], BF16)
    kt = [kb.sb([128, 256]) for _ in range(2)]; vt = [kb.sb([128, 512], BF16) for _ in range(2)]
    gt = [kb.sb([128, 256]) for _ in range(2)]; krem = [kb.sb([128, 256], BF16) for _ in range(2)]
    attb = [kb.sb([128, 128], BF16) for _ in range(2)]
    ot = [kb.sb([128, 512]) for _ in range(2)]; o0t = [kb.sb([128, 512]) for _ in range(2)]; junk = kb.sb([128, 512])
    ssq = [kb.sb([128, 1]) for _ in range(2)]
    ps_g = kb.ps(); ps_t = kb.ps(); ps_a = kb.ps(); ps_o = [kb.ps() for _ in range(2)]; ps_s = [kb.ps() for _ in range(2)]
    eps_t = eps_ap(kb, EPS)
    ci = 0
    for d in range(2):
        for kc in range(2):
            kb.op("dve", lambda e: e.memset(S[kc][:], 0.0), writes=[S[kc]])
            kb.op("dve", lambda e: e.memset(Sb[kc][:], 0.0), writes=[Sb[kc]])
        groups = [(0, 2)] + [(2 + 4 * g, 4) for g in range(16)]
        if d == 1:
            groups = [groups[0]] + groups[:0:-1]
        for (c0, nc_) in groups:
            t0 = c0 * 128; TT = nc_ * 128
            kb.dma("sp", q_g[:, :, :TT], qT[:, :, t0:t0 + TT], reads=[qT], writes=[q_g])
            kb.dma("sp", k_g[:, :, :TT], kT[:, :, t0:t0 + TT], reads=[kT], writes=[k_g])
            kb.dma("sp", gl_g[:, :TT], glT[:, d, t0:t0 + TT], reads=[glT], writes=[gl_g])
            for kc in range(2):
                kb.op("pe", lambda e: e.matmul(ps_g[:, :TT], w2_s[:, d, kc * 128:(kc + 1) * 128], gl_g[:, :TT], start=True, stop=True), reads=[w2_s, gl_g], writes=[ps_g])
                kb.op("act", lambda e: e.activation(out=gF[:, kc, :TT], in_=ps_g[:, :TT], func=AF.Sigmoid, bias=gbF_s[:, d, kc:kc + 1]), reads=[ps_g, gbF_s], writes=[gF])
            kb.op("act", lambda e: e.activation(out=gF[:, :, :TT], in_=gF[:, :, :TT], func=AF.Ln), reads=[gF], writes=[gF])
            kb.op("dve", lambda e: e.tensor_scalar(out=gF[:, :, :TT], in0=gF[:, :, :TT], scalar1=1.0 / 16.0, scalar2=None, op0=ALU.mult), reads=[gF], writes=[gF])
            for kc in range(2):
                if d == 0:
                    kb.op("dve", lambda e: e.tensor_tensor_scan(out=bc[:, kc, :TT], data0=rmask_s[:, :TT], data1=gF[:, kc, :TT], initial=0.0, op0=ALU.mult, op1=ALU.add),
                          reads=[rmask_s, gF], writes=[bc])
                else:
                    kb.op("dve", lambda e: e.tensor_tensor_scan(out=bc[:, kc, :TT][:, ::-1], data0=rmask_s[:, :TT], data1=gF[:, kc, :TT][:, ::-1], initial=0.0, op0=ALU.mult, op1=ALU.add),
                          reads=[rmask_s, gF], writes=[bc])
            kb.op("act", lambda e: e.activation(out=Eq[:, :, :TT], in_=bc[:, :, :TT], func=AF.Exp), reads=[bc], writes=[Eq])
            kb.op("act", lambda e: e.activation(out=Ek[:, :, :TT], in_=bc[:, :, :TT], func=AF.Exp, scale=-1.0), reads=[bc], writes=[Ek])
            kb.op("dve", lambda e: e.tensor_tensor(out=qe[:, :, :TT], in0=q_g[:, :, :TT], in1=Eq[:, :, :TT], op=ALU.mult), reads=[q_g, Eq], writes=[qe])
            kb.op("dve", lambda e: e.tensor_tensor(out=ke[:, :, :TT], in0=k_g[:, :, :TT], in1=Ek[:, :, :TT], op=ALU.mult), reads=[k_g, Ek], writes=[ke])
            chunks = list(range(nc_)) if d == 0 else list(range(nc_ - 1, -1, -1))
            for cc in chunks:
                c = c0 + cc; a0 = cc * 128; tt0 = c * 128
                kt_ = kt[ci % 2]; vt_ = vt[ci % 2]; gt_ = gt[ci % 2]; kr_ = krem[ci % 2]; ab_ = attb[ci % 2]
                ot_ = ot[ci % 2]; o0_ = o0t[ci % 2]; pso = ps_o[ci % 2]; ss_ = ssq[ci % 2]; ci += 1
                kb.dma("sp", kt_[:], ktok[tt0:tt0 + 128, :], reads=[ktok], writes=[kt_])
                kb.dma("pool", vt_[:], vtok[tt0:tt0 + 128, :], reads=[vtok], writes=[vt_])
                kb.op("pe", lambda e: e.matmul(ps_t[:, 0:256], gl_g[:, a0:a0 + 128], w2_s[:, d, :], start=True, stop=True), reads=[gl_g, w2_s], writes=[ps_t])
                kb.op("dve", lambda e: e.tensor_tensor(out=gt_[:], in0=ps_t[:, 0:256], in1=gbT_s[:, d, :], op=ALU.add), reads=[ps_t, gbT_s], writes=[gt_])
                kb.op("act", lambda e: e.activation(out=gt_[:], in_=gt_[:], func=AF.Sigmoid), reads=[gt_], writes=[gt_])
                kb.op("act", lambda e: e.activation(out=gt_[:], in_=gt_[:], func=AF.Ln), reads=[gt_], writes=[gt_])
                kb.op("pe", lambda e: e.matmul(ps_t[:, 256:512], masks_s[:, d, 0, :], gt_[:], start=True, stop=True), reads=[masks_s, gt_], writes=[ps_t])
                kb.op("act", lambda e: e.activation(out=gt_[:], in_=ps_t[:, 256:512], func=AF.Exp), reads=[ps_t], writes=[gt_])
                kb.op("dve", lambda e: e.tensor_tensor(out=kr_[:], in0=kt_[:], in1=gt_[:], op=ALU.mult), reads=[kt_, gt_], writes=[kr_])
                for kc in range(2):
                    kb.op("pe", lambda e: e.matmul(ps_a[:, 0:128], ke[:, kc, a0:a0 + 128], qe[:, kc, a0:a0 + 128], start=(kc == 0), stop=(kc == 1)), reads=[ke, qe], writes=[ps_a])
                kb.op("dve", lambda e: e.tensor_tensor(out=ab_[:], in0=ps_a[:, 0:128], in1=masks_s[:, d, 1, :], op=ALU.mult), reads=[ps_a, masks_s], writes=[ab_])
                kb.op("pe", lambda e: e.matmul(pso[:], ab_[:], vt_[:], start=True, stop=False), reads=[ab_, vt_], writes=[pso])
                for kc in range(2):
                    kb.op("pe", lambda e: e.matmul(pso[:], qe[:, kc, a0:a0 + 128], Sb[kc][:], start=False, stop=(kc == 1)), reads=[qe, Sb[kc]], writes=[pso])
                if d == 0:
                    kb.op("act", lambda e: e.activation(out=ot_[:], in_=pso[:], func=AF.Identity), reads=[pso], writes=[ot_])
                    kb.dma("sp", o0[tt0:tt0 + 128, :], ot_[:], reads=[ot_], writes=[o0])
                else:
                    kb.dma("sp", o0_[:], o0[tt0:tt0 + 128, :], reads=[o0], writes=[o0_])
                    kb.op("dve", lambda e: e.tensor_tensor(out=ot_[:], in0=pso[:], in1=o0_[:], op=ALU.add), reads=[pso, o0_], writes=[ot_])
                    kb.op("act", lambda e: e.activation(out=junk[:], in_=ot_[:], func=AF.Square, accum_out=ss_[:]), reads=[ot_], writes=[junk, ss_])
                    kb.op("act", lambda e: e.activation(out=ss_[:], in_=ss_[:], func=AF.Sqrt, scale=1.0 / 512.0, bias=eps_t), reads=[ss_], writes=[ss_])
                    kb.op("dve", lambda e: e.reciprocal(out=ss_[:], in_=ss_[:]), reads=[ss_], writes=[ss_])
                    kb.op("dve", lambda e: e.scalar_tensor_tensor(out=ot_[:], in0=ot_[:], scalar=ss_[:, 0:1], in1=ngT_s[:], op0=ALU.mult, op1=ALU.mult), reads=[ot_, ss_, ngT_s], writes=[ot_])
                    kb.dma("sp", p1[tt0:tt0 + 128, :], ot_[:], reads=[ot_], writes=[p1])
                lastcol = a0 + 127 if d == 0 else a0
                for kc in range(2):
                    pss_ = ps_s[kc]
                    kb.op("pe", lambda e: e.matmul(pss_[:], kr_[:, kc * 128:(kc + 1) * 128], vt_[:], start=True, stop=True), reads=[kr_, vt_], writes=[pss_])
                    kb.op("dve", lambda e: e.scalar_tensor_tensor(out=S[kc][:], in0=S[kc][:], scalar=Eq[:, kc, lastcol:lastcol + 1], in1=pss_[:], op0=ALU.mult, op1=ALU.add),
                          reads=[S[kc], Eq, pss_], writes=[S[kc]])
                    kb.op("act", lambda e: e.activation(out=Sb[kc][:], in_=S[kc][:], func=AF.Identity), reads=[S[kc]], writes=[Sb[kc]])
    kb.finish([p1])
    kb.close()
    return kb


def gla_b_consts():
    s = np.arange(128)[:, None]; t = np.arange(128)[None, :]
    masks = np.zeros((128, 2, 2, 128), np.float32)
    masks[:, 0, 0, :] = (s > t) / 16.0
    masks[:, 1, 0, :] = (s < t) / 16.0
    masks[:, 0, 1, :] = (s <= t)
    masks[:, 1, 1, :] = (s >= t)
    rmask = np.ones((128, 512), np.float32); rmask[:, ::128] = 0.0
    return masks, rmask


US_W = 8720
US_LAT = 1
US_CTX = 8195


def build_rwkv_a():
    kb = KB()
    h = kb.dram("h", [128, 16, TCP], kind="ExternalInput")
    mods = kb.dram("mods", [128, 96, 2], kind="ExternalInput")
    ng = kb.dram("ng", [128, 4, 16], kind="ExternalInput")
    mu = kb.dram("mu", [128, 6, 16], kind="ExternalInput")
    w = kb.dram("w", [48, 128, 16 * 128], kind="ExternalInput")
    wl = kb.dram("wl", [128, 16 * 384], kind="ExternalInput")
    g1 = kb.dram("g1", [128, 16 * 256], kind="ExternalInput")
    g2 = kb.dram("g2", [16, 128, 2 * 128], kind="ExternalInput")
    rkv = kb.dram("rkv", [128, 48, TCP], kind="ExternalOutput")
    lora = kb.dram("lora", [96, 4, TCP], kind="ExternalOutput")
    gout = kb.dram("g", [128, 16, TCP], kind="ExternalOutput")
    us = kb.dram("us", [128, 16, US_W])

    pm = PreMod(kb, h, mods, ng, out_dt=F32, hs_w=516)
    mu_s = kb.sb([128, 6, 16]); wl_s = kb.sb([128, 16, 384], BF16); g1_s = kb.sb([128, 16, 256], BF16); g2_s = kb.sb([128, 16, 2, 128], BF16)
    kb.dma("sp", mu_s[:], mu[:], reads=[mu], writes=[mu_s])
    kb.dma("pool", wl_s[:].rearrange("p k n -> p (k n)"), wl[:], reads=[wl], writes=[wl_s])
    kb.dma("pool", g1_s[:].rearrange("p k n -> p (k n)"), g1[:], reads=[g1], writes=[g1_s])
    for n in range(16):
        kb.dma("pool", g2_s[:, n].rearrange("p k n -> p (k n)"), g2[n], reads=[g2], writes=[g2_s])
    zt = pm.hs
    kb.op("dve", lambda e: e.memset(zt[:], 0.0), writes=[zt])
    c0 = 0
    while c0 < US_W:
        wd = min(516, US_W - c0)
        kb.dma("sp", us[:, :, c0:c0 + wd], zt[:, :, :wd], reads=[zt], writes=[us])
        c0 += wd
    for (t0, TT, ms) in TILES_A:
        u32 = pm.tile(t0, TT, ms)
        if ms == 1:
            kb.dma("sp", us[:, :, US_LAT + t0:US_LAT + t0 + TT], u32[:, :, :TT], reads=[u32], writes=[us])
        else:
            kb.dma("sp", us[:, :, US_CTX:US_CTX + CTXC], u32[:, :, :CTXC], reads=[u32], writes=[us])
    uc = pm.hs; dx = pm.u
    xm = [kb.sb([128, 16, 512], BF16) for _ in range(2)]
    wt = [kb.sb([128, 16, 128], BF16) for _ in range(2)]
    pss = [kb.ps() for _ in range(2)]
    o = [kb.sb([128, 512]) for _ in range(3)]
    glb = kb.sb([128, 2, 512], BF16)
    i = 0; xi = 0
    for (t0, TT, ms) in TILES_A:
        off = (US_LAT + t0) if ms == 1 else US_CTX
        kb.dma("sp", uc[:, :, 0:TT + 2], us[:, :, off - 1:off + TT + 1], reads=[us], writes=[uc])
        kb.op("dve", lambda e: e.tensor_tensor(out=dx[:, :, :TT], in0=uc[:, :, 0:TT], in1=uc[:, :, 2:TT + 2], op=ALU.add), reads=[uc], writes=[dx])
        kb.op("dve", lambda e: e.scalar_tensor_tensor(out=dx[:, :, :TT], in0=dx[:, :, :TT], scalar=0.5, in1=uc[:, :, 1:TT + 1], op0=ALU.mult, op1=ALU.subtract), reads=[dx, uc], writes=[dx])
        for m in (0, 2, 3, 1, 4, 5):
            x_ = xm[xi % 2]; xi += 1
            for kc in range(16):
                kb.op("dve", lambda e: e.scalar_tensor_tensor(out=x_[:, kc, :TT], in0=dx[:, kc, :TT], scalar=mu_s[:, m, kc:kc + 1], in1=uc[:, kc, 1:TT + 1], op0=ALU.mult, op1=ALU.add),
                      reads=[dx, mu_s, uc], writes=[x_])
            if m in (0, 2, 3):
                base = {0: 0, 2: 16, 3: 32}[m]
                for n in range(16):
                    w_ = wt[i % 2]; ps = pss[i % 2]; oo = o[i % 3]; i += 1
                    kb.dma("pool", w_[:].rearrange("p k n -> p (k n)"), w[base + n], reads=[w], writes=[w_])
                    for kc in range(16):
                        kb.op("pe", lambda e: e.matmul(ps[:, :TT], w_[:, kc, :], x_[:, kc, :TT], start=(kc == 0), stop=(kc == 15)), reads=[w_, x_], writes=[ps])
                    kb.op("act", lambda e: e.activation(out=oo[:, :TT], in_=ps[:, :TT], func=AF.Identity), reads=[ps], writes=[oo])
                    kb.dma("sp", rkv[:, base + n, t0:t0 + TT], oo[:, :TT], reads=[oo], writes=[rkv])
            elif m in (1, 4):
                for d in range(2):
                    col = (0 if m == 1 else 192) + d * 96
                    ps = pss[i % 2]; oo = o[i % 3]; i += 1
                    for kc in range(16):
                        kb.op("pe", lambda e: e.matmul(ps[0:96, :TT], wl_s[:, kc, col:col + 96], x_[:, kc, :TT], start=(kc == 0), stop=(kc == 15)), reads=[wl_s, x_], writes=[ps])
                    kb.op("act", lambda e: e.activation(out=oo[0:96, :TT], in_=ps[0:96, :TT], func=(AF.Tanh if m == 1 else AF.Identity)), reads=[ps], writes=[oo])
                    kb.dma("sp", lora[:, (0 if m == 1 else 2) + d, t0:t0 + TT], oo[0:96, :TT], reads=[oo], writes=[lora])
            else:
                for c2 in range(2):
                    ps = pss[i % 2]; i += 1
                    for kc in range(16):
                        kb.op("pe", lambda e: e.matmul(ps[:, :TT], g1_s[:, kc, c2 * 128:(c2 + 1) * 128], x_[:, kc, :TT], start=(kc == 0), stop=(kc == 15)), reads=[g1_s, x_], writes=[ps])
                    kb.op("act", lambda e: e.activation(out=glb[:, c2, :TT], in_=ps[:, :TT], func=AF.Sigmoid), reads=[ps], writes=[glb])
                for n in range(16):
                    ps = pss[i % 2]; oo = o[i % 3]; i += 1
                    for kc in range(2):
                        kb.op("pe", lambda e: e.matmul(ps[:, :TT], g2_s[:, n, kc, :], glb[:, kc, :TT], start=(kc == 0), stop=(kc == 1)), reads=[g2_s, glb], writes=[ps])
                    kb.op("act", lambda e: e.activation(out=oo[:, :TT], in_=ps[:, :TT], func=AF.Identity), reads=[ps], writes=[oo])
                    kb.dma("sp", gout[:, n, t0:t0 + TT], oo[:, :TT], reads=[oo], writes=[gout])
    kb.finish([rkv, lora, gout])
    kb.close()
    return kb


NCHK = NS // 128
GN_EPS = 64e-5
P_KK, P_KA, P_RK, P_LNW, P_LNB, P_W0, P_A0 = 0, 1, 2, 3, 4, 5, 7


def build_rwkv_b(NS=NS, stage=9):
    kb = KB()
    NCHK = NS // 128
    rT = kb.dram("rT", [128, 4, NS], kind="ExternalInput")
    kT = kb.dram("kT", [128, 4, NS], kind="ExternalInput")
    vT = kb.dram("vT", [128, 4, NS], kind="ExternalInput")
    lo = kb.dram("lo", [96, 4, NS], kind="ExternalInput")
    w2 = kb.dram("w2", [96, 2, 512], kind="ExternalInput")
    a2 = kb.dram("a2", [96, 2, 512], kind="ExternalInput")
    prm = kb.dram("prm", [128, 9, 4], kind="ExternalInput")
    cst = kb.dram("cst", [128, 2, 128], kind="ExternalInput")
    msk = kb.dram("msk", [128, 2, 640], kind="ExternalInput")
    rmk = kb.dram("rmk", [128, 512], kind="ExternalInput")
    y0 = kb.dram("y0", [128, 4, NS]); bv0 = kb.dram("bv0", [128, 4, NS])
    p1 = kb.dram("p1", [128, 4, NS], kind="ExternalOutput")

    w2_s = kb.sb([96, 2, 512]); a2_s = kb.sb([96, 2, 512]); prm_s = kb.sb([128, 9, 4]); cst_s = kb.sb([128, 2, 128])
    msk_s = kb.sb([128, 2, 640]); rmk_s = kb.sb([128, 512])
    for s, d in ((w2_s, w2), (a2_s, a2), (prm_s, prm), (cst_s, cst), (msk_s, msk), (rmk_s, rmk)):
        kb.dma("sp", s[:], d[:], reads=[d], writes=[s])
    bones = cst_s[:, 0, :]; ident = cst_s[:, 1, :]

    def pb(idx):
        return prm_s[:, idx, :].unsqueeze(2).broadcast_to([128, 4, 128])

    X3 = [128, 4, 128]
    R = kb.sb(X3); K = kb.sb(X3); V = kb.sb(X3); LO = kb.sb([96, 4, 128])
    LW = kb.sb(X3); AS = kb.sb(X3); KK = kb.sb(X3); KD = kb.sb(X3); B_ = kb.sb(X3); CUM = kb.sb(X3)
    G = kb.sb(X3); GI = kb.sb(X3); GP = kb.sb(X3); t1 = kb.sb(X3); t2 = kb.sb(X3); BV = kb.sb(X3)
    AR = kb.sb([128, 4, 2, 128]); BK = kb.sb([128, 4, 2, 128])
    YT = kb.sb(X3); Y0 = kb.sb(X3); BV0 = kb.sb(X3)
    Hz = [kb.sb([128, 128]) for _ in range(4)]
    BZ = [kb.sb([128, 128]) for _ in range(2)]; KZ = [kb.sb([128, 128]) for _ in range(2)]
    VZ = [kb.sb([128, 128]) for _ in range(2)]; UZ = [kb.sb([128, 128]) for _ in range(2)]
    for t in BZ + KZ + VZ + UZ:
        kb.op("dve", lambda e: e.memset(t[:], 0.0), writes=[t])
    SC = [kb.sb([128, 512]) for _ in range(2)]
    XX = [kb.sb([128, 2, 2, 128]) for _ in range(2)]
    TT = kb.sb([128, 2, 128]); RHS = [kb.sb([128, 64]) for _ in range(2)]
    ps_z = kb.ps(); ps_tr = kb.ps(); ps_sc = kb.ps(); ps_p = kb.ps(); ps_t = kb.ps(); ps_r = kb.ps(); ps_y = kb.ps(); ps_h = kb.ps()
    eps_gn = eps_ap(kb, GN_EPS)

    def bmm(out_ps, src, src_t):
        kb.op("pe", lambda e: e.matmul(out_ps[:, 0:512], bones, src[:].rearrange("p a t -> p (a t)"), start=True, stop=True), reads=[cst_s, src_t], writes=[out_ps])

    for d in range(2):
        for hz in Hz:
            kb.op("dve", lambda e: e.memset(hz[:], 0.0), writes=[hz])
        order = list(range(NCHK)) if d == 0 else [1, 0] + list(range(NCHK - 1, 1, -1))
        last = 127 if d == 0 else 0
        for c in order:
            t0 = c * 128
            kb.dma("sp", R[:], rT[:, :, t0:t0 + 128], reads=[rT], writes=[R])
            kb.dma("sp", K[:], kT[:, :, t0:t0 + 128], reads=[kT], writes=[K])
            kb.dma("sp", V[:], vT[:, :, t0:t0 + 128], reads=[vT], writes=[V])
            kb.dma("sp", LO[:], lo[:, :, t0:t0 + 128], reads=[lo], writes=[LO])
            for hp in range(4):
                kb.op("pe", lambda e: e.matmul(ps_z[:, hp * 128:(hp + 1) * 128], w2_s[:, d, hp * 128:(hp + 1) * 128], LO[:, d, :], start=True, stop=True), reads=[w2_s, LO], writes=[ps_z])
            kb.op("dve", lambda e: e.tensor_tensor(out=t1[:], in0=ps_z[:, 0:512].rearrange("p (a t) -> p a t", a=4), in1=pb(P_W0 + d), op=ALU.add), reads=[ps_z, prm_s], writes=[t1])
            kb.op("act", lambda e: e.activation(out=t1[:], in_=t1[:], func=AF.Sigmoid), reads=[t1], writes=[t1])
            kb.op("dve", lambda e: e.tensor_scalar(out=LW[:], in0=t1[:], scalar1=-0.6065306597126334, scalar2=None, op0=ALU.mult), reads=[t1], writes=[LW])
            for hp in range(4):
                kb.op("pe", lambda e: e.matmul(ps_z[:, hp * 128:(hp + 1) * 128], a2_s[:, d, hp * 128:(hp + 1) * 128], LO[:, 2 + d, :], start=True, stop=True), reads=[a2_s, LO], writes=[ps_z])
            kb.op("dve", lambda e: e.tensor_tensor(out=AS[:], in0=ps_z[:, 0:512].rearrange("p (a t) -> p a t", a=4), in1=pb(P_A0 + d), op=ALU.add), reads=[ps_z, prm_s], writes=[AS])
            kb.op("act", lambda e: e.activation(out=AS[:], in_=AS[:], func=AF.Sigmoid), reads=[AS], writes=[AS])
            kb.op("dve", lambda e: e.tensor_tensor(out=KK[:], in0=K[:], in1=pb(P_KK), op=ALU.mult), reads=[K, prm_s], writes=[KK])
            kb.op("dve", lambda e: e.tensor_tensor(out=t2[:], in0=KK[:], in1=KK[:], op=ALU.mult), reads=[KK], writes=[t2])
            bmm(ps_z, t2, t2)
            kb.op("act", lambda e: e.activation(out=t2[:], in_=ps_z[:, 0:512].rearrange("p (a t) -> p a t", a=4), func=AF.Sqrt), reads=[ps_z], writes=[t2])
            kb.op("dve", lambda e: e.tensor_scalar(out=t2[:], in0=t2[:], scalar1=1e-12, scalar2=None, op0=ALU.max), reads=[t2], writes=[t2])
            kb.op("dve", lambda e: e.reciprocal(out=t2[:], in_=t2[:]), reads=[t2], writes=[t2])
            kb.op("dve", lambda e: e.tensor_tensor(out=KK[:], in0=KK[:], in1=t2[:], op=ALU.mult), reads=[KK, t2], writes=[KK])
            kb.op("dve", lambda e: e.scalar_tensor_tensor(out=t2[:], in0=AS[:], scalar=-1.0, in1=pb(P_KA), op0=ALU.add, op1=ALU.mult), reads=[AS, prm_s], writes=[t2])
            kb.op("dve", lambda e: e.scalar_tensor_tensor(out=KD[:], in0=t2[:], scalar=1.0, in1=K[:], op0=ALU.add, op1=ALU.mult), reads=[t2, K], writes=[KD])
            kb.op("dve", lambda e: e.tensor_tensor(out=B_[:], in0=KK[:], in1=AS[:], op=ALU.mult), reads=[KK, AS], writes=[B_])
            lwf = LW[:].rearrange("p a t -> p (a t)"); cumf = CUM[:].rearrange("p a t -> p (a t)")
            if d == 0:
                kb.op("dve", lambda e: e.tensor_tensor_scan(out=cumf, data0=rmk_s[:], data1=lwf, initial=0.0, op0=ALU.mult, op1=ALU.add), reads=[rmk_s, LW], writes=[CUM])
            else:
                kb.op("dve", lambda e: e.tensor_tensor_scan(out=cumf[:, ::-1], data0=rmk_s[:], data1=lwf[:, ::-1], initial=0.0, op0=ALU.mult, op1=ALU.add), reads=[rmk_s, LW], writes=[CUM])
            kb.op("act", lambda e: e.activation(out=G[:], in_=CUM[:], func=AF.Exp), reads=[CUM], writes=[G])
            kb.op("act", lambda e: e.activation(out=GI[:], in_=CUM[:], func=AF.Exp, scale=-1.0), reads=[CUM], writes=[GI])
            kb.op("dve", lambda e: e.tensor_tensor(out=t2[:], in0=CUM[:], in1=LW[:], op=ALU.subtract), reads=[CUM, LW], writes=[t2])
            kb.op("act", lambda e: e.activation(out=GP[:], in_=t2[:], func=AF.Exp), reads=[t2], writes=[GP])
            kb.op("dve", lambda e: e.scalar_tensor_tensor(out=AR[:, :, 0, :], in0=KK[:], scalar=-1.0, in1=GP[:], op0=ALU.mult, op1=ALU.mult), reads=[KK, GP], writes=[AR])
            kb.op("dve", lambda e: e.tensor_tensor(out=AR[:, :, 1, :], in0=R[:], in1=G[:], op=ALU.mult), reads=[R, G], writes=[AR])
            kb.op("dve", lambda e: e.tensor_tensor(out=BK[:, :, 0, :], in0=B_[:], in1=GI[:], op=ALU.mult), reads=[B_, GI], writes=[BK])
            kb.op("dve", lambda e: e.tensor_tensor(out=BK[:, :, 1, :], in0=KD[:], in1=GI[:], op=ALU.mult), reads=[KD, GI], writes=[BK])
            kb.op("dve", lambda e: e.tensor_tensor(out=t2[:], in0=R[:], in1=KD[:], op=ALU.mult), reads=[R, KD], writes=[t2])
            kb.op("dve", lambda e: e.tensor_tensor(out=t2[:], in0=t2[:], in1=pb(P_RK), op=ALU.mult), reads=[t2, prm_s], writes=[t2])
            bmm(ps_z, t2, t2)
            kb.op("dve", lambda e: e.tensor_tensor(out=BV[:], in0=ps_z[:, 0:512].rearrange("p (a t) -> p a t", a=4), in1=V[:], op=ALU.mult), reads=[ps_z, V], writes=[BV])
            for hp in range(4 if stage >= 1 else 0):
                hz = Hz[hp]
                kb.op("pe", lambda e: e.matmul(ps_tr[:, 0:128], BK[:, hp, 0, :], ident, start=True, stop=True), reads=[BK, cst_s], writes=[ps_tr])
                kb.op("pe", lambda e: e.matmul(ps_tr[:, 128:256], BK[:, hp, 1, :], ident, start=True, stop=True), reads=[BK, cst_s], writes=[ps_tr])
                kb.op("pe", lambda e: e.matmul(ps_tr[:, 256:384], V[:, hp, :], ident, start=True, stop=True), reads=[V, cst_s], writes=[ps_tr])
                for h2 in range(2):
                    cs = slice(64 * h2, 64 * h2 + 64)
                    kb.op("act", lambda e: e.activation(out=BZ[h2][:, cs], in_=ps_tr[:, 64 * h2:64 * h2 + 64], func=AF.Identity), reads=[ps_tr], writes=[BZ[h2]])
                    kb.op("dve", lambda e: e.tensor_copy(out=KZ[h2][:, cs], in_=ps_tr[:, 128 + 64 * h2:128 + 64 * h2 + 64]), reads=[ps_tr], writes=[KZ[h2]])
                    kb.op("act", lambda e: e.activation(out=VZ[h2][:, cs], in_=ps_tr[:, 256 + 64 * h2:256 + 64 * h2 + 64], func=AF.Identity), reads=[ps_tr], writes=[VZ[h2]])
                if stage < 2:
                    continue
                xx = XX[0]
                for h2 in range(2):
                    P = slice(64 * h2, 64 * h2 + 64)
                    arf = AR[P, hp, :, :].rearrange("p a t -> p (a t)")
                    kb.op("pe", lambda e: e.matmul(ps_sc[:, 0:256], BK[P, hp, 0, :], arf, start=True, stop=True), reads=[BK, AR], writes=[ps_sc])
                    kb.op("pe", lambda e: e.matmul(ps_sc[:, 256:512], BK[P, hp, 1, :], arf, start=True, stop=True), reads=[BK, AR], writes=[ps_sc])
                    kb.op("pe", lambda e: e.matmul(ps_t[:, 256:384], AR[P, hp, 0, :], BK[P, hp, 0, :], start=True, stop=True), reads=[BK, AR], writes=[ps_t])
                    kb.op("dve", lambda e: e.tensor_tensor(out=SC[h2][:], in0=ps_sc[:, 0:512], in1=msk_s[:, d, 0:512], op=ALU.mult), reads=[ps_sc, msk_s], writes=[SC[h2]])
                    kb.op("act", lambda e: e.activation(out=xx[:, h2, 0, :], in_=SC[h2][:, 0:128], func=AF.Identity), reads=[SC[h2]], writes=[xx])
                    kb.op("dve", lambda e: e.tensor_tensor(out=xx[:, h2, 1, :], in0=ps_t[:, 256:384], in1=msk_s[:, d, 512:640], op=ALU.mult), reads=[ps_t, msk_s], writes=[xx])
                    kb.op("dve", lambda e: e.tensor_tensor(out=TT[:, h2, :], in0=SC[h2][:, 0:128], in1=ident, op=ALU.add), reads=[SC[h2], cst_s], writes=[TT])
                if stage < 3:
                    continue
                cur = 0
                for lvl in range(6):
                    xa = XX[cur]; xb = XX[1 - cur]
                    for h2 in range(2):
                        kb.op("pe", lambda e: e.matmul(ps_p[:, h2 * 256:h2 * 256 + 128], xa[:, h2, 1, :], xa[:, h2, 0, :], start=True, stop=True), reads=[xa], writes=[ps_p])
                        kb.op("pe", lambda e: e.matmul(ps_p[:, h2 * 256 + 128:h2 * 256 + 256], xa[:, h2, 0, :], xa[:, h2, 1, :], start=True, stop=True), reads=[xa], writes=[ps_p])
                    kb.op("act", lambda e: e.activation(out=xb[:].rearrange("p h a t -> p (h a t)"), in_=ps_p[:, 0:512], func=AF.Identity), reads=[ps_p], writes=[xb])
                    for h2 in range(2):
                        kb.op("pe", lambda e: e.matmul(ps_t[:, h2 * 128:(h2 + 1) * 128], xb[:, h2, 1, :], TT[:, h2, :], start=True, stop=True), reads=[xb, TT], writes=[ps_t])
                    kb.op("dve", lambda e: e.tensor_tensor(out=TT[:].rearrange("p h t -> p (h t)"), in0=TT[:].rearrange("p h t -> p (h t)"), in1=ps_t[:, 0:256], op=ALU.add), reads=[TT, ps_t], writes=[TT])
                    cur = 1 - cur
                if stage < 4:
                    continue
                for h2 in range(2):
                    P = slice(64 * h2, 64 * h2 + 64); cs = P
                    kb.op("pe", lambda e: e.matmul(ps_r[:, h2 * 64:h2 * 64 + 64], AR[P, hp, 0, :], hz[P, cs], start=True, stop=False), reads=[AR, hz], writes=[ps_r])
                    kb.op("pe", lambda e: e.matmul(ps_r[:, h2 * 64:h2 * 64 + 64], SC[h2][:, 256:384], VZ[h2][:, cs], start=False, stop=True), reads=[SC[h2], VZ[h2]], writes=[ps_r])
                    kb.op("act", lambda e: e.activation(out=RHS[h2][:], in_=ps_r[:, h2 * 64:h2 * 64 + 64], func=AF.Identity), reads=[ps_r], writes=[RHS[h2]])
                    kb.op("pe", lambda e: e.matmul(ps_r[:, 128 + h2 * 64:128 + h2 * 64 + 64], TT[:, h2, :], RHS[h2][:], start=True, stop=True), reads=[TT, RHS[h2]], writes=[ps_r])
                    kb.op("dve", lambda e: e.tensor_copy(out=UZ[h2][:, cs], in_=ps_r[:, 128 + h2 * 64:128 + h2 * 64 + 64]), reads=[ps_r], writes=[UZ[h2]])
                if stage < 5:
                    continue
                for h2 in range(2):
                    P = slice(64 * h2, 64 * h2 + 64)
                    kb.op("pe", lambda e: e.matmul(ps_y[:, 0:128], hz[P, :], AR[P, hp, 1, :], start=(h2 == 0), stop=False), reads=[hz, AR], writes=[ps_y])
                    kb.op("pe", lambda e: e.matmul(ps_y[:, 0:128], UZ[h2][:], SC[h2][:, 128:256], start=False, stop=False), reads=[UZ[h2], SC[h2]], writes=[ps_y])
                    kb.op("pe", lambda e: e.matmul(ps_y[:, 0:128], VZ[h2][:], SC[h2][:, 384:512], start=False, stop=(h2 == 1)), reads=[VZ[h2], SC[h2]], writes=[ps_y])
                kb.op("act", lambda e: e.activation(out=YT[:, hp, :], in_=ps_y[:, 0:128], func=AF.Identity), reads=[ps_y], writes=[YT])
                if stage < 6:
                    continue
                for h2 in range(2):
                    cs = slice(64 * h2, 64 * h2 + 64)
                    kb.op("pe", lambda e: e.matmul(ps_h[:, 0:64], BZ[h2][:], UZ[h2][:, cs], start=(h2 == 0), stop=False), reads=[BZ[h2], UZ[h2]], writes=[ps_h])
                    kb.op("pe", lambda e: e.matmul(ps_h[:, 0:64], KZ[h2][:], VZ[h2][:, cs], start=False, stop=(h2 == 1)), reads=[KZ[h2], VZ[h2]], writes=[ps_h])
                for h2 in range(2):
                    P = slice(64 * h2, 64 * h2 + 64)
                    kb.op("dve", lambda e: e.tensor_tensor(out=hz[P, P], in0=hz[P, P], in1=ps_h[P, 0:64], op=ALU.add), reads=[hz, ps_h], writes=[hz])
                    kb.op("dve", lambda e: e.tensor_scalar(out=hz[P, P], in0=hz[P, P], scalar1=G[P, hp, last:last + 1], scalar2=None, op0=ALU.mult), reads=[hz, G], writes=[hz])
            if d == 0:
                kb.dma("sp", y0[:, :, t0:t0 + 128], YT[:], reads=[YT], writes=[y0])
                kb.dma("sp", bv0[:, :, t0:t0 + 128], BV[:], reads=[BV], writes=[bv0])
            else:
                kb.dma("sp", Y0[:], y0[:, :, t0:t0 + 128], reads=[y0], writes=[Y0])
                kb.dma("sp", BV0[:], bv0[:, :, t0:t0 + 128], reads=[bv0], writes=[BV0])
                kb.op("dve", lambda e: e.tensor_tensor(out=YT[:], in0=YT[:], in1=Y0[:], op=ALU.add), reads=[YT, Y0], writes=[YT])
                bmm(ps_z, YT, YT)
                kb.op("dve", lambda e: e.scalar_tensor_tensor(out=YT[:], in0=ps_z[:, 0:512].rearrange("p (a t) -> p a t", a=4), scalar=-1.0 / 64.0, in1=YT[:], op0=ALU.mult, op1=ALU.add), reads=[ps_z, YT], writes=[YT])
                kb.op("dve", lambda e: e.tensor_tensor(out=t2[:], in0=YT[:], in1=YT[:], op=ALU.mult), reads=[YT], writes=[t2])
                bmm(ps_z, t2, t2)
                kb.op("act", lambda e: e.activation(out=t2[:], in_=ps_z[:, 0:512].rearrange("p (a t) -> p a t", a=4), func=AF.Sqrt, scale=1.0 / 64.0, bias=eps_gn), reads=[ps_z], writes=[t2])
                kb.op("dve", lambda e: e.reciprocal(out=t2[:], in_=t2[:]), reads=[t2], writes=[t2])
                kb.op("dve", lambda e: e.tensor_tensor(out=YT[:], in0=YT[:], in1=t2[:], op=ALU.mult), reads=[YT, t2], writes=[YT])
                kb.op("dve", lambda e: e.tensor_tensor(out=YT[:], in0=YT[:], in1=pb(P_LNW), op=ALU.mult), reads=[YT, prm_s], writes=[YT])
                kb.op("dve", lambda e: e.tensor_tensor(out=YT[:], in0=YT[:], in1=pb(P_LNB), op=ALU.add), reads=[YT, prm_s], writes=[YT])
                kb.op("dve", lambda e: e.tensor_tensor(out=BV[:], in0=BV[:], in1=BV0[:], op=ALU.add), reads=[BV, BV0], writes=[BV])
                kb.op("dve", lambda e: e.tensor_tensor(out=YT[:], in0=YT[:], in1=BV[:], op=ALU.add), reads=[YT, BV], writes=[YT])
                kb.dma("sp", p1[:, :, t0:t0 + 128], YT[:], reads=[YT], writes=[p1])
    kb.finish([p1])
    kb.close()
    return kb


def rwkv_b_consts():
    p = np.arange(128)[:, None]; q = np.arange(128)[None, :]
    cst = np.zeros((128, 2, 128), np.float32)
    cst[:, 0, :] = (p // 64 == q // 64)
    cst[:, 1, :] = (p == q)
    msk = np.zeros((128, 2, 640), np.float32)
    for d, (strict, incl) in enumerate((((p < q), (p <= q)), ((p > q), (p >= q)))):
        msk[:, d, 0:128] = strict; msk[:, d, 128:256] = incl; msk[:, d, 256:384] = strict; msk[:, d, 384:512] = incl
        msk[:, d, 512:640] = strict.T
    rmk = np.ones((128, 512), np.float32); rmk[:, ::128] = 0.0
    return cst, msk, rmk


GRID_W = 64
ROWS = 128


def core_bq(c):
    return c // 4, c % 4


def shard_tokens(lat, ctx):
    out = []
    F = lat.shape[-1]
    for b in range(NTL):
        a = np.zeros((F, TCP), np.float32)
        a[:, 0:LATC] = lat[b].T
        a[:, LATC:LATC + CTXC] = ctx[b].T
        out.append(fm(a))
    return out


def unshard_tokens(cores):
    n = cores[0].shape[1]
    F = n * 128
    lat = np.zeros((2, LATC, F), np.float32); ctx = np.zeros((2, CTXC, F), np.float32)
    for b in range(NTL):
        a = cores[b].transpose(1, 0, 2).reshape(F, TCP)
        lat[b] = a[:, 0:LATC].T
        ctx[b] = a[:, LATC:LATC + CTXC].T
    return lat, ctx


def to_col(u):
    b, n, d = u.shape
    return u.reshape(b, ROWS, GRID_W, d).transpose(0, 2, 1, 3).reshape(b, n, d)


def to_row(u):
    b, n, d = u.shape
    return u.reshape(b, GRID_W, ROWS, d).transpose(0, 2, 1, 3).reshape(b, n, d)


def mods_core(mods_l, b):
    return fm(np.stack([mods_l[:, 2], mods_l[:, b]], axis=1))


def ng_core(norm_g_l):
    return np.ascontiguousarray(norm_g_l.reshape(4, 16, 128).transpose(2, 0, 1))


_cache = {}
def cached(name, fn):
    if name not in _cache:
        _cache[name] = fn()
    return _cache[name]


def lru_layer(h_cores, inp, mods_l, layer, j, col_major):
    ng = ng_core(inp["norm_g"][layer])
    w = np.ascontiguousarray(inp["lru_w_in"][j].reshape(16, 128, 32, 128).transpose(2, 1, 0, 3).reshape(32, 128, 2048))
    bin_ = np.ascontiguousarray(inp["lru_b_in"][j].reshape(32, 128).T)
    kb = cached("lru_a", build_lru_a)
    res = run(kb, [dict(h=h_cores[c], mods=mods_core(mods_l, c), ng=ng, w=w, bin=bin_) for c in range(NTL)])
    gate_cores = [r["gate"] for r in res]
    rec_lat, rec_ctx = unshard_tokens([r["rec"] for r in res])
    if col_major:
        rec_lat = to_col(rec_lat)
    recT = np.concatenate([rec_ctx, rec_lat], axis=1).transpose(0, 2, 1)
    maps = []
    for c in range(NCORES):
        blks = [2 * c, 2 * c + 1]
        rec_c = np.ascontiguousarray(recT.reshape(2, 16, 128, NS)[:, blks].transpose(1, 0, 2, 3))
        cw = inp["lru_conv_w"][j].reshape(4, 16, 128)[:, blks].transpose(2, 1, 0)
        cb = inp["lru_conv_b"][j].reshape(16, 128)[blks].T
        gw = inp["lru_gate_w"][j][:, :, blks].transpose(3, 2, 0, 1, 4)
        gb = inp["lru_gate_b"][j].reshape(2, 2, 16, 128)[:, :, blks].transpose(3, 2, 0, 1)
        ll = inp["lru_log_lambda"][j].reshape(2, 16, 128)[:, blks].transpose(2, 1, 0)
        maps.append(dict(rec=rec_c, cw=np.ascontiguousarray(cw), cb=np.ascontiguousarray(cb), gw=np.ascontiguousarray(gw),
                         gb=np.ascontiguousarray(gb), ll=np.ascontiguousarray(ll)))
    kb = cached("lru_b", build_lru_b)
    res = run(kb, maps)
    hs = np.stack([r["hsum"] for r in res], axis=0)
    hsT = hs.transpose(2, 0, 1, 3, 4).reshape(2, 2048, NS)
    hs_ctx = hsT[:, :, :256].transpose(0, 2, 1); hs_lat = hsT[:, :, 256:].transpose(0, 2, 1)
    if col_major:
        hs_lat = to_row(np.ascontiguousarray(hs_lat))
    p1_cores = shard_tokens(hs_lat, hs_ctx)
    return post_layer(h_cores, p1_cores, gate_cores, inp, mods_l, layer, inp["lru_w_out"][j], inp["lru_b_out"][j])


def post_layer(h_cores, p1_cores, p2_cores, inp, mods_l, layer, w_out, b_out):
    ng = ng_core(inp["norm_g"][layer])
    W = prep_c_weights(w_out, inp["ffn_w_in"][layer], inp["ffn_w_out"][layer])
    bo = np.ascontiguousarray(b_out.reshape(16, 128).T)
    kb = cached("c", lambda: build_c(True))
    res = run(kb, [dict(h=h_cores[c], p1=p1_cores[c], p2=p2_cores[c], mods=mods_core(mods_l, c), ng=ng, bout=bo, **W) for c in range(NTL)])
    return [r["h2"] for r in res]


def scan_order(lat, ctx, col_major):
    if col_major:
        lat = to_col(lat)
    return np.concatenate([ctx, lat], axis=1)


def from_scan_order(seq, col_major):
    ctx = seq[:, :256]; lat = np.ascontiguousarray(seq[:, 256:])
    if col_major:
        lat = to_row(lat)
    return lat, ctx


def gla_layer(h_cores, inp, mods_l, layer, j, col_major):
    ng = ng_core(inp["norm_g"][layer])
    w = np.ascontiguousarray(inp["gla_w_in"][j].reshape(16, 128, 48, 128).transpose(2, 1, 0, 3).reshape(48, 128, 2048))
    w1 = np.ascontiguousarray(inp["gla_gate_w1"][j].transpose(1, 0, 2).reshape(16, 128, 32).transpose(1, 0, 2).reshape(128, 512))
    br = np.ascontiguousarray(inp["gla_b_r"][j].reshape(16, 128).T)
    kb = cached("gla_a", build_gla_a)
    res = run(kb, [dict(h=h_cores[c], mods=mods_core(mods_l, c), ng=ng, w=w, w1=w1, br=br) for c in range(NTL)])
    sr_cores = [r["sr"] for r in res]
    qkv_lat, qkv_ctx = unshard_tokens([r["qkv"] for r in res])
    gl_cores = [np.pad(r["gl"], ((0, 96), (0, 0))).reshape(1, 128, TCP).transpose(1, 0, 2) for r in res]
    gl_lat, gl_ctx = unshard_tokens(gl_cores)
    qkv = scan_order(qkv_lat, qkv_ctx, col_major)
    gl = scan_order(gl_lat, gl_ctx, col_major)[:, :, :32]
    masks, rmask = gla_b_consts()
    maps = []
    for c in range(NCORES):
        b, hd = core_bq(c)
        q = qkv[b][:, hd * 256:(hd + 1) * 256]; k = qkv[b][:, 1024 + hd * 256:1024 + (hd + 1) * 256]
        v = qkv[b][:, 2048 + hd * 512:2048 + (hd + 1) * 512]
        w2 = inp["gla_gate_w2"][j][:, :, hd * 256:(hd + 1) * 256]
        gb = inp["gla_gate_b"][j][:, hd * 256:(hd + 1) * 256]
        ngh = inp["gla_norm_g"][j][hd * 512:(hd + 1) * 512]
        maps.append(dict(qT=np.ascontiguousarray(q.T.reshape(2, 128, NS).transpose(1, 0, 2)), kT=np.ascontiguousarray(k.T.reshape(2, 128, NS).transpose(1, 0, 2)),
                         ktok=np.ascontiguousarray(k), vtok=np.ascontiguousarray(v),
                         glT=np.ascontiguousarray(gl[b].reshape(NS, 2, 16).transpose(2, 1, 0)), w2=np.ascontiguousarray(w2.transpose(1, 0, 2)),
                         gbF=np.ascontiguousarray(gb.reshape(2, 2, 128).transpose(2, 0, 1)), gbT=np.ascontiguousarray(np.broadcast_to(gb[None], (128, 2, 256))),
                         ngT=np.ascontiguousarray(np.broadcast_to(ngh[None], (128, 512))), masks=masks, rmask=rmask))
    kb = cached("gla_b", build_gla_b)
    res = run(kb, maps)
    seq = np.zeros((2, NS, 2048), np.float32)
    for c in range(NCORES):
        b, hd = core_bq(c)
        seq[b][:, hd * 512:(hd + 1) * 512] = res[c]["p1"]
    p_lat, p_ctx = from_scan_order(seq, col_major)
    p1_cores = shard_tokens(p_lat, p_ctx)
    return post_layer(h_cores, p1_cores, sr_cores, inp, mods_l, layer, inp["gla_w_out"][j], np.zeros(2048, np.float32))


def rwkv_layer(h_cores, inp, mods_l, layer, j, col_major):
    ng = ng_core(inp["norm_g"][layer])
    mu = np.ascontiguousarray(inp["rwkv_mu"][j].reshape(6, 16, 128).transpose(2, 0, 1))
    w = np.ascontiguousarray(inp["rwkv_w_rkv"][j].reshape(3, 16, 128, 16, 128).transpose(0, 3, 2, 1, 4).reshape(48, 128, 2048))
    wl = np.concatenate([inp["rwkv_w1"][j][0], inp["rwkv_w1"][j][1], inp["rwkv_a1"][j][0], inp["rwkv_a1"][j][1]], axis=1)
    wl = np.ascontiguousarray(wl.reshape(16, 128, 384).transpose(1, 0, 2).reshape(128, 16 * 384))
    g1 = np.ascontiguousarray(inp["rwkv_g1"][j].reshape(16, 128, 256).transpose(1, 0, 2).reshape(128, 16 * 256))
    g2 = np.ascontiguousarray(inp["rwkv_g2"][j].reshape(2, 128, 16, 128).transpose(2, 1, 0, 3).reshape(16, 128, 256))
    kb = cached("rwkv_a", build_rwkv_a)
    res = run(kb, [dict(h=h_cores[c], mods=mods_core(mods_l, c), ng=ng, mu=mu, w=w, wl=wl, g1=g1, g2=g2) for c in range(NTL)])
    g_cores = [r["g"] for r in res]
    rkv_lat, rkv_ctx = unshard_tokens([r["rkv"] for r in res])
    lo_cores = [np.pad(r["lora"], ((0, 32), (0, 0), (0, 0))) for r in res]
    lo_lat, lo_ctx = unshard_tokens(lo_cores)
    rkv = scan_order(rkv_lat, rkv_ctx, col_major)
    los = scan_order(lo_lat, lo_ctx, col_major)
    cst, msk, rmk = rwkv_b_consts()
    maps = []
    for c in range(NCORES):
        b, hq = core_bq(c)
        ch = slice(hq * 512, (hq + 1) * 512)

        def f4(x):
            return np.ascontiguousarray(x.T.reshape(4, 128, -1).transpose(1, 0, 2))

        def pp(v):
            return v[ch].reshape(4, 128).T
        prm = np.stack([pp(inp["rwkv_k_k"][j]), pp(inp["rwkv_k_a"][j]), pp(inp["rwkv_r_k"][j].reshape(-1)), pp(inp["rwkv_ln_w"][j]), pp(inp["rwkv_ln_b"][j]),
                        pp(inp["rwkv_w0"][j][0]), pp(inp["rwkv_w0"][j][1]), pp(inp["rwkv_a0"][j][0]), pp(inp["rwkv_a0"][j][1])], axis=1)
        lo = np.ascontiguousarray(los[b].reshape(NS, 4, 128)[:, :, :96].transpose(2, 1, 0))
        maps.append(dict(rT=f4(rkv[b][:, 0:2048][:, ch]), kT=f4(rkv[b][:, 2048:4096][:, ch]), vT=f4(rkv[b][:, 4096:6144][:, ch]), lo=lo,
                         w2=np.ascontiguousarray(inp["rwkv_w2"][j][:, :, ch].transpose(1, 0, 2)), a2=np.ascontiguousarray(inp["rwkv_a2"][j][:, :, ch].transpose(1, 0, 2)),
                         prm=np.ascontiguousarray(prm.astype(np.float32)), cst=cst, msk=msk, rmk=rmk))
    kb = cached("rwkv_b", build_rwkv_b)
    res = run(kb, maps)
    seq = np.zeros((2, NS, 2048), np.float32)
    for c in range(NCORES):
        b, hq = core_bq(c)
        seq[b][:, hq * 512:(hq + 1) * 512] = res[c]["p1"].transpose(1, 0, 2).reshape(512, NS).T
    p_lat, p_ctx = from_scan_order(seq, col_major)
    p1_cores = shard_tokens(p_lat, p_ctx)
    return post_layer(h_cores, p1_cores, g_cores, inp, mods_l, layer, inp["rwkv_w_out"][j], np.zeros(2048, np.float32))

N_MIXERS = 3
DEPTH = 4


def kernel(**inp):
    inp = {k: np.ascontiguousarray(np.asarray(v)) for k, v in inp.items()}
    mods = run_ada(inp)
    h = shard_tokens(inp["x"], inp["ctx"])
    for i in range(DEPTH):
        kind, j = i % N_MIXERS, i // N_MIXERS
        col_major = i % 2 == 1
        if kind == 0:
            h = lru_layer(h, inp, mods[i], i, j, col_major)
        elif kind == 1:
            h = gla_layer(h, inp, mods[i], i, j, col_major)
        else:
            h = rwkv_layer(h, inp, mods[i], i, j, col_major)
    lat, _ = unshard_tokens(h)
    return np.ascontiguousarray(lat.astype(np.float32))
```

```python
import numpy as np
from contextlib import ExitStack
import concourse.bass as bass
import concourse.mybir as mybir
from concourse.bass_utils import run_bass_kernel_spmd

F32 = mybir.dt.float32
BF16 = mybir.dt.bfloat16
AF = mybir.ActivationFunctionType
ALU = mybir.AluOpType
AX = mybir.AxisListType

NCORES = 8
NTL = 2
LATC = 8192
CTXC = 256
TCP = 8704
TILES_A = [(512 * i, 512, 1) for i in range(16)] + [(8192, 512, 0)]


class T:
    def __init__(self, t, name=""):
        self.t = t
        self.name = name
        self.w = None
        self.r = {}
        self.psum = False

    def __getitem__(self, idx):
        return self.t[idx]


class KB:
    NSP = 14
    NPOOLQ = 8

    def __init__(self, same_engine_sync=True):
        self.nc = bass.Bass("TRN2", target_bir_lowering=False)
        self.es = ExitStack()
        nc = self.nc
        self.eng = dict(pe=nc.tensor, act=nc.scalar, dve=nc.vector, pool=nc.gpsimd, sp=nc.sync)
        self.sem = {}
        self.cnt = {}
        for e in self.eng:
            self.sem[e] = self.es.enter_context(nc.semaphore("s_" + e))
            self.cnt[e] = 0
        self.dq = {}
        for q, n in (("sp", self.NSP), ("pool", self.NPOOLQ), ("act", 4)):
            lst = []
            for i in range(n):
                k = f"d_{q}{i}"
                self.sem[k] = self.es.enter_context(nc.semaphore(k))
                self.cnt[k] = 0
                lst.append(k)
            self.dq[q] = [lst, 0]
        self.seen = {e: {} for e in self.eng}
        self.cur = {e: e for e in self.eng}
        self.nep = {e: 0 for e in self.eng}
        self.own = {e: e for e in self.eng}
        self.same_engine_sync = same_engine_sync
        self.n_inst = 0
        self.uid = 0
        self.alloc_es = self.es
        self.phase_id = 0
        self.consts = {}

    def sb(self, shape, dt=F32, name=None):
        self.uid += 1
        name = name or f"sb{self.uid}"
        t = self.alloc_es.enter_context(self.nc.sbuf_tensor(name, list(shape), dt))
        return T(t, name)

    def ps(self, shape=(128, 512), dt=F32, name=None):
        self.uid += 1
        name = name or f"ps{self.uid}"
        t = self.alloc_es.enter_context(self.nc.psum_tensor(name, list(shape), dt))
        tt = T(t, name)
        tt.psum = True
        return tt

    def barrier(self):
        keys = [self.cur[e] for e in ("pe", "act", "dve", "pool")]
        for q in self.dq:
            keys += self.dq[q][0]
        for e in ("pe", "act", "dve", "pool", "sp"):
            for k in keys:
                if self.cnt[k] and self.own.get(k) != e:
                    self._wait(e, (k, self.cnt[k]))

    def begin_phase(self):
        self.alloc_es = ExitStack()
        self.phase_id += 1
        self.consts = {}

    def end_phase(self):
        self.barrier()
        self.alloc_es.close()
        self.alloc_es = self.es
        self.consts = {}

    def dram(self, name, shape, dt=F32, kind="Internal"):
        t = self.nc.dram_tensor(name, list(shape), dt, kind=kind)
        return T(t.ap(), name)

    def _wait(self, e, dep):
        k, v = dep
        if self.own.get(k) == e and (e == "pe" or not self.same_engine_sync):
            return
        if self.seen[e].get(k, 0) >= v:
            return
        self.eng[e].wait_ge(self.sem[k], v)
        self.seen[e][k] = v

    def _deps(self, e, reads, writes):
        for t in reads:
            if t.w is not None:
                self._wait(e, t.w)
            if t.psum:
                for k, v in list(t.r.items()):
                    if self.own.get(k) != e:
                        self._wait(e, (k, v))
        for t in writes:
            if t.w is not None:
                self._wait(e, t.w)
            for k, v in t.r.items():
                self._wait(e, (k, v))

    def _mark(self, mark, reads, writes):
        k, v = mark
        for t in reads:
            if t.r.get(k, 0) < v:
                t.r[k] = v
        for t in writes:
            t.w = mark
            t.r = {}

    EPOCH = 30000

    def op(self, e, fn, reads=(), writes=()):
        reads = list(reads) + [t for t in self.consts.values() if t not in writes]
        self._deps(e, reads, writes)
        inst = fn(self.eng[e])
        k = self.cur[e]
        if self.cnt[k] >= self.EPOCH:
            self.nep[e] += 1
            k = f"{e}#{self.nep[e]}"
            self.sem[k] = self.es.enter_context(self.nc.semaphore("s_" + k.replace("#", "_")))
            self.cnt[k] = 0
            self.own[k] = e
            self.cur[e] = k
        self.cnt[k] += 1
        inst.then_inc(self.sem[k], 1)
        self._mark((k, self.cnt[k]), reads, writes)
        self.n_inst += 1
        return inst

    def dma(self, q, out, in_, reads=(), writes=(), **kw):
        lst, i = self.dq[q]
        k = lst[i % len(lst)]
        self.dq[q][1] = i + 1
        if self.cnt[k] > 0:
            self._wait(q, (k, self.cnt[k]))
        self._deps(q, reads, writes)
        inst = self.eng[q].dma_start(out=out, in_=in_, **kw)
        self.cnt[k] += 16
        inst.then_inc(self.sem[k], 16)
        self._mark((k, self.cnt[k]), reads, writes)
        self.n_inst += 1
        return inst

    def finish(self, outs):
        for t in outs:
            if t.w is not None:
                self._wait("sp", t.w)
        for e in ("pe", "act", "dve", "pool"):
            k = self.cur[e]
            if self.cnt[k]:
                self._wait("sp", (k, self.cnt[k]))
        for q in self.dq:
            for k in self.dq[q][0]:
                if self.cnt[k]:
                    self._wait("sp", (k, self.cnt[k]))

    def close(self):
        self.es.close()


def run(kb, in_maps):
    res = run_bass_kernel_spmd(kb.nc, in_maps, core_ids=list(range(len(in_maps))))
    return res.results


D = 2048
TC = TCP
TILES = TILES_A
EPS = 1e-6


def norm_rstd(kb, ps_ss, TT, rs_t, tmp_t, eps=EPS, n=D):
    kb.op("act", lambda e: e.activation(out=tmp_t[:, :TT], in_=ps_ss[:, :TT], func=AF.Sqrt, scale=1.0 / n, bias=eps_ap(kb, eps)),
          reads=[ps_ss], writes=[tmp_t])
    kb.op("dve", lambda e: e.reciprocal(out=rs_t[:, :TT], in_=tmp_t[:, :TT]), reads=[tmp_t], writes=[rs_t])


def eps_ap(kb, val):
    if val not in kb.consts:
        t = kb.sb([128, 1], F32)
        kb.consts[val] = t
        kb.op("pool", lambda e: e.memset(t[:], val), writes=[t])
    return kb.consts[val][:, 0:1]


def emit_c(kb, h, p1, p2, mods, ng, bout, wout, wg, wu, wo, h2, mods_off=0):
    has_p2 = p2 is not None
    eps_ap(kb, EPS)
    mods_s = kb.sb([128, 96, 2]); ng_s = kb.sb([128, 4, 16]); bout_s = kb.sb([128, 16])
    CG1 = kb.sb([128, 2, 16]); SC2 = kb.sb([128, 2, 16]); SH2 = kb.sb([128, 2, 16]); CG3 = kb.sb([128, 2, 16])
    ones = kb.sb([128, 128], BF16)
    kb.op("pool", lambda e: e.memset(ones[:], 1.0), writes=[ones])
    epsT = eps_ap(kb, EPS)
    kb.dma("sp", mods_s[:], mods[:, mods_off:mods_off + 96, :], reads=[mods], writes=[mods_s])
    kb.dma("sp", ng_s[:], ng[:], reads=[ng], writes=[ng_s])
    kb.dma("sp", bout_s[:], bout[:], reads=[bout], writes=[bout_s])
    for ms in range(2):
        def m(i):
            return mods_s[:, i * 16:(i + 1) * 16, ms]
        kb.op("dve", lambda e: e.tensor_tensor(out=CG1[:, ms, :], in0=m(2), in1=ng_s[:, 1, :], op=ALU.mult), reads=[mods_s, ng_s], writes=[CG1])
        kb.op("dve", lambda e: e.scalar_tensor_tensor(out=SC2[:, ms, :], in0=m(4), scalar=1.0, in1=ng_s[:, 2, :], op0=ALU.add, op1=ALU.mult), reads=[mods_s, ng_s], writes=[SC2])
        kb.op("dve", lambda e: e.tensor_copy(out=SH2[:, ms, :], in_=m(3)), reads=[mods_s], writes=[SH2])
        kb.op("dve", lambda e: e.tensor_tensor(out=CG3[:, ms, :], in0=m(5), in1=ng_s[:, 3, :], op=ALU.mult), reads=[mods_s, ng_s], writes=[CG3])

    hs = kb.sb([128, 16, 512]); y = kb.sb([128, 16, 512]); pb = kb.sb([128, 16, 512], BF16)
    hid = kb.sb([128, 22, 512], BF16)
    sa = [kb.sb([128, 512]) for _ in range(2)]; sb_ = [kb.sb([128, 512]) for _ in range(2)]
    sq = [kb.sb([128, 512], BF16) for _ in range(2)]
    tmp = [kb.sb([128, 512]) for _ in range(2)]
    rs = kb.sb([128, 512]); rtmp = kb.sb([128, 512])
    wt = [kb.sb([128, 16, 128], BF16) for _ in range(2)]
    wgt = [kb.sb([128, 16, 128], BF16) for _ in range(2)]; wut = [kb.sb([128, 16, 128], BF16) for _ in range(2)]
    wot = [kb.sb([128, 22, 128], BF16) for _ in range(2)]
    ps_mm = [kb.ps() for _ in range(2)]; ps_g = [kb.ps() for _ in range(2)]; ps_u = [kb.ps() for _ in range(2)]
    ps_ss = kb.ps()
    ctr = dict(sq=0, tmp=0, mm=0, w=0, gu=0, wo=0, io=0)

    def sumsq_step(src_ap, src_t, TT, first, last):
        s = sq[ctr["sq"] % 2]; ctr["sq"] += 1
        kb.op("act", lambda e: e.activation(out=s[:, :TT], in_=src_ap, func=AF.Square), reads=[src_t], writes=[s])
        kb.op("pe", lambda e: e.matmul(ps_ss[:, :TT], ones[:], s[:, :TT], start=first, stop=last), reads=[ones, s], writes=[ps_ss])

    for (t0, TT, ms) in TILES:
        kb.dma("sp", hs[:, :, :TT], h[:, :, t0:t0 + TT], reads=[h], writes=[hs])
        for kc in range(16):
            a = sa[ctr["io"] % 2]; b = sb_[ctr["io"] % 2]; ctr["io"] += 1
            kb.dma("sp", a[:, :TT], p1[:, kc, t0:t0 + TT], reads=[p1], writes=[a])
            if has_p2:
                kb.dma("sp", b[:, :TT], p2[:, kc, t0:t0 + TT], reads=[p2], writes=[b])
                kb.op("dve", lambda e: e.tensor_tensor(out=pb[:, kc, :TT], in0=a[:, :TT], in1=b[:, :TT], op=ALU.mult), reads=[a, b], writes=[pb])
            else:
                kb.op("dve", lambda e: e.tensor_copy(out=pb[:, kc, :TT], in_=a[:, :TT]), reads=[a], writes=[pb])
        for n in range(16):
            w_ = wt[ctr["w"] % 2]; ctr["w"] += 1
            kb.dma("pool", w_[:].rearrange("p k n -> p (k n)"), wout[n], reads=[wout], writes=[w_])
            ps = ps_mm[ctr["mm"] % 2]; ctr["mm"] += 1
            for kc in range(16):
                kb.op("pe", lambda e: e.matmul(ps[:, :TT], w_[:, kc, :], pb[:, kc, :TT], start=(kc == 0), stop=(kc == 15)), reads=[w_, pb], writes=[ps])
            kb.op("act", lambda e: e.activation(out=y[:, n, :TT], in_=ps[:, :TT], func=AF.Identity, bias=bout_s[:, n:n + 1]), reads=[ps, bout_s], writes=[y])
            sumsq_step(y[:, n, :TT], y, TT, n == 0, n == 15)
        norm_rstd(kb, ps_ss, TT, rs, rtmp)
        for kc in range(16):
            t_ = tmp[ctr["tmp"] % 2]; ctr["tmp"] += 1
            kb.op("dve", lambda e: e.tensor_tensor(out=t_[:, :TT], in0=y[:, kc, :TT], in1=rs[:, :TT], op=ALU.mult), reads=[y, rs], writes=[t_])
            kb.op("dve", lambda e: e.scalar_tensor_tensor(out=hs[:, kc, :TT], in0=t_[:, :TT], scalar=CG1[:, ms, kc:kc + 1], in1=hs[:, kc, :TT], op0=ALU.mult, op1=ALU.add),
                  reads=[t_, CG1, hs], writes=[hs])
            sumsq_step(hs[:, kc, :TT], hs, TT, kc == 0, kc == 15)
        norm_rstd(kb, ps_ss, TT, rs, rtmp)
        for kc in range(16):
            t_ = tmp[ctr["tmp"] % 2]; ctr["tmp"] += 1
            kb.op("dve", lambda e: e.tensor_tensor(out=t_[:, :TT], in0=hs[:, kc, :TT], in1=rs[:, :TT], op=ALU.mult), reads=[hs, rs], writes=[t_])
            kb.op("act", lambda e: e.activation(out=pb[:, kc, :TT], in_=t_[:, :TT], func=AF.Identity, scale=SC2[:, ms, kc:kc + 1], bias=SH2[:, ms, kc:kc + 1]),
                  reads=[t_, SC2, SH2], writes=[pb])
        for half in range(2):
            for jj in range(22):
                j = half * 22 + jj
                g_ = wgt[ctr["gu"] % 2]; u_ = wut[ctr["gu"] % 2]
                pg = ps_g[ctr["gu"] % 2]; pu = ps_u[ctr["gu"] % 2]; ctr["gu"] += 1
                kb.dma("pool", g_[:].rearrange("p k n -> p (k n)"), wg[j], reads=[wg], writes=[g_])
                kb.dma("pool", u_[:].rearrange("p k n -> p (k n)"), wu[j], reads=[wu], writes=[u_])
                for kc in range(16):
                    kb.op("pe", lambda e: e.matmul(pg[:, :TT], g_[:, kc, :], pb[:, kc, :TT], start=(kc == 0), stop=(kc == 15)), reads=[g_, pb], writes=[pg])
                for kc in range(16):
                    kb.op("pe", lambda e: e.matmul(pu[:, :TT], u_[:, kc, :], pb[:, kc, :TT], start=(kc == 0), stop=(kc == 15)), reads=[u_, pb], writes=[pu])
                t_ = tmp[ctr["tmp"] % 2]; ctr["tmp"] += 1
                kb.op("act", lambda e: e.activation(out=t_[:, :TT], in_=pg[:, :TT], func=AF.Silu), reads=[pg], writes=[t_])
                kb.op("dve", lambda e: e.tensor_tensor(out=hid[:, jj, :TT], in0=t_[:, :TT], in1=pu[:, :TT], op=ALU.mult), reads=[t_, pu], writes=[hid])
            for n in range(16):
                w_ = wot[ctr["wo"] % 2]; ctr["wo"] += 1
                kb.dma("pool", w_[:].rearrange("p k n -> p (k n)"), wo[n, half], reads=[wo], writes=[w_])
                ps = ps_mm[ctr["mm"] % 2]; ctr["mm"] += 1
                for jj in range(22):
                    kb.op("pe", lambda e: e.matmul(ps[:, :TT], w_[:, jj, :], hid[:, jj, :TT], start=(jj == 0), stop=(jj == 21)), reads=[w_, hid], writes=[ps])
                if half == 0:
                    kb.op("act", lambda e: e.activation(out=y[:, n, :TT], in_=ps[:, :TT], func=AF.Identity), reads=[ps], writes=[y])
                else:
                    kb.op("dve", lambda e: e.tensor_tensor(out=y[:, n, :TT], in0=y[:, n, :TT], in1=ps[:, :TT], op=ALU.add), reads=[ps, y], writes=[y])
                    sumsq_step(y[:, n, :TT], y, TT, n == 0, n == 15)
        norm_rstd(kb, ps_ss, TT, rs, rtmp)
        for kc in range(16):
            t_ = tmp[ctr["tmp"] % 2]; ctr["tmp"] += 1
            kb.op("dve", lambda e: e.tensor_tensor(out=t_[:, :TT], in0=y[:, kc, :TT], in1=rs[:, :TT], op=ALU.mult), reads=[y, rs], writes=[t_])
            kb.op("dve", lambda e: e.scalar_tensor_tensor(out=hs[:, kc, :TT], in0=t_[:, :TT], scalar=CG3[:, ms, kc:kc + 1], in1=hs[:, kc, :TT], op0=ALU.mult, op1=ALU.add),
                  reads=[t_, CG3, hs], writes=[hs])
        kb.dma("sp", h2[:, :, t0:t0 + TT], hs[:, :, :TT], reads=[hs], writes=[h2])


def fm(xT):
    n = xT.shape[0] // 128
    return np.ascontiguousarray(xT.reshape(n, 128, -1).transpose(1, 0, 2))


def prep_c_weights(w_out, ffn_w_in, ffn_w_out):
    wout = np.ascontiguousarray(w_out.reshape(16, 128, 16, 128).transpose(2, 1, 0, 3).reshape(16, 128, 2048))
    wi = ffn_w_in.reshape(16, 128, 88, 128).transpose(2, 1, 0, 3).reshape(88, 128, 2048)
    wg = np.ascontiguousarray(wi[:44]); wu = np.ascontiguousarray(wi[44:])
    wo = ffn_w_out.reshape(2, 22, 128, 16, 128).transpose(3, 0, 2, 1, 4).reshape(16, 2, 128, 22 * 128)
    return dict(wout=wout, wg=wg, wu=wu, wo=np.ascontiguousarray(wo))


def premod_setup(kb, mods, ng, mi_shift, mi_scale, gi, mods_off=0):
    mods_s = kb.sb([128, 96, 2]); ng_s = kb.sb([128, 4, 16])
    kb.dma("sp", mods_s[:], mods[:, mods_off:mods_off + 96, :], reads=[mods], writes=[mods_s])
    kb.dma("sp", ng_s[:], ng[:], reads=[ng], writes=[ng_s])
    SC = kb.sb([128, 2, 16]); SH = kb.sb([128, 2, 16])
    for ms in range(2):
        kb.op("dve", lambda e: e.scalar_tensor_tensor(out=SC[:, ms, :], in0=mods_s[:, mi_scale * 16:(mi_scale + 1) * 16, ms], scalar=1.0, in1=ng_s[:, gi, :], op0=ALU.add, op1=ALU.mult),
              reads=[mods_s, ng_s], writes=[SC])
        kb.op("dve", lambda e: e.tensor_copy(out=SH[:, ms, :], in_=mods_s[:, mi_shift * 16:(mi_shift + 1) * 16, ms]), reads=[mods_s], writes=[SH])
    return SC, SH


class PreMod:
    def __init__(self, kb, h, mods, ng, out_dt=BF16, hs_w=512, mods_off=0):
        self.kb = kb; self.h = h
        eps_ap(kb, EPS)
        self.SC, self.SH = premod_setup(kb, mods, ng, 0, 1, 0, mods_off)
        self.hs = kb.sb([128, 16, hs_w]); self.u = kb.sb([128, 16, 512], out_dt)
        self.sq = [kb.sb([128, 512], BF16) for _ in range(2)]
        self.tmp = [kb.sb([128, 512]) for _ in range(2)]
        self.rs = kb.sb([128, 512]); self.rtmp = kb.sb([128, 512])
        self.ones = kb.sb([128, 128], BF16)
        kb.op("pool", lambda e: e.memset(self.ones[:], 1.0), writes=[self.ones])
        self.ps_ss = kb.ps()
        self.c = 0

    def tile(self, t0, TT, ms):
        kb = self.kb; hs = self.hs
        kb.dma("sp", hs[:, :, :TT], self.h[:, :, t0:t0 + TT], reads=[self.h], writes=[hs])
        for kc in range(16):
            s = self.sq[self.c % 2]; self.c += 1
            kb.op("act", lambda e: e.activation(out=s[:, :TT], in_=hs[:, kc, :TT], func=AF.Square), reads=[hs], writes=[s])
            kb.op("pe", lambda e: e.matmul(self.ps_ss[:, :TT], self.ones[:], s[:, :TT], start=(kc == 0), stop=(kc == 15)), reads=[self.ones, s], writes=[self.ps_ss])
        norm_rstd(kb, self.ps_ss, TT, self.rs, self.rtmp)
        for kc in range(16):
            t_ = self.tmp[self.c % 2]; self.c += 1
            kb.op("dve", lambda e: e.tensor_tensor(out=t_[:, :TT], in0=hs[:, kc, :TT], in1=self.rs[:, :TT], op=ALU.mult), reads=[hs, self.rs], writes=[t_])
            kb.op("act", lambda e: e.activation(out=self.u[:, kc, :TT], in_=t_[:, :TT], func=AF.Identity, scale=self.SC[:, ms, kc:kc + 1], bias=self.SH[:, ms, kc:kc + 1]),
                  reads=[t_, self.SC, self.SH], writes=[self.u])
        return self.u


def emit_lru_a(kb, h, mods, ng, w, bin_, gate, rec, mods_off=0):
    pm = PreMod(kb, h, mods, ng, mods_off=mods_off)
    b_s = kb.sb([128, 32])
    kb.dma("sp", b_s[:], bin_[:], reads=[bin_], writes=[b_s])
    wt = [kb.sb([128, 16, 128], BF16) for _ in range(2)]
    pss = [kb.ps() for _ in range(2)]
    z = [kb.sb([128, 512]) for _ in range(2)]; z2 = [kb.sb([128, 512]) for _ in range(2)]
    o = [kb.sb([128, 512]) for _ in range(3)]
    i = 0
    for (t0, TT, ms) in TILES_A:
        u = pm.tile(t0, TT, ms)
        for n in range(32):
            w_ = wt[i % 2]; ps = pss[i % 2]; zz = z[i % 2]; zq = z2[i % 2]; oo = o[i % 3]; i += 1
            kb.dma("pool", w_[:].rearrange("p k n -> p (k n)"), w[n], reads=[w], writes=[w_])
            for kc in range(16):
                kb.op("pe", lambda e: e.matmul(ps[:, :TT], w_[:, kc, :], u[:, kc, :TT], start=(kc == 0), stop=(kc == 15)), reads=[w_, u], writes=[ps])
            if n < 16:
                kb.op("act", lambda e: e.activation(out=zz[:, :TT], in_=ps[:, :TT], func=AF.Identity, bias=b_s[:, n:n + 1]), reads=[ps, b_s], writes=[zz])
                kb.op("act", lambda e: e.activation(out=zq[:, :TT], in_=ps[:, :TT], func=AF.Square, bias=b_s[:, n:n + 1]), reads=[ps, b_s], writes=[zq])
                kb.op("dve", lambda e: e.tensor_scalar(out=zq[:, :TT], in0=zq[:, :TT], scalar1=0.044715, scalar2=1.0, op0=ALU.mult, op1=ALU.add), reads=[zq], writes=[zq])
                kb.op("dve", lambda e: e.tensor_tensor(out=zq[:, :TT], in0=zq[:, :TT], in1=zz[:, :TT], op=ALU.mult), reads=[zq, zz], writes=[zq])
                kb.op("act", lambda e: e.activation(out=zq[:, :TT], in_=zq[:, :TT], func=AF.Sigmoid, scale=1.5957691216057308), reads=[zq], writes=[zq])
                kb.op("dve", lambda e: e.tensor_tensor(out=oo[:, :TT], in0=zq[:, :TT], in1=zz[:, :TT], op=ALU.mult), reads=[zq, zz], writes=[oo])
                kb.dma("sp", gate[:, n, t0:t0 + TT], oo[:, :TT], reads=[oo], writes=[gate])
            else:
                kb.op("act", lambda e: e.activation(out=oo[:, :TT], in_=ps[:, :TT], func=AF.Identity, bias=b_s[:, n:n + 1]), reads=[ps, b_s], writes=[oo])
                kb.dma("sp", rec[:, n - 16, t0:t0 + TT], oo[:, :TT], reads=[oo], writes=[rec])


NS = 8448
CTXN = 256
OFF_C = 2
OFF_L = 261
BUFW = 8454


def emit_lru_b(kb, rec, cw, cb, gw, gb, ll, hsum, col_major):
    NB = 16
    cw_s = kb.sb([128, NB, 4]); cb_s = kb.sb([128, NB]); gwb = [kb.sb([128, 2, 2, 128]) for _ in range(2)]; gb_s = kb.sb([128, NB, 2, 2]); ll_s = kb.sb([128, NB, 2])
    for s, d in ((cw_s, cw), (cb_s, cb), (gb_s, gb), (ll_s, ll)):
        kb.dma("sp", s[:], d[:], reads=[d], writes=[s])
    one_ap = eps_ap(kb, 1.0)
    cl = kb.sb([128, NB, 2]); cl_t = kb.sb([128, NB, 2])
    kb.op("act", lambda e: e.activation(out=cl_t[:], in_=ll_s[:], func=AF.Exp, scale=-1.0), reads=[ll_s], writes=[cl_t])
    kb.op("act", lambda e: e.activation(out=cl_t[:], in_=cl_t[:], func=AF.Ln, bias=one_ap), reads=[cl_t], writes=[cl_t])
    kb.op("dve", lambda e: e.tensor_scalar(out=cl[:], in0=cl_t[:], scalar1=-8.0, scalar2=None, op0=ALU.mult), reads=[cl_t], writes=[cl])

    buf = kb.sb([128, BUFW]); x = kb.sb([128, NS]); a = kb.sb([128, NS]); bt = kb.sb([128, NS]); o0 = kb.sb([128, NS])
    o1 = buf
    pss = [kb.ps() for _ in range(4)]
    tr = [kb.sb([128, 512]) for _ in range(2)]; ti = [kb.sb([128, 512]) for _ in range(2)]; tq = [kb.sb([128, 512]) for _ in range(2)]
    ttiles = [(0, 256)] + [(256 + 512 * i, 512) for i in range(16)]
    segs = [(0, CTXN, OFF_C), (CTXN, NS - CTXN, OFF_L)]
    pieces = [(0, 256)] + [(256 + 2048 * i, 2048) for i in range(4)]
    c = 0
    for blk in range(NB):
        for b in range(1):
            gw_s = gwb[blk % 2]
            kb.dma("sp", gw_s[:], gw[:, blk], reads=[gw], writes=[gw_s])
            kb.op("pool", lambda e: e.memset(buf[:, 0:OFF_C], 0.0), writes=[buf])
            kb.op("pool", lambda e: e.memset(buf[:, OFF_C + CTXN:OFF_L], 0.0), writes=[buf])
            kb.op("pool", lambda e: e.memset(buf[:, OFF_L + NS - CTXN:BUFW], 0.0), writes=[buf])
            kb.dma("sp", buf[:, OFF_C:OFF_C + CTXN], rec[:, blk, LATC:LATC + CTXN], reads=[rec], writes=[buf])
            if col_major:
                kb.dma("sp", a[:, 0:LATC], rec[:, blk, 0:LATC], reads=[rec], writes=[a])
                kb.op("pool", lambda e: e.tensor_copy(out=buf[:, OFF_L:OFF_L + LATC].rearrange("p (c r) -> p c r", c=64, r=128), in_=a[:, 0:LATC].rearrange("p (r c) -> p c r", r=128, c=64)), reads=[a], writes=[buf])
            else:
                kb.dma("sp", buf[:, OFF_L:OFF_L + LATC], rec[:, blk, 0:LATC], reads=[rec], writes=[buf])
            for (s0, sl, off) in segs:
                kb.op("act", lambda e: e.activation(out=x[:, s0:s0 + sl], in_=buf[:, off - 2:off - 2 + sl], func=AF.Identity, scale=cw_s[:, blk, 0:1], bias=cb_s[:, blk:blk + 1]),
                      reads=[buf, cw_s, cb_s], writes=[x])
                for j in range(1, 4):
                    kb.op("dve", lambda e: e.scalar_tensor_tensor(out=x[:, s0:s0 + sl], in0=buf[:, off - 2 + j:off - 2 + j + sl], scalar=cw_s[:, blk, j:j + 1], in1=x[:, s0:s0 + sl], op0=ALU.mult, op1=ALU.add),
                          reads=[buf, cw_s, x], writes=[x])
            for d in range(2):
                for (t0, TT) in ttiles:
                    pr = pss[c % 2]; pi = pss[2 + c % 2]; r_ = tr[c % 2]; i_ = ti[c % 2]; q_ = tq[c % 2]; c += 1
                    kb.op("pe", lambda e: e.matmul(pr[:, :TT], gw_s[:, d, 0, :], x[:, t0:t0 + TT], start=True, stop=True), reads=[gw_s, x], writes=[pr])
                    kb.op("pe", lambda e: e.matmul(pi[:, :TT], gw_s[:, d, 1, :], x[:, t0:t0 + TT], start=True, stop=True), reads=[gw_s, x], writes=[pi])
                    kb.op("act", lambda e: e.activation(out=r_[:, :TT], in_=pr[:, :TT], func=AF.Sigmoid, bias=gb_s[:, blk, d, 0:1]), reads=[pr, gb_s], writes=[r_])
                    kb.op("act", lambda e: e.activation(out=i_[:, :TT], in_=pi[:, :TT], func=AF.Sigmoid, bias=gb_s[:, blk, d, 1:2]), reads=[pi, gb_s], writes=[i_])
                    kb.op("act", lambda e: e.activation(out=a[:, t0:t0 + TT], in_=r_[:, :TT], func=AF.Exp, scale=cl[:, blk, d:d + 1]), reads=[r_, cl], writes=[a])
                    kb.op("act", lambda e: e.activation(out=q_[:, :TT], in_=a[:, t0:t0 + TT], func=AF.Square), reads=[a], writes=[q_])
                    kb.op("dve", lambda e: e.tensor_scalar(out=q_[:, :TT], in0=q_[:, :TT], scalar1=-1.0, scalar2=1.0, op0=ALU.mult, op1=ALU.add), reads=[q_], writes=[q_])
                    kb.op("act", lambda e: e.activation(out=q_[:, :TT], in_=q_[:, :TT], func=AF.Sqrt), reads=[q_], writes=[q_])
                    kb.op("dve", lambda e: e.tensor_tensor(out=i_[:, :TT], in0=i_[:, :TT], in1=x[:, t0:t0 + TT], op=ALU.mult), reads=[i_, x], writes=[i_])
                    kb.op("dve", lambda e: e.tensor_tensor(out=bt[:, t0:t0 + TT], in0=i_[:, :TT], in1=q_[:, :TT], op=ALU.mult), reads=[i_, q_], writes=[bt])
                if d == 0:
                    prev = None
                    for (p0, pl) in pieces:
                        init = 0.0 if prev is None else o0[:, prev - 1:prev]
                        kb.op("dve", lambda e: e.tensor_tensor_scan(out=o0[:, p0:p0 + pl], data0=a[:, p0:p0 + pl], data1=bt[:, p0:p0 + pl], initial=init, op0=ALU.mult, op1=ALU.add),
                              reads=[a, bt, o0], writes=[o0])
                        prev = p0 + pl
                else:
                    prev = None
                    for (p0, pl) in [pieces[0]] + pieces[:0:-1]:
                        init = 0.0 if prev is None else o1[:, prev:prev + 1]
                        kb.op("dve", lambda e: e.tensor_tensor_scan(out=o1[:, p0:p0 + pl][:, ::-1], data0=a[:, p0:p0 + pl][:, ::-1], data1=bt[:, p0:p0 + pl][:, ::-1], initial=init, op0=ALU.mult, op1=ALU.add),
                              reads=[a, bt, o1], writes=[o1])
                        prev = p0
            for (p0, pl) in pieces:
                kb.op("pool", lambda e: e.tensor_tensor(out=o0[:, p0:p0 + pl], in0=o0[:, p0:p0 + pl], in1=o1[:, p0:p0 + pl], op=ALU.add), reads=[o0, o1], writes=[o0])
            kb.dma("sp", hsum[:, blk, LATC:LATC + CTXN], o0[:, 0:CTXN], reads=[o0], writes=[hsum])
            if col_major:
                kb.op("pool", lambda e: e.tensor_copy(out=x[:, 0:LATC].rearrange("p (r c) -> p c r", r=128, c=64), in_=o0[:, CTXN:NS].rearrange("p (c r) -> p c r", c=64, r=128)), reads=[o0], writes=[x])
                kb.dma("sp", hsum[:, blk, 0:LATC], x[:, 0:LATC], reads=[x], writes=[hsum])
            else:
                kb.dma("sp", hsum[:, blk, 0:LATC], o0[:, CTXN:NS], reads=[o0], writes=[hsum])


def emit_gla_a(kb, h, mods, ng, w, w1, br, qkv, sr, gl, mods_off=0):
    pm = PreMod(kb, h, mods, ng, mods_off=mods_off)
    br_s = kb.sb([128, 16]); w1_s = kb.sb([128, 16, 32], BF16)
    kb.dma("sp", br_s[:], br[:], reads=[br], writes=[br_s])
    kb.dma("pool", w1_s[:].rearrange("p k n -> p (k n)"), w1[:], reads=[w1], writes=[w1_s])
    wt = [kb.sb([128, 16, 128], BF16) for _ in range(2)]
    pss = [kb.ps() for _ in range(2)]
    o = [kb.sb([128, 512]) for _ in range(3)]
    i = 0
    for (t0, TT, ms) in TILES_A:
        u = pm.tile(t0, TT, ms)
        ps = pss[i % 2]; oo = o[i % 3]; i += 1
        for kc in range(16):
            kb.op("pe", lambda e: e.matmul(ps[0:32, :TT], w1_s[:, kc, :], u[:, kc, :TT], start=(kc == 0), stop=(kc == 15)), reads=[w1_s, u], writes=[ps])
        kb.op("act", lambda e: e.activation(out=oo[0:32, :TT], in_=ps[0:32, :TT], func=AF.Identity), reads=[ps], writes=[oo])
        kb.dma("sp", gl[:, t0:t0 + TT], oo[0:32, :TT], reads=[oo], writes=[gl])
        for n in range(48):
            w_ = wt[i % 2]; ps = pss[i % 2]; oo = o[i % 3]; i += 1
            kb.dma("pool", w_[:].rearrange("p k n -> p (k n)"), w[n], reads=[w], writes=[w_])
            for kc in range(16):
                kb.op("pe", lambda e: e.matmul(ps[:, :TT], w_[:, kc, :], u[:, kc, :TT], start=(kc == 0), stop=(kc == 15)), reads=[w_, u], writes=[ps])
            if n < 32:
                sc = 1.0 / 16.0 if n < 8 else 1.0
                kb.op("act", lambda e: e.activation(out=oo[:, :TT], in_=ps[:, :TT], func=AF.Identity, scale=sc), reads=[ps], writes=[oo])
                kb.dma("sp", qkv[:, n, t0:t0 + TT], oo[:, :TT], reads=[oo], writes=[qkv])
            else:
                kb.op("act", lambda e: e.activation(out=oo[:, :TT], in_=ps[:, :TT], func=AF.Silu, bias=br_s[:, n - 32:n - 31]), reads=[ps, br_s], writes=[oo])
                kb.dma("sp", sr[:, n - 32, t0:t0 + TT], oo[:, :TT], reads=[oo], writes=[sr])


NCH = NS // 128


def emit_gla_b(kb, hd, qk_s, ktok_s, vtok_s, glT, w2, gbF, gbT, ngT, masks, rmask, ident, o0, pT_s):
    ident_s = kb.sb([128, 128]); otT = [kb.sb([128, 4, 128]) for _ in range(2)]; ps_x = kb.ps()
    kb.dma("sp", ident_s[:], ident[:], reads=[ident], writes=[ident_s])
    w2_s = kb.sb([16, 2, 256]); gbF_s = kb.sb([128, 2, 2]); gbT_s = kb.sb([128, 2, 256]); ngT_s = kb.sb([128, 512])
    masks_s = kb.sb([128, 2, 2, 128]); rmask_s = kb.sb([128, 512])
    kb.dma("sp", w2_s[:], w2[:, :, hd * 256:(hd + 1) * 256], reads=[w2], writes=[w2_s])
    kb.dma("sp", gbF_s[:], gbF[:, :, hd * 2:hd * 2 + 2], reads=[gbF], writes=[gbF_s])
    kb.dma("sp", gbT_s[:], gbT[:, :, hd * 256:(hd + 1) * 256], reads=[gbT], writes=[gbT_s])
    kb.dma("sp", ngT_s[:], ngT[:, hd * 512:(hd + 1) * 512], reads=[ngT], writes=[ngT_s])
    kb.dma("sp", masks_s[:], masks[:], reads=[masks], writes=[masks_s])
    kb.dma("sp", rmask_s[:], rmask[:], reads=[rmask], writes=[rmask_s])
    S = [kb.sb([128, 512]) for _ in range(2)]; Sb = [kb.sb([128, 512], BF16) for _ in range(2)]
    q_g = kb.sb([128, 2, 512]); k_g = kb.sb([128, 2, 512]); gl_g = kb.sb([16, 512])
    gF = kb.sb([128, 2, 512]); bc = kb.sb([128, 2, 512]); Eq = kb.sb([128, 2, 512]); Ek = kb.sb([128, 2, 512])
    qe = kb.sb([128, 2, 512], BF16); ke = kb.sb([128, 2, 512], BF16)
    kt = [kb.sb([128, 256]) for _ in range(2)]; vt = [kb.sb([128, 512], BF16) for _ in range(2)]
    gt = [kb.sb([128, 256]) for _ in range(2)]; krem = [kb.sb([128, 256], BF16) for _ in range(2)]
    attb = [kb.sb([128, 128], BF16) for _ in range(2)]
    ot = [kb.sb([128, 512]) for _ in range(2)]; o0t = [kb.sb([128, 512]) for _ in range(2)]; junk = kb.sb([128, 512])
    ssq = [kb.sb([128, 1]) for _ in range(2)]
    ps_g = kb.ps(); ps_t = kb.ps(); ps_a = kb.ps(); ps_o = [kb.ps() for _ in range(2)]; ps_s = [kb.ps() for _ in range(2)]
    eps_t = eps_ap(kb, EPS)
    ci = 0
    for d in range(2):
        for kc in range(2):
            kb.op("dve", lambda e: e.memset(S[kc][:], 0.0), writes=[S[kc]])
            kb.op("dve", lambda e: e.memset(Sb[kc][:], 0.0), writes=[Sb[kc]])
        groups = [(0, 2)] + [(2 + 4 * g, 4) for g in range(16)]
        if d == 1:
            groups = [groups[0]] + groups[:0:-1]
        for (c0, nc_) in groups:
            t0 = c0 * 128; TT = nc_ * 128
            kb.dma("sp", q_g[:, :, :TT], qk_s[:, hd * 2:hd * 2 + 2, t0:t0 + TT], reads=[qk_s], writes=[q_g])
            kb.dma("sp", k_g[:, :, :TT], qk_s[:, 8 + hd * 2:8 + hd * 2 + 2, t0:t0 + TT], reads=[qk_s], writes=[k_g])
            kb.dma("sp", gl_g[:, :TT], glT[:, d, t0:t0 + TT], reads=[glT], writes=[gl_g])
            for kc in range(2):
                kb.op("pe", lambda e: e.matmul(ps_g[:, :TT], w2_s[:, d, kc * 128:(kc + 1) * 128], gl_g[:, :TT], start=True, stop=True), reads=[w2_s, gl_g], writes=[ps_g])
                kb.op("act", lambda e: e.activation(out=gF[:, kc, :TT], in_=ps_g[:, :TT], func=AF.Sigmoid, bias=gbF_s[:, d, kc:kc + 1]), reads=[ps_g, gbF_s], writes=[gF])
            kb.op("act", lambda e: e.activation(out=gF[:, :, :TT], in_=gF[:, :, :TT], func=AF.Ln), reads=[gF], writes=[gF])
            kb.op("dve", lambda e: e.tensor_scalar(out=gF[:, :, :TT], in0=gF[:, :, :TT], scalar1=1.0 / 16.0, scalar2=None, op0=ALU.mult), reads=[gF], writes=[gF])
            for kc in range(2):
                if d == 0:
                    kb.op("dve", lambda e: e.tensor_tensor_scan(out=bc[:, kc, :TT], data0=rmask_s[:, :TT], data1=gF[:, kc, :TT], initial=0.0, op0=ALU.mult, op1=ALU.add),
                          reads=[rmask_s, gF], writes=[bc])
                else:
                    kb.op("dve", lambda e: e.tensor_tensor_scan(out=bc[:, kc, :TT][:, ::-1], data0=rmask_s[:, :TT], data1=gF[:, kc, :TT][:, ::-1], initial=0.0, op0=ALU.mult, op1=ALU.add),
                          reads=[rmask_s, gF], writes=[bc])
            kb.op("act", lambda e: e.activation(out=Eq[:, :, :TT], in_=bc[:, :, :TT], func=AF.Exp), reads=[bc], writes=[Eq])
            kb.op("act", lambda e: e.activation(out=Ek[:, :, :TT], in_=bc[:, :, :TT], func=AF.Exp, scale=-1.0), reads=[bc], writes=[Ek])
            kb.op("dve", lambda e: e.tensor_tensor(out=qe[:, :, :TT], in0=q_g[:, :, :TT], in1=Eq[:, :, :TT], op=ALU.mult), reads=[q_g, Eq], writes=[qe])
            kb.op("dve", lambda e: e.tensor_tensor(out=ke[:, :, :TT], in0=k_g[:, :, :TT], in1=Ek[:, :, :TT], op=ALU.mult), reads=[k_g, Ek], writes=[ke])
            chunks = list(range(nc_)) if d == 0 else list(range(nc_ - 1, -1, -1))
            for cc in chunks:
                c = c0 + cc; a0 = cc * 128; tt0 = c * 128
                kt_ = kt[ci % 2]; vt_ = vt[ci % 2]; gt_ = gt[ci % 2]; kr_ = krem[ci % 2]; ab_ = attb[ci % 2]
                ot_ = ot[ci % 2]; o0_ = o0t[ci % 2]; pso = ps_o[ci % 2]; ss_ = ssq[ci % 2]; ci += 1
                kb.dma("sp", kt_[:], ktok_s[tt0:tt0 + 128, hd * 256:(hd + 1) * 256], reads=[ktok_s], writes=[kt_])
                kb.dma("pool", vt_[:], vtok_s[tt0:tt0 + 128, hd * 512:(hd + 1) * 512], reads=[vtok_s], writes=[vt_])
                kb.op("pe", lambda e: e.matmul(ps_t[:, 0:256], gl_g[:, a0:a0 + 128], w2_s[:, d, :], start=True, stop=True), reads=[gl_g, w2_s], writes=[ps_t])
                kb.op("dve", lambda e: e.tensor_tensor(out=gt_[:], in0=ps_t[:, 0:256], in1=gbT_s[:, d, :], op=ALU.add), reads=[ps_t, gbT_s], writes=[gt_])
                kb.op("act", lambda e: e.activation(out=gt_[:], in_=gt_[:], func=AF.Sigmoid), reads=[gt_], writes=[gt_])
                kb.op("act", lambda e: e.activation(out=gt_[:], in_=gt_[:], func=AF.Ln), reads=[gt_], writes=[gt_])
                kb.op("pe", lambda e: e.matmul(ps_t[:, 256:512], masks_s[:, d, 0, :], gt_[:], start=True, stop=True), reads=[masks_s, gt_], writes=[ps_t])
                kb.op("act", lambda e: e.activation(out=gt_[:], in_=ps_t[:, 256:512], func=AF.Exp), reads=[ps_t], writes=[gt_])
                kb.op("dve", lambda e: e.tensor_tensor(out=kr_[:], in0=kt_[:], in1=gt_[:], op=ALU.mult), reads=[kt_, gt_], writes=[kr_])
                for kc in range(2):
                    kb.op("pe", lambda e: e.matmul(ps_a[:, 0:128], ke[:, kc, a0:a0 + 128], qe[:, kc, a0:a0 + 128], start=(kc == 0), stop=(kc == 1)), reads=[ke, qe], writes=[ps_a])
                kb.op("dve", lambda e: e.tensor_tensor(out=ab_[:], in0=ps_a[:, 0:128], in1=masks_s[:, d, 1, :], op=ALU.mult), reads=[ps_a, masks_s], writes=[ab_])
                kb.op("pe", lambda e: e.matmul(pso[:], ab_[:], vt_[:], start=True, stop=False), reads=[ab_, vt_], writes=[pso])
                for kc in range(2):
                    kb.op("pe", lambda e: e.matmul(pso[:], qe[:, kc, a0:a0 + 128], Sb[kc][:], start=False, stop=(kc == 1)), reads=[qe, Sb[kc]], writes=[pso])
                if d == 0:
                    kb.op("act", lambda e: e.activation(out=ot_[:], in_=pso[:], func=AF.Identity), reads=[pso], writes=[ot_])
                    kb.dma("sp", o0[tt0:tt0 + 128, :], ot_[:], reads=[ot_], writes=[o0])
                else:
                    kb.dma("sp", o0_[:], o0[tt0:tt0 + 128, :], reads=[o0], writes=[o0_])
                    kb.op("dve", lambda e: e.tensor_tensor(out=ot_[:], in0=pso[:], in1=o0_[:], op=ALU.add), reads=[pso, o0_], writes=[ot_])
                    kb.op("act", lambda e: e.activation(out=junk[:], in_=ot_[:], func=AF.Square, accum_out=ss_[:]), reads=[ot_], writes=[junk, ss_])
                    kb.op("act", lambda e: e.activation(out=ss_[:], in_=ss_[:], func=AF.Sqrt, scale=1.0 / 512.0, bias=eps_t), reads=[ss_], writes=[ss_])
                    kb.op("dve", lambda e: e.reciprocal(out=ss_[:], in_=ss_[:]), reads=[ss_], writes=[ss_])
                    kb.op("dve", lambda e: e.scalar_tensor_tensor(out=ot_[:], in0=ot_[:], scalar=ss_[:, 0:1], in1=ngT_s[:], op0=ALU.mult, op1=ALU.mult), reads=[ot_, ss_, ngT_s], writes=[ot_])
                    oT_ = otT[ci % 2]
                    for jx in range(4):
                        kb.op("pe", lambda e: e.matmul(ps_x[:, jx * 128:(jx + 1) * 128], ot_[:, jx * 128:(jx + 1) * 128], ident_s[:], start=True, stop=True), reads=[ot_, ident_s], writes=[ps_x])
                    kb.op("act", lambda e: e.activation(out=oT_[:].rearrange("p a t -> p (a t)"), in_=ps_x[:, 0:512], func=AF.Identity), reads=[ps_x], writes=[oT_])
                    kb.dma("sp", pT_s[:, hd * 4:(hd + 1) * 4, tt0:tt0 + 128], oT_[:], reads=[oT_], writes=[pT_s])
                lastcol = a0 + 127 if d == 0 else a0
                for kc in range(2):
                    pss_ = ps_s[kc]
                    kb.op("pe", lambda e: e.matmul(pss_[:], kr_[:, kc * 128:(kc + 1) * 128], vt_[:], start=True, stop=True), reads=[kr_, vt_], writes=[pss_])
                    kb.op("dve", lambda e: e.scalar_tensor_tensor(out=S[kc][:], in0=S[kc][:], scalar=Eq[:, kc, lastcol:lastcol + 1], in1=pss_[:], op0=ALU.mult, op1=ALU.add),
                          reads=[S[kc], Eq, pss_], writes=[S[kc]])
                    kb.op("act", lambda e: e.activation(out=Sb[kc][:], in_=S[kc][:], func=AF.Identity), reads=[S[kc]], writes=[Sb[kc]])


def gla_b_consts():
    s = np.arange(128)[:, None]; t = np.arange(128)[None, :]
    masks = np.zeros((128, 2, 2, 128), np.float32)
    masks[:, 0, 0, :] = (s > t) / 16.0
    masks[:, 1, 0, :] = (s < t) / 16.0
    masks[:, 0, 1, :] = (s <= t)
    masks[:, 1, 1, :] = (s >= t)
    rmask = np.ones((128, 512), np.float32); rmask[:, ::128] = 0.0
    return masks, rmask


US_W = 8720
US_LAT = 1
US_CTX = 8195


def emit_rwkv_a(kb, h, mods, ng, mu, w, wl, g1, g2, rkv, lora, gout, us, mods_off=0):
    pm = PreMod(kb, h, mods, ng, out_dt=F32, hs_w=516, mods_off=mods_off)
    mu_s = kb.sb([128, 6, 16]); wl_s = kb.sb([128, 16, 384], BF16); g1_s = kb.sb([128, 16, 256], BF16); g2_s = kb.sb([128, 16, 2, 128], BF16)
    kb.dma("sp", mu_s[:], mu[:], reads=[mu], writes=[mu_s])
    kb.dma("pool", wl_s[:].rearrange("p k n -> p (k n)"), wl[:], reads=[wl], writes=[wl_s])
    kb.dma("pool", g1_s[:].rearrange("p k n -> p (k n)"), g1[:], reads=[g1], writes=[g1_s])
    for n in range(16):
        kb.dma("pool", g2_s[:, n].rearrange("p k n -> p (k n)"), g2[n], reads=[g2], writes=[g2_s])
    zt = pm.hs
    kb.op("dve", lambda e: e.memset(zt[:], 0.0), writes=[zt])
    c0 = 0
    while c0 < US_W:
        wd = min(516, US_W - c0)
        kb.dma("sp", us[:, :, c0:c0 + wd], zt[:, :, :wd], reads=[zt], writes=[us])
        c0 += wd
    for (t0, TT, ms) in TILES_A:
        u32 = pm.tile(t0, TT, ms)
        if ms == 1:
            kb.dma("sp", us[:, :, US_LAT + t0:US_LAT + t0 + TT], u32[:, :, :TT], reads=[u32], writes=[us])
        else:
            kb.dma("sp", us[:, :, US_CTX:US_CTX + CTXC], u32[:, :, :CTXC], reads=[u32], writes=[us])
    uc = pm.hs; dx = pm.u
    xm = [kb.sb([128, 16, 512], BF16) for _ in range(2)]
    wt = [kb.sb([128, 16, 128], BF16) for _ in range(2)]
    pss = [kb.ps() for _ in range(2)]
    o = [kb.sb([128, 512]) for _ in range(3)]
    glb = kb.sb([128, 2, 512], BF16)
    i = 0; xi = 0
    for (t0, TT, ms) in TILES_A:
        off = (US_LAT + t0) if ms == 1 else US_CTX
        kb.dma("sp", uc[:, :, 0:TT + 2], us[:, :, off - 1:off + TT + 1], reads=[us], writes=[uc])
        kb.op("dve", lambda e: e.tensor_tensor(out=dx[:, :, :TT], in0=uc[:, :, 0:TT], in1=uc[:, :, 2:TT + 2], op=ALU.add), reads=[uc], writes=[dx])
        kb.op("dve", lambda e: e.scalar_tensor_tensor(out=dx[:, :, :TT], in0=dx[:, :, :TT], scalar=0.5, in1=uc[:, :, 1:TT + 1], op0=ALU.mult, op1=ALU.subtract), reads=[dx, uc], writes=[dx])
        for m in (0, 2, 3, 1, 4, 5):
            x_ = xm[xi % 2]; xi += 1
            for kc in range(16):
                kb.op("dve", lambda e: e.scalar_tensor_tensor(out=x_[:, kc, :TT], in0=dx[:, kc, :TT], scalar=mu_s[:, m, kc:kc + 1], in1=uc[:, kc, 1:TT + 1], op0=ALU.mult, op1=ALU.add),
                      reads=[dx, mu_s, uc], writes=[x_])
            if m in (0, 2, 3):
                base = {0: 0, 2: 16, 3: 32}[m]
                for n in range(16):
                    w_ = wt[i % 2]; ps = pss[i % 2]; oo = o[i % 3]; i += 1
                    kb.dma("pool", w_[:].rearrange("p k n -> p (k n)"), w[base + n], reads=[w], writes=[w_])
                    for kc in range(16):
                        kb.op("pe", lambda e: e.matmul(ps[:, :TT], w_[:, kc, :], x_[:, kc, :TT], start=(kc == 0), stop=(kc == 15)), reads=[w_, x_], writes=[ps])
                    kb.op("act", lambda e: e.activation(out=oo[:, :TT], in_=ps[:, :TT], func=AF.Identity), reads=[ps], writes=[oo])
                    kb.dma("sp", rkv[:, base + n, t0:t0 + TT], oo[:, :TT], reads=[oo], writes=[rkv])
            elif m in (1, 4):
                for d in range(2):
                    col = (0 if m == 1 else 192) + d * 96
                    ps = pss[i % 2]; oo = o[i % 3]; i += 1
                    for kc in range(16):
                        kb.op("pe", lambda e: e.matmul(ps[0:96, :TT], wl_s[:, kc, col:col + 96], x_[:, kc, :TT], start=(kc == 0), stop=(kc == 15)), reads=[wl_s, x_], writes=[ps])
                    kb.op("act", lambda e: e.activation(out=oo[0:96, :TT], in_=ps[0:96, :TT], func=(AF.Tanh if m == 1 else AF.Identity)), reads=[ps], writes=[oo])
                    kb.dma("sp", lora[:, (0 if m == 1 else 2) + d, t0:t0 + TT], oo[0:96, :TT], reads=[oo], writes=[lora])
            else:
                for c2 in range(2):
                    ps = pss[i % 2]; i += 1
                    for kc in range(16):
                        kb.op("pe", lambda e: e.matmul(ps[:, :TT], g1_s[:, kc, c2 * 128:(c2 + 1) * 128], x_[:, kc, :TT], start=(kc == 0), stop=(kc == 15)), reads=[g1_s, x_], writes=[ps])
                    kb.op("act", lambda e: e.activation(out=glb[:, c2, :TT], in_=ps[:, :TT], func=AF.Sigmoid), reads=[ps], writes=[glb])
                for n in range(16):
                    ps = pss[i % 2]; oo = o[i % 3]; i += 1
                    for kc in range(2):
                        kb.op("pe", lambda e: e.matmul(ps[:, :TT], g2_s[:, n, kc, :], glb[:, kc, :TT], start=(kc == 0), stop=(kc == 1)), reads=[g2_s, glb], writes=[ps])
                    kb.op("act", lambda e: e.activation(out=oo[:, :TT], in_=ps[:, :TT], func=AF.Identity), reads=[ps], writes=[oo])
                    kb.dma("sp", gout[:, n, t0:t0 + TT], oo[:, :TT], reads=[oo], writes=[gout])


NCHK = NS // 128
GN_EPS = 64e-5
P_KK, P_KA, P_RK, P_LNW, P_LNB, P_W0, P_A0 = 0, 1, 2, 3, 4, 5, 7


def emit_rwkv_b(kb, hq, rkv, lo, w2, a2, prm, cst, msk, rmk, y0, bv0, p1):
    stage = 9
    NCHK = NS // 128

    def col(c):
        return LATC + c * 128 if c < 2 else (c - 2) * 128
    w2_s = kb.sb([96, 2, 512]); a2_s = kb.sb([96, 2, 512]); prm_s = kb.sb([128, 9, 4]); cst_s = kb.sb([128, 2, 128])
    msk_s = kb.sb([128, 2, 640]); rmk_s = kb.sb([128, 512])
    kb.dma("sp", w2_s[:], w2[:, :, hq * 512:(hq + 1) * 512], reads=[w2], writes=[w2_s])
    kb.dma("sp", a2_s[:], a2[:, :, hq * 512:(hq + 1) * 512], reads=[a2], writes=[a2_s])
    kb.dma("sp", prm_s[:], prm[:, :, hq * 4:(hq + 1) * 4], reads=[prm], writes=[prm_s])
    for s, d in ((cst_s, cst), (msk_s, msk), (rmk_s, rmk)):
        kb.dma("sp", s[:], d[:], reads=[d], writes=[s])
    bones = cst_s[:, 0, :]; ident = cst_s[:, 1, :]

    def pb(idx):
        return prm_s[:, idx, :].unsqueeze(2).broadcast_to([128, 4, 128])

    X3 = [128, 4, 128]
    R = kb.sb(X3); K = kb.sb(X3); V = kb.sb(X3); LO = kb.sb([96, 4, 128])
    LW = kb.sb(X3); AS = kb.sb(X3); KK = kb.sb(X3); KD = kb.sb(X3); B_ = kb.sb(X3); CUM = kb.sb(X3)
    G = kb.sb(X3); GI = kb.sb(X3); GP = kb.sb(X3); t1 = kb.sb(X3); t2 = kb.sb(X3); BV = kb.sb(X3)
    AR = kb.sb([128, 4, 2, 128]); BK = kb.sb([128, 4, 2, 128])
    YT = kb.sb(X3); Y0 = kb.sb(X3); BV0 = kb.sb(X3)
    Hz = [kb.sb([128, 128]) for _ in range(4)]
    BZ = [kb.sb([128, 128]) for _ in range(2)]; KZ = [kb.sb([128, 128]) for _ in range(2)]
    VZ = [kb.sb([128, 128]) for _ in range(2)]; UZ = [kb.sb([128, 128]) for _ in range(2)]
    for t in BZ + KZ + VZ + UZ:
        kb.op("dve", lambda e: e.memset(t[:], 0.0), writes=[t])
    SC = [kb.sb([128, 512]) for _ in range(2)]
    XX = [kb.sb([128, 2, 2, 128]) for _ in range(2)]
    TT = kb.sb([128, 2, 128]); RHS = [kb.sb([128, 64]) for _ in range(2)]
    ps_z = kb.ps(); ps_tr = kb.ps(); ps_sc = kb.ps(); ps_p = kb.ps(); ps_t = kb.ps(); ps_r = kb.ps(); ps_y = kb.ps(); ps_h = kb.ps()
    eps_gn = eps_ap(kb, GN_EPS)

    def bmm(out_ps, src, src_t):
        kb.op("pe", lambda e: e.matmul(out_ps[:, 0:512], bones, src[:].rearrange("p a t -> p (a t)"), start=True, stop=True), reads=[cst_s, src_t], writes=[out_ps])

    for d in range(2):
        for hz in Hz:
            kb.op("dve", lambda e: e.memset(hz[:], 0.0), writes=[hz])
        order = list(range(NCHK)) if d == 0 else [1, 0] + list(range(NCHK - 1, 1, -1))
        last = 127 if d == 0 else 0
        for c in order:
            t0 = c * 128
            sc0 = col(c)
            kb.dma("sp", R[:], rkv[:, hq * 4:hq * 4 + 4, sc0:sc0 + 128], reads=[rkv], writes=[R])
            kb.dma("sp", K[:], rkv[:, 16 + hq * 4:16 + hq * 4 + 4, sc0:sc0 + 128], reads=[rkv], writes=[K])
            kb.dma("sp", V[:], rkv[:, 32 + hq * 4:32 + hq * 4 + 4, sc0:sc0 + 128], reads=[rkv], writes=[V])
            kb.dma("sp", LO[:], lo[:, :, sc0:sc0 + 128], reads=[lo], writes=[LO])
            for hp in range(4):
                kb.op("pe", lambda e: e.matmul(ps_z[:, hp * 128:(hp + 1) * 128], w2_s[:, d, hp * 128:(hp + 1) * 128], LO[:, d, :], start=True, stop=True), reads=[w2_s, LO], writes=[ps_z])
            kb.op("dve", lambda e: e.tensor_tensor(out=t1[:], in0=ps_z[:, 0:512].rearrange("p (a t) -> p a t", a=4), in1=pb(P_W0 + d), op=ALU.add), reads=[ps_z, prm_s], writes=[t1])
            kb.op("act", lambda e: e.activation(out=t1[:], in_=t1[:], func=AF.Sigmoid), reads=[t1], writes=[t1])
            kb.op("dve", lambda e: e.tensor_scalar(out=LW[:], in0=t1[:], scalar1=-0.6065306597126334, scalar2=None, op0=ALU.mult), reads=[t1], writes=[LW])
            for hp in range(4):
                kb.op("pe", lambda e: e.matmul(ps_z[:, hp * 128:(hp + 1) * 128], a2_s[:, d, hp * 128:(hp + 1) * 128], LO[:, 2 + d, :], start=True, stop=True), reads=[a2_s, LO], writes=[ps_z])
            kb.op("dve", lambda e: e.tensor_tensor(out=AS[:], in0=ps_z[:, 0:512].rearrange("p (a t) -> p a t", a=4), in1=pb(P_A0 + d), op=ALU.add), reads=[ps_z, prm_s], writes=[AS])
            kb.op("act", lambda e: e.activation(out=AS[:], in_=AS[:], func=AF.Sigmoid), reads=[AS], writes=[AS])
            kb.op("dve", lambda e: e.tensor_tensor(out=KK[:], in0=K[:], in1=pb(P_KK), op=ALU.mult), reads=[K, prm_s], writes=[KK])
            kb.op("dve", lambda e: e.tensor_tensor(out=t2[:], in0=KK[:], in1=KK[:], op=ALU.mult), reads=[KK], writes=[t2])
            bmm(ps_z, t2, t2)
            kb.op("act", lambda e: e.activation(out=t2[:], in_=ps_z[:, 0:512].rearrange("p (a t) -> p a t", a=4), func=AF.Sqrt), reads=[ps_z], writes=[t2])
            kb.op("dve", lambda e: e.tensor_scalar(out=t2[:], in0=t2[:], scalar1=1e-12, scalar2=None, op0=ALU.max), reads=[t2], writes=[t2])
            kb.op("dve", lambda e: e.reciprocal(out=t2[:], in_=t2[:]), reads=[t2], writes=[t2])
            kb.op("dve", lambda e: e.tensor_tensor(out=KK[:], in0=KK[:], in1=t2[:], op=ALU.mult), reads=[KK, t2], writes=[KK])
            kb.op("dve", lambda e: e.scalar_tensor_tensor(out=t2[:], in0=AS[:], scalar=-1.0, in1=pb(P_KA), op0=ALU.add, op1=ALU.mult), reads=[AS, prm_s], writes=[t2])
            kb.op("dve", lambda e: e.scalar_tensor_tensor(out=KD[:], in0=t2[:], scalar=1.0, in1=K[:], op0=ALU.add, op1=ALU.mult), reads=[t2, K], writes=[KD])
            kb.op("dve", lambda e: e.tensor_tensor(out=B_[:], in0=KK[:], in1=AS[:], op=ALU.mult), reads=[KK, AS], writes=[B_])
            lwf = LW[:].rearrange("p a t -> p (a t)"); cumf = CUM[:].rearrange("p a t -> p (a t)")
            if d == 0:
                kb.op("dve", lambda e: e.tensor_tensor_scan(out=cumf, data0=rmk_s[:], data1=lwf, initial=0.0, op0=ALU.mult, op1=ALU.add), reads=[rmk_s, LW], writes=[CUM])
            else:
                kb.op("dve", lambda e: e.tensor_tensor_scan(out=cumf[:, ::-1], data0=rmk_s[:], data1=lwf[:, ::-1], initial=0.0, op0=ALU.mult, op1=ALU.add), reads=[rmk_s, LW], writes=[CUM])
            kb.op("act", lambda e: e.activation(out=G[:], in_=CUM[:], func=AF.Exp), reads=[CUM], writes=[G])
            kb.op("act", lambda e: e.activation(out=GI[:], in_=CUM[:], func=AF.Exp, scale=-1.0), reads=[CUM], writes=[GI])
            kb.op("dve", lambda e: e.tensor_tensor(out=t2[:], in0=CUM[:], in1=LW[:], op=ALU.subtract), reads=[CUM, LW], writes=[t2])
            kb.op("act", lambda e: e.activation(out=GP[:], in_=t2[:], func=AF.Exp), reads=[t2], writes=[GP])
            kb.op("dve", lambda e: e.scalar_tensor_tensor(out=AR[:, :, 0, :], in0=KK[:], scalar=-1.0, in1=GP[:], op0=ALU.mult, op1=ALU.mult), reads=[KK, GP], writes=[AR])
            kb.op("dve", lambda e: e.tensor_tensor(out=AR[:, :, 1, :], in0=R[:], in1=G[:], op=ALU.mult), reads=[R, G], writes=[AR])
            kb.op("dve", lambda e: e.tensor_tensor(out=BK[:, :, 0, :], in0=B_[:], in1=GI[:], op=ALU.mult), reads=[B_, GI], writes=[BK])
            kb.op("dve", lambda e: e.tensor_tensor(out=BK[:, :, 1, :], in0=KD[:], in1=GI[:], op=ALU.mult), reads=[KD, GI], writes=[BK])
            kb.op("dve", lambda e: e.tensor_tensor(out=t2[:], in0=R[:], in1=KD[:], op=ALU.mult), reads=[R, KD], writes=[t2])
            kb.op("dve", lambda e: e.tensor_tensor(out=t2[:], in0=t2[:], in1=pb(P_RK), op=ALU.mult), reads=[t2, prm_s], writes=[t2])
            bmm(ps_z, t2, t2)
            kb.op("dve", lambda e: e.tensor_tensor(out=BV[:], in0=ps_z[:, 0:512].rearrange("p (a t) -> p a t", a=4), in1=V[:], op=ALU.mult), reads=[ps_z, V], writes=[BV])
            for hp in range(4 if stage >= 1 else 0):
                hz = Hz[hp]
                kb.op("pe", lambda e: e.matmul(ps_tr[:, 0:128], BK[:, hp, 0, :], ident, start=True, stop=True), reads=[BK, cst_s], writes=[ps_tr])
                kb.op("pe", lambda e: e.matmul(ps_tr[:, 128:256], BK[:, hp, 1, :], ident, start=True, stop=True), reads=[BK, cst_s], writes=[ps_tr])
                kb.op("pe", lambda e: e.matmul(ps_tr[:, 256:384], V[:, hp, :], ident, start=True, stop=True), reads=[V, cst_s], writes=[ps_tr])
                for h2 in range(2):
                    cs = slice(64 * h2, 64 * h2 + 64)
                    kb.op("act", lambda e: e.activation(out=BZ[h2][:, cs], in_=ps_tr[:, 64 * h2:64 * h2 + 64], func=AF.Identity), reads=[ps_tr], writes=[BZ[h2]])
                    kb.op("dve", lambda e: e.tensor_copy(out=KZ[h2][:, cs], in_=ps_tr[:, 128 + 64 * h2:128 + 64 * h2 + 64]), reads=[ps_tr], writes=[KZ[h2]])
                    kb.op("act", lambda e: e.activation(out=VZ[h2][:, cs], in_=ps_tr[:, 256 + 64 * h2:256 + 64 * h2 + 64], func=AF.Identity), reads=[ps_tr], writes=[VZ[h2]])
                if stage < 2:
                    continue
                xx = XX[0]
                for h2 in range(2):
                    P = slice(64 * h2, 64 * h2 + 64)
                    arf = AR[P, hp, :, :].rearrange("p a t -> p (a t)")
                    kb.op("pe", lambda e: e.matmul(ps_sc[:, 0:256], BK[P, hp, 0, :], arf, start=True, stop=True), reads=[BK, AR], writes=[ps_sc])
                    kb.op("pe", lambda e: e.matmul(ps_sc[:, 256:512], BK[P, hp, 1, :], arf, start=True, stop=True), reads=[BK, AR], writes=[ps_sc])
                    kb.op("pe", lambda e: e.matmul(ps_t[:, 256:384], AR[P, hp, 0, :], BK[P, hp, 0, :], start=True, stop=True), reads=[BK, AR], writes=[ps_t])
                    kb.op("dve", lambda e: e.tensor_tensor(out=SC[h2][:], in0=ps_sc[:, 0:512], in1=msk_s[:, d, 0:512], op=ALU.mult), reads=[ps_sc, msk_s], writes=[SC[h2]])
                    kb.op("act", lambda e: e.activation(out=xx[:, h2, 0, :], in_=SC[h2][:, 0:128], func=AF.Identity), reads=[SC[h2]], writes=[xx])
                    kb.op("dve", lambda e: e.tensor_tensor(out=xx[:, h2, 1, :], in0=ps_t[:, 256:384], in1=msk_s[:, d, 512:640], op=ALU.mult), reads=[ps_t, msk_s], writes=[xx])
                    kb.op("dve", lambda e: e.tensor_tensor(out=TT[:, h2, :], in0=SC[h2][:, 0:128], in1=ident, op=ALU.add), reads=[SC[h2], cst_s], writes=[TT])
                if stage < 3:
                    continue
                cur = 0
                for lvl in range(6):
                    xa = XX[cur]; xb = XX[1 - cur]
                    for h2 in range(2):
                        kb.op("pe", lambda e: e.matmul(ps_p[:, h2 * 256:h2 * 256 + 128], xa[:, h2, 1, :], xa[:, h2, 0, :], start=True, stop=True), reads=[xa], writes=[ps_p])
                        kb.op("pe", lambda e: e.matmul(ps_p[:, h2 * 256 + 128:h2 * 256 + 256], xa[:, h2, 0, :], xa[:, h2, 1, :], start=True, stop=True), reads=[xa], writes=[ps_p])
                    kb.op("act", lambda e: e.activation(out=xb[:].rearrange("p h a t -> p (h a t)"), in_=ps_p[:, 0:512], func=AF.Identity), reads=[ps_p], writes=[xb])
                    for h2 in range(2):
                        kb.op("pe", lambda e: e.matmul(ps_t[:, h2 * 128:(h2 + 1) * 128], xb[:, h2, 1, :], TT[:, h2, :], start=True, stop=True), reads=[xb, TT], writes=[ps_t])
                    kb.op("dve", lambda e: e.tensor_tensor(out=TT[:].rearrange("p h t -> p (h t)"), in0=TT[:].rearrange("p h t -> p (h t)"), in1=ps_t[:, 0:256], op=ALU.add), reads=[TT, ps_t], writes=[TT])
                    cur = 1 - cur
                if stage < 4:
                    continue
                for h2 in range(2):
                    P = slice(64 * h2, 64 * h2 + 64); cs = P
                    kb.op("pe", lambda e: e.matmul(ps_r[:, h2 * 64:h2 * 64 + 64], AR[P, hp, 0, :], hz[P, cs], start=True, stop=False), reads=[AR, hz], writes=[ps_r])
                    kb.op("pe", lambda e: e.matmul(ps_r[:, h2 * 64:h2 * 64 + 64], SC[h2][:, 256:384], VZ[h2][:, cs], start=False, stop=True), reads=[SC[h2], VZ[h2]], writes=[ps_r])
                    kb.op("act", lambda e: e.activation(out=RHS[h2][:], in_=ps_r[:, h2 * 64:h2 * 64 + 64], func=AF.Identity), reads=[ps_r], writes=[RHS[h2]])
                    kb.op("pe", lambda e: e.matmul(ps_r[:, 128 + h2 * 64:128 + h2 * 64 + 64], TT[:, h2, :], RHS[h2][:], start=True, stop=True), reads=[TT, RHS[h2]], writes=[ps_r])
                    kb.op("dve", lambda e: e.tensor_copy(out=UZ[h2][:, cs], in_=ps_r[:, 128 + h2 * 64:128 + h2 * 64 + 64]), reads=[ps_r], writes=[UZ[h2]])
                if stage < 5:
                    continue
                for h2 in range(2):
                    P = slice(64 * h2, 64 * h2 + 64)
                    kb.op("pe", lambda e: e.matmul(ps_y[:, 0:128], hz[P, :], AR[P, hp, 1, :], start=(h2 == 0), stop=False), reads=[hz, AR], writes=[ps_y])
                    kb.op("pe", lambda e: e.matmul(ps_y[:, 0:128], UZ[h2][:], SC[h2][:, 128:256], start=False, stop=False), reads=[UZ[h2], SC[h2]], writes=[ps_y])
                    kb.op("pe", lambda e: e.matmul(ps_y[:, 0:128], VZ[h2][:], SC[h2][:, 384:512], start=False, stop=(h2 == 1)), reads=[VZ[h2], SC[h2]], writes=[ps_y])
                kb.op("act", lambda e: e.activation(out=YT[:, hp, :], in_=ps_y[:, 0:128], func=AF.Identity), reads=[ps_y], writes=[YT])
                if stage < 6:
                    continue
                for h2 in range(2):
                    cs = slice(64 * h2, 64 * h2 + 64)
                    kb.op("pe", lambda e: e.matmul(ps_h[:, 0:64], BZ[h2][:], UZ[h2][:, cs], start=(h2 == 0), stop=False), reads=[BZ[h2], UZ[h2]], writes=[ps_h])
                    kb.op("pe", lambda e: e.matmul(ps_h[:, 0:64], KZ[h2][:], VZ[h2][:, cs], start=False, stop=(h2 == 1)), reads=[KZ[h2], VZ[h2]], writes=[ps_h])
                for h2 in range(2):
                    P = slice(64 * h2, 64 * h2 + 64)
                    kb.op("dve", lambda e: e.tensor_tensor(out=hz[P, P], in0=hz[P, P], in1=ps_h[P, 0:64], op=ALU.add), reads=[hz, ps_h], writes=[hz])
                    kb.op("dve", lambda e: e.tensor_scalar(out=hz[P, P], in0=hz[P, P], scalar1=G[P, hp, last:last + 1], scalar2=None, op0=ALU.mult), reads=[hz, G], writes=[hz])
            if d == 0:
                kb.dma("sp", y0[:, :, t0:t0 + 128], YT[:], reads=[YT], writes=[y0])
                kb.dma("sp", bv0[:, :, t0:t0 + 128], BV[:], reads=[BV], writes=[bv0])
            else:
                kb.dma("sp", Y0[:], y0[:, :, t0:t0 + 128], reads=[y0], writes=[Y0])
                kb.dma("sp", BV0[:], bv0[:, :, t0:t0 + 128], reads=[bv0], writes=[BV0])
                kb.op("dve", lambda e: e.tensor_tensor(out=YT[:], in0=YT[:], in1=Y0[:], op=ALU.add), reads=[YT, Y0], writes=[YT])
                bmm(ps_z, YT, YT)
                kb.op("dve", lambda e: e.scalar_tensor_tensor(out=YT[:], in0=ps_z[:, 0:512].rearrange("p (a t) -> p a t", a=4), scalar=-1.0 / 64.0, in1=YT[:], op0=ALU.mult, op1=ALU.add), reads=[ps_z, YT], writes=[YT])
                kb.op("dve", lambda e: e.tensor_tensor(out=t2[:], in0=YT[:], in1=YT[:], op=ALU.mult), reads=[YT], writes=[t2])
                bmm(ps_z, t2, t2)
                kb.op("act", lambda e: e.activation(out=t2[:], in_=ps_z[:, 0:512].rearrange("p (a t) -> p a t", a=4), func=AF.Sqrt, scale=1.0 / 64.0, bias=eps_gn), reads=[ps_z], writes=[t2])
                kb.op("dve", lambda e: e.reciprocal(out=t2[:], in_=t2[:]), reads=[t2], writes=[t2])
                kb.op("dve", lambda e: e.tensor_tensor(out=YT[:], in0=YT[:], in1=t2[:], op=ALU.mult), reads=[YT, t2], writes=[YT])
                kb.op("dve", lambda e: e.tensor_tensor(out=YT[:], in0=YT[:], in1=pb(P_LNW), op=ALU.mult), reads=[YT, prm_s], writes=[YT])
                kb.op("dve", lambda e: e.tensor_tensor(out=YT[:], in0=YT[:], in1=pb(P_LNB), op=ALU.add), reads=[YT, prm_s], writes=[YT])
                kb.op("dve", lambda e: e.tensor_tensor(out=BV[:], in0=BV[:], in1=BV0[:], op=ALU.add), reads=[BV, BV0], writes=[BV])
                kb.op("dve", lambda e: e.tensor_tensor(out=YT[:], in0=YT[:], in1=BV[:], op=ALU.add), reads=[YT, BV], writes=[YT])
                kb.dma("sp", p1[:, hq * 4:hq * 4 + 4, sc0:sc0 + 128], YT[:], reads=[YT], writes=[p1])


def rwkv_b_consts():
    p = np.arange(128)[:, None]; q = np.arange(128)[None, :]
    cst = np.zeros((128, 2, 128), np.float32)
    cst[:, 0, :] = (p // 64 == q // 64)
    cst[:, 1, :] = (p == q)
    msk = np.zeros((128, 2, 640), np.float32)
    for d, (strict, incl) in enumerate((((p < q), (p <= q)), ((p > q), (p >= q)))):
        msk[:, d, 0:128] = strict; msk[:, d, 128:256] = incl; msk[:, d, 256:384] = strict; msk[:, d, 384:512] = incl
        msk[:, d, 512:640] = strict.T
    rmk = np.ones((128, 512), np.float32); rmk[:, ::128] = 0.0
    return cst, msk, rmk


def emit_ada(kb, cT, w, bia, mods_all):
    NCHG = 96
    c32 = kb.sb([128, 16, 2]); cb = kb.sb([128, 16, 2], BF16)
    bs = kb.sb([128, 384]); osb = kb.sb([128, 384, 2])
    wts = [kb.sb([128, 4, 16, 128], BF16) for _ in range(3)]
    pss = [kb.ps([128, 512]) for _ in range(2)]
    kb.dma("sp", c32[:], cT[:], reads=[cT], writes=[c32])
    kb.dma("sp", bs[:], bia[:], reads=[bia], writes=[bs])
    kb.op("act", lambda e: e.activation(out=cb[:], in_=c32[:], func=AF.Silu), reads=[c32], writes=[cb])
    for g in range(NCHG):
        wt = wts[g % 3]
        kb.dma("pool", wt[:].rearrange("p a k n -> p (a k n)"), w[g], reads=[w], writes=[wt])
        for a in range(4):
            j = g * 4 + a
            ps = pss[j % 2]
            for kc in range(16):
                kb.op("pe", lambda e: e.matmul(ps[:, 0:2], wt[:, a, kc, :], cb[:, kc, :], start=(kc == 0), stop=(kc == 15)), reads=[wt, cb], writes=[ps])
            kb.op("act", lambda e: e.activation(out=osb[:, j, :], in_=ps[:, 0:2], func=AF.Identity, bias=bs[:, j:j + 1]), reads=[ps, bs], writes=[osb])
    kb.dma("sp", mods_all[:], osb[:], reads=[osb], writes=[mods_all])


def perm_fwd(kb, eng, dst_t, dst_ap, src_t, src_ap):
    kb.op(eng, lambda e: e.tensor_copy(out=dst_ap.rearrange("p (c r) -> p c r", c=64, r=128), in_=src_ap.rearrange("p (r c) -> p c r", r=128, c=64)), reads=[src_t], writes=[dst_t])


def perm_inv(kb, eng, dst_t, dst_ap, src_t, src_ap):
    kb.op(eng, lambda e: e.tensor_copy(out=dst_ap.rearrange("p (r c) -> p c r", r=128, c=64), in_=src_ap.rearrange("p (c r) -> p c r", c=64, r=128)), reads=[src_t], writes=[dst_t])


def emit_gla_prep(kb, qkv_d, gl_d, ident, qk_s, ktok_s, vtok_s, glT_s, col_major):
    R = kb.sb([128, NS]); S = kb.sb([128, NS]); ident_s = kb.sb([128, 128])
    ev = [kb.sb([128, 512]) for _ in range(2)]; pst = [kb.ps() for _ in range(2)]
    kb.dma("sp", ident_s[:], ident[:], reads=[ident], writes=[ident_s])
    n = 0
    for f in range(32):
        kb.dma("sp", R[:], qkv_d[:, f, 0:NS], reads=[qkv_d], writes=[R])
        kb.op("act", lambda e: e.activation(out=S[:, 0:CTXC], in_=R[:, LATC:LATC + CTXC], func=AF.Identity), reads=[R], writes=[S])
        if col_major:
            perm_fwd(kb, "pool", S, S[:, CTXC:NS], R, R[:, 0:LATC])
        else:
            kb.op("pool", lambda e: e.tensor_copy(out=S[:, CTXC:NS], in_=R[:, 0:LATC]), reads=[R], writes=[S])
        if f < 16:
            kb.dma("sp", qk_s[:, f, :], S[:], reads=[S], writes=[qk_s])
        if f >= 8:
            tok, fc = (ktok_s, f - 8) if f < 16 else (vtok_s, f - 16)
            for b0 in range(0, NS // 128, 4):
                nb = min(4, NS // 128 - b0)
                ps = pst[n % 2]; e_ = ev[n % 2]; n += 1
                for j in range(nb):
                    kb.op("pe", lambda e: e.matmul(ps[:, j * 128:(j + 1) * 128], S[:, (b0 + j) * 128:(b0 + j + 1) * 128], ident_s[:], start=True, stop=True), reads=[S, ident_s], writes=[ps])
                kb.op("dve", lambda e: e.tensor_copy(out=e_[:, :nb * 128], in_=ps[:, :nb * 128]), reads=[ps], writes=[e_])
                kb.dma("sp", tok[b0 * 128:(b0 + nb) * 128, fc * 128:(fc + 1) * 128].rearrange("(j p) c -> p j c", p=128), e_[:, :nb * 128].rearrange("p (j c) -> p j c", c=128), reads=[e_], writes=[tok])
    for d in range(2):
        kb.dma("sp", R[0:16, :], gl_d[d * 16:(d + 1) * 16, 0:NS], reads=[gl_d], writes=[R])
        kb.op("act", lambda e: e.activation(out=S[0:16, 0:CTXC], in_=R[0:16, LATC:LATC + CTXC], func=AF.Identity), reads=[R], writes=[S])
        if col_major:
            perm_fwd(kb, "pool", S, S[0:16, CTXC:NS], R, R[0:16, 0:LATC])
        else:
            kb.op("pool", lambda e: e.tensor_copy(out=S[0:16, CTXC:NS], in_=R[0:16, 0:LATC]), reads=[R], writes=[S])
        kb.dma("sp", glT_s[:, d, :], S[0:16, :], reads=[S], writes=[glT_s])


def emit_gla_post(kb, pT_s, p1_d, col_major):
    R = kb.sb([128, NS]); S = kb.sb([128, NS])
    for f in range(16):
        kb.dma("sp", S[:], pT_s[:, f, :], reads=[pT_s], writes=[S])
        kb.op("act", lambda e: e.activation(out=R[:, LATC:LATC + CTXC], in_=S[:, 0:CTXC], func=AF.Identity), reads=[S], writes=[R])
        if col_major:
            perm_inv(kb, "pool", R, R[:, 0:LATC], S, S[:, CTXC:NS])
        else:
            kb.op("pool", lambda e: e.tensor_copy(out=R[:, 0:LATC], in_=S[:, CTXC:NS]), reads=[S], writes=[R])
        kb.dma("sp", p1_d[:, f, 0:NS], R[:], reads=[R], writes=[p1_d])


def build_fused():
    kb = KB()
    I = lambda name, shape: kb.dram(name, shape, kind="ExternalInput")
    N = lambda name, shape: kb.dram(name, shape)
    h = [I("h0", [128, 16, TCP]), N("h1", [128, 16, TCP]), N("h2", [128, 16, TCP]), N("h3", [128, 16, TCP]), kb.dram("hout", [128, 16, TCP], kind="ExternalOutput")]
    cT = I("cT", [128, 16, 2]); ada_w = I("ada_w", [96, 128, 8192]); ada_b = I("ada_b", [128, 384])
    mods_all = N("mods_all", [128, 384, 2])
    ng = [I(f"ng{l}", [128, 4, 16]) for l in range(4)]
    CW = [dict(wout=I(f"wout{l}", [16, 128, 2048]), wg=I(f"wg{l}", [44, 128, 2048]), wu=I(f"wu{l}", [44, 128, 2048]), wo=I(f"wo{l}", [16, 2, 128, 2816]), bout=I(f"bout{l}", [128, 16])) for l in range(4)]
    LR = [dict(w=I(f"lw{j}", [32, 128, 2048]), bin_=I(f"lb{j}", [128, 32]), cw=I(f"lcw{j}", [128, 16, 4]), cb=I(f"lcb{j}", [128, 16]), gw=I(f"lgw{j}", [128, 16, 2, 2, 128]),
               gb=I(f"lgb{j}", [128, 16, 2, 2]), ll=I(f"lll{j}", [128, 16, 2])) for j in range(2)]
    GL = dict(w=I("gw_in", [48, 128, 2048]), w1=I("gw1", [128, 512]), br=I("gbr", [128, 16]), w2=I("gw2", [16, 2, 1024]), gbF=I("ggbF", [128, 2, 8]), gbT=I("ggbT", [128, 2, 1024]),
              ngT=I("gngT", [128, 2048]), masks=I("gmasks", [128, 2, 2, 128]), rmask=I("grmask", [128, 512]))
    ident = I("ident", [128, 128])
    RW = dict(mu=I("rmu", [128, 6, 16]), w=I("rw", [48, 128, 2048]), wl=I("rwl", [128, 6144]), g1=I("rg1", [128, 4096]), g2=I("rg2", [16, 128, 256]), w2=I("rw2", [96, 2, 2048]),
              a2=I("ra2", [96, 2, 2048]), prm=I("rprm", [128, 9, 16]), cst=I("rcst", [128, 2, 128]), msk=I("rmsk", [128, 2, 640]), rmk=I("rrmk", [128, 512]))
    t_p1 = N("t_p1", [128, 16, TCP]); t_p2 = N("t_p2", [128, 16, TCP]); t_rec = N("t_rec", [128, 16, TCP])
    qkv_d = N("qkv_d", [128, 32, TCP]); gl_d = N("gl_d", [32, TCP]); qk_s = N("qk_s", [128, 16, NS]); ktok_s = N("ktok_s", [NS, 1024]); vtok_s = N("vtok_s", [NS, 2048])
    glT_s = N("glT_s", [16, 2, NS]); o0 = N("o0", [NS, 512]); pT_s = N("pT_s", [128, 16, NS])
    rkv_d = N("rkv_d", [128, 48, TCP]); lora_d = N("lora_d", [96, 4, TCP]); us = N("us", [128, 16, US_W]); y0 = N("y0", [128, 4, NS]); bv0 = N("bv0", [128, 4, NS])

    def phase(fn):
        kb.begin_phase(); fn(); kb.end_phase()

    phase(lambda: emit_ada(kb, cT, ada_w, ada_b, mods_all))
    for l in range(4):
        kind, j = l % 3, l // 3
        col_major = l % 2 == 1
        mo = l * 96
        if kind == 0:
            P = LR[j]
            phase(lambda: emit_lru_a(kb, h[l], mods_all, ng[l], P["w"], P["bin_"], t_p2, t_rec, mods_off=mo))
            phase(lambda: emit_lru_b(kb, t_rec, P["cw"], P["cb"], P["gw"], P["gb"], P["ll"], t_p1, col_major))
        elif kind == 1:
            phase(lambda: emit_gla_a(kb, h[l], mods_all, ng[l], GL["w"], GL["w1"], GL["br"], qkv_d, t_p2, gl_d, mods_off=mo))
            phase(lambda: emit_gla_prep(kb, qkv_d, gl_d, ident, qk_s, ktok_s, vtok_s, glT_s, col_major))
            for hd in range(4):
                phase(lambda: emit_gla_b(kb, hd, qk_s, ktok_s, vtok_s, glT_s, GL["w2"], GL["gbF"], GL["gbT"], GL["ngT"], GL["masks"], GL["rmask"], ident, o0, pT_s))
            phase(lambda: emit_gla_post(kb, pT_s, t_p1, col_major))
        else:
            phase(lambda: emit_rwkv_a(kb, h[l], mods_all, ng[l], RW["mu"], RW["w"], RW["wl"], RW["g1"], RW["g2"], rkv_d, lora_d, t_p2, us, mods_off=mo))
            for hq in range(4):
                phase(lambda: emit_rwkv_b(kb, hq, rkv_d, lora_d, RW["w2"], RW["a2"], RW["prm"], RW["cst"], RW["msk"], RW["rmk"], y0, bv0, t_p1))
        C = CW[l]
        phase(lambda: emit_c(kb, h[l], t_p1, t_p2, mods_all, ng[l], C["bout"], C["wout"], C["wg"], C["wu"], C["wo"], h[l + 1], mods_off=mo))
    kb.finish([h[4]])
    kb.close()
    return kb


def fused_inputs(inp):
    f32 = np.float32
    A = lambda x: np.ascontiguousarray(x, dtype=f32)
    aw = inp["ada_w"].reshape(4, 16, 128, 96, 128).transpose(0, 3, 2, 1, 4).reshape(384, 128, 2048)
    ada_w = A(aw.reshape(96, 4, 128, 2048).transpose(0, 2, 1, 3).reshape(96, 128, 8192))
    ada_b = A(inp["ada_b"].reshape(384, 128).T)
    shared = dict(ada_w=ada_w, ada_b=ada_b)
    for l in range(4):
        shared[f"ng{l}"] = A(inp["norm_g"][l].reshape(4, 16, 128).transpose(2, 0, 1))
    def cw(l, w_out, b_out):
        W = prep_c_weights(w_out, inp["ffn_w_in"][l], inp["ffn_w_out"][l])
        shared[f"wout{l}"] = W["wout"]; shared[f"wg{l}"] = W["wg"]; shared[f"wu{l}"] = W["wu"]; shared[f"wo{l}"] = W["wo"]
        shared[f"bout{l}"] = A(b_out.reshape(16, 128).T)
    z = np.zeros(2048, f32)
    cw(0, inp["lru_w_out"][0], inp["lru_b_out"][0]); cw(1, inp["gla_w_out"][0], z); cw(2, inp["rwkv_w_out"][0], z); cw(3, inp["lru_w_out"][1], inp["lru_b_out"][1])
    for j in range(2):
        shared[f"lw{j}"] = A(inp["lru_w_in"][j].reshape(16, 128, 32, 128).transpose(2, 1, 0, 3).reshape(32, 128, 2048))
        shared[f"lb{j}"] = A(inp["lru_b_in"][j].reshape(32, 128).T)
        shared[f"lcw{j}"] = A(inp["lru_conv_w"][j].reshape(4, 16, 128).transpose(2, 1, 0))
        shared[f"lcb{j}"] = A(inp["lru_conv_b"][j].reshape(16, 128).T)
        shared[f"lgw{j}"] = A(inp["lru_gate_w"][j].transpose(3, 2, 0, 1, 4))
        shared[f"lgb{j}"] = A(inp["lru_gate_b"][j].reshape(2, 2, 16, 128).transpose(3, 2, 0, 1))
        shared[f"lll{j}"] = A(inp["lru_log_lambda"][j].reshape(2, 16, 128).transpose(2, 1, 0))
    shared["gw_in"] = A(inp["gla_w_in"][0].reshape(16, 128, 48, 128).transpose(2, 1, 0, 3).reshape(48, 128, 2048))
    shared["gw1"] = A(inp["gla_gate_w1"][0].transpose(1, 0, 2).reshape(16, 128, 32).transpose(1, 0, 2).reshape(128, 512))
    shared["gbr"] = A(inp["gla_b_r"][0].reshape(16, 128).T)
    shared["gw2"] = A(inp["gla_gate_w2"][0].transpose(1, 0, 2))
    gb = inp["gla_gate_b"][0]
    shared["ggbF"] = A(gb.reshape(2, 8, 128).transpose(2, 0, 1))
    shared["ggbT"] = A(np.broadcast_to(gb[None], (128, 2, 1024)))
    shared["gngT"] = A(np.broadcast_to(inp["gla_norm_g"][0][None], (128, 2048)))
    masks, rmask = gla_b_consts()
    shared["gmasks"] = masks; shared["grmask"] = rmask
    shared["ident"] = np.eye(128, dtype=f32)
    shared["rmu"] = A(inp["rwkv_mu"][0].reshape(6, 16, 128).transpose(2, 0, 1))
    shared["rw"] = A(inp["rwkv_w_rkv"][0].reshape(3, 16, 128, 16, 128).transpose(0, 3, 2, 1, 4).reshape(48, 128, 2048))
    wl = np.concatenate([inp["rwkv_w1"][0][0], inp["rwkv_w1"][0][1], inp["rwkv_a1"][0][0], inp["rwkv_a1"][0][1]], axis=1)
    shared["rwl"] = A(wl.reshape(16, 128, 384).transpose(1, 0, 2).reshape(128, 6144))
    shared["rg1"] = A(inp["rwkv_g1"][0].reshape(16, 128, 256).transpose(1, 0, 2).reshape(128, 4096))
    shared["rg2"] = A(inp["rwkv_g2"][0].reshape(2, 128, 16, 128).transpose(2, 1, 0, 3).reshape(16, 128, 256))
    shared["rw2"] = A(inp["rwkv_w2"][0].transpose(1, 0, 2)); shared["ra2"] = A(inp["rwkv_a2"][0].transpose(1, 0, 2))
    def pp(v):
        return v.reshape(16, 128).T
    shared["rprm"] = A(np.stack([pp(inp["rwkv_k_k"][0]), pp(inp["rwkv_k_a"][0]), pp(inp["rwkv_r_k"][0].reshape(-1)), pp(inp["rwkv_ln_w"][0]), pp(inp["rwkv_ln_b"][0]),
                                 pp(inp["rwkv_w0"][0][0]), pp(inp["rwkv_w0"][0][1]), pp(inp["rwkv_a0"][0][0]), pp(inp["rwkv_a0"][0][1])], axis=1))
    cst, msk, rmk = rwkv_b_consts()
    shared["rcst"] = cst; shared["rmsk"] = msk; shared["rrmk"] = rmk
    maps = []
    for b in range(2):
        a = np.zeros((2048, TCP), f32)
        a[:, 0:LATC] = inp["x"][b].T; a[:, LATC:LATC + CTXC] = inp["ctx"][b].T
        cT = np.stack([inp["c_ctx"], inp["c"][b]], axis=1)
        maps.append(dict(h0=fm(a), cT=A(cT.reshape(16, 128, 2).transpose(1, 0, 2)), **shared))
    return maps

def kernel(**inp):
    inp = {k: np.asarray(v) for k, v in inp.items()}
    kb = build_fused()
    maps = fused_inputs(inp)
    res = run(kb, maps)
    out = np.stack([res[b]["hout"].transpose(1, 0, 2).reshape(2048, TCP)[:, 0:LATC].T for b in range(2)])
    return np.ascontiguousarray(out.astype(np.float32))
```

```python
import numpy as np
from contextlib import ExitStack
import concourse.bass as bass
import concourse.mybir as mybir
from concourse.bass_utils import run_bass_kernel_spmd

F32 = mybir.dt.float32
BF16 = mybir.dt.bfloat16
AF = mybir.ActivationFunctionType
ALU = mybir.AluOpType
AX = mybir.AxisListType

NCORES = 8
NTL = 2
LATC = 8192
CTXC = 256
TCP = 8704
TILES_A = [(512 * i, 512, 1) for i in range(16)] + [(8192, 512, 0)]


class T:
    def __init__(self, t, name=""):
        self.t = t
        self.name = name
        self.w = None
        self.r = {}
        self.psum = False

    def __getitem__(self, idx):
        return self.t[idx]


class KB:
    NSP = 14
    NPOOLQ = 8

    def __init__(self, same_engine_sync=True):
        self.nc = bass.Bass("TRN2", target_bir_lowering=False)
        self.es = ExitStack()
        nc = self.nc
        self.eng = dict(pe=nc.tensor, act=nc.scalar, dve=nc.vector, pool=nc.gpsimd, sp=nc.sync)
        self.sem = {}
        self.cnt = {}
        for e in self.eng:
            self.sem[e] = self.es.enter_context(nc.semaphore("s_" + e))
            self.cnt[e] = 0
        self.dq = {}
        for q, n in (("sp", self.NSP), ("pool", self.NPOOLQ), ("act", 4)):
            lst = []
            for i in range(n):
                k = f"d_{q}{i}"
                self.sem[k] = self.es.enter_context(nc.semaphore(k))
                self.cnt[k] = 0
                lst.append(k)
            self.dq[q] = [lst, 0]
        self.seen = {e: {} for e in self.eng}
        self.cur = {e: e for e in self.eng}
        self.nep = {e: 0 for e in self.eng}
        self.own = {e: e for e in self.eng}
        self.same_engine_sync = same_engine_sync
        self.n_inst = 0
        self.uid = 0
        self.alloc_es = self.es
        self.phase_id = 0
        self.consts = {}
        self.defer = None

    def sb(self, shape, dt=F32, name=None):
        self.uid += 1
        name = name or f"sb{self.uid}"
        t = self.alloc_es.enter_context(self.nc.sbuf_tensor(name, list(shape), dt))
        return T(t, name)

    def ps(self, shape=(128, 512), dt=F32, name=None):
        self.uid += 1
        name = name or f"ps{self.uid}"
        t = self.alloc_es.enter_context(self.nc.psum_tensor(name, list(shape), dt))
        tt = T(t, name)
        tt.psum = True
        return tt

    def barrier(self):
        keys = [self.cur[e] for e in ("pe", "act", "dve", "pool")]
        for q in self.dq:
            keys += self.dq[q][0]
        for e in ("pe", "act", "dve", "pool", "sp"):
            for k in keys:
                if self.cnt[k] and self.own.get(k) != e:
                    self._wait(e, (k, self.cnt[k]))

    def begin_phase(self):
        self.alloc_es = ExitStack()
        self.phase_id += 1
        self.consts = {}

    def end_phase(self):
        self.barrier()
        self.alloc_es.close()
        self.alloc_es = self.es
        self.consts = {}

    def dram(self, name, shape, dt=F32, kind="Internal"):
        t = self.nc.dram_tensor(name, list(shape), dt, kind=kind)
        return T(t.ap(), name)

    def _wait(self, e, dep):
        k, v = dep
        if self.own.get(k) == e and (e == "pe" or not self.same_engine_sync):
            return
        if self.seen[e].get(k, 0) >= v:
            return
        self.seen[e][k] = v
        if self.defer is not None and self.defer[0] == e:
            self.defer[1].append((k, v))
        else:
            self.eng[e].wait_ge(self.sem[k], v)

    def _deps(self, e, reads, writes):
        for t in reads:
            if t.w is not None:
                self._wait(e, t.w)
            if t.psum:
                for k, v in list(t.r.items()):
                    if self.own.get(k) != e:
                        self._wait(e, (k, v))
        for t in writes:
            if t.w is not None:
                self._wait(e, t.w)
            for k, v in t.r.items():
                self._wait(e, (k, v))

    def _mark(self, mark, reads, writes):
        k, v = mark
        for t in reads:
            if t.r.get(k, 0) < v:
                t.r[k] = v
        for t in writes:
            t.w = mark
            t.r = {}

    EPOCH = 30000

    def op(self, e, fn, reads=(), writes=()):
        reads = list(reads) + [t for t in self.consts.values() if t not in writes]
        saved = self.defer
        self.defer = (e, []) if e in ("act", "dve", "pool") else None
        self._deps(e, reads, writes)
        pend = self.defer[1] if self.defer is not None else []
        self.defer = None
        best = {}
        for k_, v_ in pend:
            best[k_] = max(best.get(k_, 0), v_)
        items = list(best.items())
        for k_, v_ in items[:-1]:
            self.eng[e].wait_ge(self.sem[k_], v_)
        inst = fn(self.eng[e])
        if items:
            inst._wait_ge(self.sem[items[-1][0]], items[-1][1])
        self.defer = saved
        k = self.cur[e]
        if self.cnt[k] >= self.EPOCH:
            self.nep[e] += 1
            k = f"{e}#{self.nep[e]}"
            self.sem[k] = self.es.enter_context(self.nc.semaphore("s_" + k.replace("#", "_")))
            self.cnt[k] = 0
            self.own[k] = e
            self.cur[e] = k
        self.cnt[k] += 1
        inst.then_inc(self.sem[k], 1)
        self._mark((k, self.cnt[k]), reads, writes)
        self.n_inst += 1
        return inst

    def dma(self, q, out, in_, reads=(), writes=(), **kw):
        lst, i = self.dq[q]
        k = lst[i % len(lst)]
        self.dq[q][1] = i + 1
        if self.cnt[k] > 0:
            self._wait(q, (k, self.cnt[k]))
        self._deps(q, reads, writes)
        inst = self.eng[q].dma_start(out=out, in_=in_, **kw)
        self.cnt[k] += 16
        inst.then_inc(self.sem[k], 16)
        self._mark((k, self.cnt[k]), reads, writes)
        self.n_inst += 1
        return inst

    def finish(self, outs):
        for t in outs:
            if t.w is not None:
                self._wait("sp", t.w)
        for e in ("pe", "act", "dve", "pool"):
            k = self.cur[e]
            if self.cnt[k]:
                self._wait("sp", (k, self.cnt[k]))
        for q in self.dq:
            for k in self.dq[q][0]:
                if self.cnt[k]:
                    self._wait("sp", (k, self.cnt[k]))

    def close(self):
        self.es.close()


def run(kb, in_maps):
    res = run_bass_kernel_spmd(kb.nc, in_maps, core_ids=list(range(len(in_maps))))
    return res.results


D = 2048
TC = TCP
TILES = TILES_A
EPS = 1e-6


def norm_rstd(kb, ps_ss, TT, rs_t, tmp_t, eps=EPS, n=D):
    kb.op("act", lambda e: e.activation(out=tmp_t[:, :TT], in_=ps_ss[:, :TT], func=AF.Sqrt, scale=1.0 / n, bias=eps_ap(kb, eps)),
          reads=[ps_ss], writes=[tmp_t])
    kb.op("dve", lambda e: e.reciprocal(out=rs_t[:, :TT], in_=tmp_t[:, :TT]), reads=[tmp_t], writes=[rs_t])


def eps_ap(kb, val):
    if val not in kb.consts:
        t = kb.sb([128, 1], F32)
        kb.consts[val] = t
        kb.op("pool", lambda e: e.memset(t[:], val), writes=[t])
    return kb.consts[val][:, 0:1]


def emit_c(kb, h, p1, p2, mods, ng, bout, wout, wg, wu, wo, h2, mods_off=0):
    has_p2 = p2 is not None
    eps_ap(kb, EPS)
    mods_s = kb.sb([128, 96, 2]); ng_s = kb.sb([128, 4, 16]); bout_s = kb.sb([128, 16])
    CG1 = kb.sb([128, 2, 16]); SC2 = kb.sb([128, 2, 16]); SH2 = kb.sb([128, 2, 16]); CG3 = kb.sb([128, 2, 16])
    ones = kb.sb([128, 128], BF16)
    kb.op("pool", lambda e: e.memset(ones[:], 1.0), writes=[ones])
    epsT = eps_ap(kb, EPS)
    kb.dma("sp", mods_s[:], mods[:, mods_off:mods_off + 96, :], reads=[mods], writes=[mods_s])
    kb.dma("sp", ng_s[:], ng[:], reads=[ng], writes=[ng_s])
    kb.dma("sp", bout_s[:], bout[:], reads=[bout], writes=[bout_s])
    for ms in range(2):
        def m(i):
            return mods_s[:, i * 16:(i + 1) * 16, ms]
        kb.op("dve", lambda e: e.tensor_tensor(out=CG1[:, ms, :], in0=m(2), in1=ng_s[:, 1, :], op=ALU.mult), reads=[mods_s, ng_s], writes=[CG1])
        kb.op("dve", lambda e: e.scalar_tensor_tensor(out=SC2[:, ms, :], in0=m(4), scalar=1.0, in1=ng_s[:, 2, :], op0=ALU.add, op1=ALU.mult), reads=[mods_s, ng_s], writes=[SC2])
        kb.op("dve", lambda e: e.tensor_copy(out=SH2[:, ms, :], in_=m(3)), reads=[mods_s], writes=[SH2])
        kb.op("dve", lambda e: e.tensor_tensor(out=CG3[:, ms, :], in0=m(5), in1=ng_s[:, 3, :], op=ALU.mult), reads=[mods_s, ng_s], writes=[CG3])

    hs = kb.sb([128, 16, 512]); y = kb.sb([128, 16, 512]); pb = kb.sb([128, 16, 512], BF16)
    hid = kb.sb([128, 22, 512], BF16)
    sa = [kb.sb([128, 512]) for _ in range(2)]; sb_ = [kb.sb([128, 512]) for _ in range(2)]
    sq = [kb.sb([128, 512], BF16) for _ in range(2)]
    tmp = [kb.sb([128, 512]) for _ in range(2)]
    rs = kb.sb([128, 512]); rtmp = kb.sb([128, 512])
    wt = [kb.sb([128, 16, 128], BF16) for _ in range(2)]
    wgt = [kb.sb([128, 16, 128], BF16) for _ in range(2)]; wut = [kb.sb([128, 16, 128], BF16) for _ in range(2)]
    wot = [kb.sb([128, 22, 128], BF16) for _ in range(2)]
    ps_mm = [kb.ps() for _ in range(2)]; ps_g = [kb.ps() for _ in range(2)]; ps_u = [kb.ps() for _ in range(2)]
    ps_ss = kb.ps()
    ctr = dict(sq=0, tmp=0, mm=0, w=0, gu=0, wo=0, io=0)

    def sumsq_step(src_ap, src_t, TT, first, last):
        s = sq[ctr["sq"] % 2]; ctr["sq"] += 1
        kb.op("act", lambda e: e.activation(out=s[:, :TT], in_=src_ap, func=AF.Square), reads=[src_t], writes=[s])
        kb.op("pe", lambda e: e.matmul(ps_ss[:, :TT], ones[:], s[:, :TT], start=first, stop=last), reads=[ones, s], writes=[ps_ss])

    for (t0, TT, ms) in TILES:
        kb.dma("sp", hs[:, :, :TT], h[:, :, t0:t0 + TT], reads=[h], writes=[hs])
        for kc in range(16):
            a = sa[ctr["io"] % 2]; b = sb_[ctr["io"] % 2]; ctr["io"] += 1
            kb.dma("sp", a[:, :TT], p1[:, kc, t0:t0 + TT], reads=[p1], writes=[a])
            if has_p2:
                kb.dma("sp", b[:, :TT], p2[:, kc, t0:t0 + TT], reads=[p2], writes=[b])
                kb.op("dve", lambda e: e.tensor_tensor(out=pb[:, kc, :TT], in0=a[:, :TT], in1=b[:, :TT], op=ALU.mult), reads=[a, b], writes=[pb])
            else:
                kb.op("dve", lambda e: e.tensor_copy(out=pb[:, kc, :TT], in_=a[:, :TT]), reads=[a], writes=[pb])
        for n in range(16):
            w_ = wt[ctr["w"] % 2]; ctr["w"] += 1
            kb.dma("pool", w_[:].rearrange("p k n -> p (k n)"), wout[n], reads=[wout], writes=[w_])
            ps = ps_mm[ctr["mm"] % 2]; ctr["mm"] += 1
            for kc in range(16):
                kb.op("pe", lambda e: e.matmul(ps[:, :TT], w_[:, kc, :], pb[:, kc, :TT], start=(kc == 0), stop=(kc == 15)), reads=[w_, pb], writes=[ps])
            kb.op("act", lambda e: e.activation(out=y[:, n, :TT], in_=ps[:, :TT], func=AF.Identity, bias=bout_s[:, n:n + 1]), reads=[ps, bout_s], writes=[y])
            sumsq_step(y[:, n, :TT], y, TT, n == 0, n == 15)
        norm_rstd(kb, ps_ss, TT, rs, rtmp)
        for kc in range(16):
            t_ = tmp[ctr["tmp"] % 2]; ctr["tmp"] += 1
            kb.op("dve", lambda e: e.tensor_tensor(out=t_[:, :TT], in0=y[:, kc, :TT], in1=rs[:, :TT], op=ALU.mult), reads=[y, rs], writes=[t_])
            kb.op("dve", lambda e: e.scalar_tensor_tensor(out=hs[:, kc, :TT], in0=t_[:, :TT], scalar=CG1[:, ms, kc:kc + 1], in1=hs[:, kc, :TT], op0=ALU.mult, op1=ALU.add),
                  reads=[t_, CG1, hs], writes=[hs])
            sumsq_step(hs[:, kc, :TT], hs, TT, kc == 0, kc == 15)
        norm_rstd(kb, ps_ss, TT, rs, rtmp)
        for kc in range(16):
            t_ = tmp[ctr["tmp"] % 2]; ctr["tmp"] += 1
            kb.op("dve", lambda e: e.tensor_tensor(out=t_[:, :TT], in0=hs[:, kc, :TT], in1=rs[:, :TT], op=ALU.mult), reads=[hs, rs], writes=[t_])
            kb.op("act", lambda e: e.activation(out=pb[:, kc, :TT], in_=t_[:, :TT], func=AF.Identity, scale=SC2[:, ms, kc:kc + 1], bias=SH2[:, ms, kc:kc + 1]),
                  reads=[t_, SC2, SH2], writes=[pb])
        for half in range(2):
            for jj in range(22):
                j = half * 22 + jj
                g_ = wgt[ctr["gu"] % 2]; u_ = wut[ctr["gu"] % 2]
                pg = ps_g[ctr["gu"] % 2]; pu = ps_u[ctr["gu"] % 2]; ctr["gu"] += 1
                kb.dma("pool", g_[:].rearrange("p k n -> p (k n)"), wg[j], reads=[wg], writes=[g_])
                kb.dma("pool", u_[:].rearrange("p k n -> p (k n)"), wu[j], reads=[wu], writes=[u_])
                for kc in range(16):
                    kb.op("pe", lambda e: e.matmul(pg[:, :TT], g_[:, kc, :], pb[:, kc, :TT], start=(kc == 0), stop=(kc == 15)), reads=[g_, pb], writes=[pg])
                for kc in range(16):
                    kb.op("pe", lambda e: e.matmul(pu[:, :TT], u_[:, kc, :], pb[:, kc, :TT], start=(kc == 0), stop=(kc == 15)), reads=[u_, pb], writes=[pu])
                t_ = tmp[ctr["tmp"] % 2]; ctr["tmp"] += 1
                kb.op("act", lambda e: e.activation(out=t_[:, :TT], in_=pg[:, :TT], func=AF.Silu), reads=[pg], writes=[t_])
                kb.op("dve", lambda e: e.tensor_tensor(out=hid[:, jj, :TT], in0=t_[:, :TT], in1=pu[:, :TT], op=ALU.mult), reads=[t_, pu], writes=[hid])
            for n in range(16):
                w_ = wot[ctr["wo"] % 2]; ctr["wo"] += 1
                kb.dma("pool", w_[:].rearrange("p k n -> p (k n)"), wo[n, half], reads=[wo], writes=[w_])
                ps = ps_mm[ctr["mm"] % 2]; ctr["mm"] += 1
                for jj in range(22):
                    kb.op("pe", lambda e: e.matmul(ps[:, :TT], w_[:, jj, :], hid[:, jj, :TT], start=(jj == 0), stop=(jj == 21)), reads=[w_, hid], writes=[ps])
                if half == 0:
                    kb.op("act", lambda e: e.activation(out=y[:, n, :TT], in_=ps[:, :TT], func=AF.Identity), reads=[ps], writes=[y])
                else:
                    kb.op("dve", lambda e: e.tensor_tensor(out=y[:, n, :TT], in0=y[:, n, :TT], in1=ps[:, :TT], op=ALU.add), reads=[ps, y], writes=[y])
                    sumsq_step(y[:, n, :TT], y, TT, n == 0, n == 15)
        norm_rstd(kb, ps_ss, TT, rs, rtmp)
        for kc in range(16):
            t_ = tmp[ctr["tmp"] % 2]; ctr["tmp"] += 1
            kb.op("dve", lambda e: e.tensor_tensor(out=t_[:, :TT], in0=y[:, kc, :TT], in1=rs[:, :TT], op=ALU.mult), reads=[y, rs], writes=[t_])
            kb.op("dve", lambda e: e.scalar_tensor_tensor(out=hs[:, kc, :TT], in0=t_[:, :TT], scalar=CG3[:, ms, kc:kc + 1], in1=hs[:, kc, :TT], op0=ALU.mult, op1=ALU.add),
                  reads=[t_, CG3, hs], writes=[hs])
        kb.dma("sp", h2[:, :, t0:t0 + TT], hs[:, :, :TT], reads=[hs], writes=[h2])


def fm(xT):
    n = xT.shape[0] // 128
    return np.ascontiguousarray(xT.reshape(n, 128, -1).transpose(1, 0, 2))


def prep_c_weights(w_out, ffn_w_in, ffn_w_out):
    wout = np.ascontiguousarray(w_out.reshape(16, 128, 16, 128).transpose(2, 1, 0, 3).reshape(16, 128, 2048))
    wi = ffn_w_in.reshape(16, 128, 88, 128).transpose(2, 1, 0, 3).reshape(88, 128, 2048)
    wg = np.ascontiguousarray(wi[:44]); wu = np.ascontiguousarray(wi[44:])
    wo = ffn_w_out.reshape(2, 22, 128, 16, 128).transpose(3, 0, 2, 1, 4).reshape(16, 2, 128, 22 * 128)
    return dict(wout=wout, wg=wg, wu=wu, wo=np.ascontiguousarray(wo))


def premod_setup(kb, mods, ng, mi_shift, mi_scale, gi, mods_off=0):
    mods_s = kb.sb([128, 96, 2]); ng_s = kb.sb([128, 4, 16])
    kb.dma("sp", mods_s[:], mods[:, mods_off:mods_off + 96, :], reads=[mods], writes=[mods_s])
    kb.dma("sp", ng_s[:], ng[:], reads=[ng], writes=[ng_s])
    SC = kb.sb([128, 2, 16]); SH = kb.sb([128, 2, 16])
    for ms in range(2):
        kb.op("dve", lambda e: e.scalar_tensor_tensor(out=SC[:, ms, :], in0=mods_s[:, mi_scale * 16:(mi_scale + 1) * 16, ms], scalar=1.0, in1=ng_s[:, gi, :], op0=ALU.add, op1=ALU.mult),
              reads=[mods_s, ng_s], writes=[SC])
        kb.op("dve", lambda e: e.tensor_copy(out=SH[:, ms, :], in_=mods_s[:, mi_shift * 16:(mi_shift + 1) * 16, ms]), reads=[mods_s], writes=[SH])
    return SC, SH


class PreMod:
    def __init__(self, kb, h, mods, ng, out_dt=BF16, hs_w=512, mods_off=0):
        self.kb = kb; self.h = h
        eps_ap(kb, EPS)
        self.SC, self.SH = premod_setup(kb, mods, ng, 0, 1, 0, mods_off)
        self.hs = kb.sb([128, 16, hs_w]); self.u = kb.sb([128, 16, 512], out_dt)
        self.sq = [kb.sb([128, 512], BF16) for _ in range(2)]
        self.tmp = [kb.sb([128, 512]) for _ in range(2)]
        self.rs = kb.sb([128, 512]); self.rtmp = kb.sb([128, 512])
        self.ones = kb.sb([128, 128], BF16)
        kb.op("pool", lambda e: e.memset(self.ones[:], 1.0), writes=[self.ones])
        self.ps_ss = kb.ps()
        self.c = 0

    def tile(self, t0, TT, ms):
        kb = self.kb; hs = self.hs
        kb.dma("sp", hs[:, :, :TT], self.h[:, :, t0:t0 + TT], reads=[self.h], writes=[hs])
        for kc in range(16):
            s = self.sq[self.c % 2]; self.c += 1
            kb.op("act", lambda e: e.activation(out=s[:, :TT], in_=hs[:, kc, :TT], func=AF.Square), reads=[hs], writes=[s])
            kb.op("pe", lambda e: e.matmul(self.ps_ss[:, :TT], self.ones[:], s[:, :TT], start=(kc == 0), stop=(kc == 15)), reads=[self.ones, s], writes=[self.ps_ss])
        norm_rstd(kb, self.ps_ss, TT, self.rs, self.rtmp)
        for kc in range(16):
            t_ = self.tmp[self.c % 2]; self.c += 1
            kb.op("dve", lambda e: e.tensor_tensor(out=t_[:, :TT], in0=hs[:, kc, :TT], in1=self.rs[:, :TT], op=ALU.mult), reads=[hs, self.rs], writes=[t_])
            kb.op("act", lambda e: e.activation(out=self.u[:, kc, :TT], in_=t_[:, :TT], func=AF.Identity, scale=self.SC[:, ms, kc:kc + 1], bias=self.SH[:, ms, kc:kc + 1]),
                  reads=[t_, self.SC, self.SH], writes=[self.u])
        return self.u


def emit_lru_a(kb, h, mods, ng, w, bin_, gate, rec, mods_off=0):
    pm = PreMod(kb, h, mods, ng, mods_off=mods_off)
    b_s = kb.sb([128, 32])
    kb.dma("sp", b_s[:], bin_[:], reads=[bin_], writes=[b_s])
    wt = [kb.sb([128, 16, 128], BF16) for _ in range(2)]
    pss = [kb.ps() for _ in range(2)]
    z = [kb.sb([128, 512]) for _ in range(2)]; z2 = [kb.sb([128, 512]) for _ in range(2)]
    o = [kb.sb([128, 512]) for _ in range(3)]
    i = 0
    for (t0, TT, ms) in TILES_A:
        u = pm.tile(t0, TT, ms)
        for n in range(32):
            w_ = wt[i % 2]; ps = pss[i % 2]; zz = z[i % 2]; zq = z2[i % 2]; oo = o[i % 3]; i += 1
            kb.dma("pool", w_[:].rearrange("p k n -> p (k n)"), w[n], reads=[w], writes=[w_])
            for kc in range(16):
                kb.op("pe", lambda e: e.matmul(ps[:, :TT], w_[:, kc, :], u[:, kc, :TT], start=(kc == 0), stop=(kc == 15)), reads=[w_, u], writes=[ps])
            if n < 16:
                kb.op("act", lambda e: e.activation(out=zz[:, :TT], in_=ps[:, :TT], func=AF.Identity, bias=b_s[:, n:n + 1]), reads=[ps, b_s], writes=[zz])
                kb.op("act", lambda e: e.activation(out=zq[:, :TT], in_=ps[:, :TT], func=AF.Square, bias=b_s[:, n:n + 1]), reads=[ps, b_s], writes=[zq])
                kb.op("dve", lambda e: e.tensor_scalar(out=zq[:, :TT], in0=zq[:, :TT], scalar1=0.044715, scalar2=1.0, op0=ALU.mult, op1=ALU.add), reads=[zq], writes=[zq])
                kb.op("dve", lambda e: e.tensor_tensor(out=zq[:, :TT], in0=zq[:, :TT], in1=zz[:, :TT], op=ALU.mult), reads=[zq, zz], writes=[zq])
                kb.op("act", lambda e: e.activation(out=zq[:, :TT], in_=zq[:, :TT], func=AF.Sigmoid, scale=1.5957691216057308), reads=[zq], writes=[zq])
                kb.op("dve", lambda e: e.tensor_tensor(out=oo[:, :TT], in0=zq[:, :TT], in1=zz[:, :TT], op=ALU.mult), reads=[zq, zz], writes=[oo])
                kb.dma("sp", gate[:, n, t0:t0 + TT], oo[:, :TT], reads=[oo], writes=[gate])
            else:
                kb.op("act", lambda e: e.activation(out=oo[:, :TT], in_=ps[:, :TT], func=AF.Identity, bias=b_s[:, n:n + 1]), reads=[ps, b_s], writes=[oo])
                kb.dma("sp", rec[:, n - 16, t0:t0 + TT], oo[:, :TT], reads=[oo], writes=[rec])


NS = 8448
CTXN = 256
OFF_C = 2
OFF_L = 261
BUFW = 8454


def emit_lru_b(kb, rec, cw, cb, gw, gb, ll, hsum, col_major):
    NB = 16
    cw_s = kb.sb([128, NB, 4]); cb_s = kb.sb([128, NB]); gwb = [kb.sb([128, 2, 2, 128]) for _ in range(2)]; gb_s = kb.sb([128, NB, 2, 2]); ll_s = kb.sb([128, NB, 2])
    for s, d in ((cw_s, cw), (cb_s, cb), (gb_s, gb), (ll_s, ll)):
        kb.dma("sp", s[:], d[:], reads=[d], writes=[s])
    one_ap = eps_ap(kb, 1.0)
    cl = kb.sb([128, NB, 2]); cl_t = kb.sb([128, NB, 2])
    kb.op("act", lambda e: e.activation(out=cl_t[:], in_=ll_s[:], func=AF.Exp, scale=-1.0), reads=[ll_s], writes=[cl_t])
    kb.op("act", lambda e: e.activation(out=cl_t[:], in_=cl_t[:], func=AF.Ln, bias=one_ap), reads=[cl_t], writes=[cl_t])
    kb.op("dve", lambda e: e.tensor_scalar(out=cl[:], in0=cl_t[:], scalar1=-8.0, scalar2=None, op0=ALU.mult), reads=[cl_t], writes=[cl])

    buf = kb.sb([128, BUFW]); x = kb.sb([128, NS]); a = kb.sb([128, NS]); bt = kb.sb([128, NS]); o0 = kb.sb([128, NS])
    o1 = buf
    pss = [kb.ps() for _ in range(4)]
    tr = [kb.sb([128, 512]) for _ in range(2)]; ti = [kb.sb([128, 512]) for _ in range(2)]; tq = [kb.sb([128, 512]) for _ in range(2)]
    ttiles = [(0, 256)] + [(256 + 512 * i, 512) for i in range(16)]
    segs = [(0, CTXN, OFF_C), (CTXN, NS - CTXN, OFF_L)]
    pieces = [(0, 256)] + [(256 + 2048 * i, 2048) for i in range(4)]
    c = 0
    for blk in range(NB):
        for b in range(1):
            gw_s = gwb[blk % 2]
            kb.dma("sp", gw_s[:], gw[:, blk], reads=[gw], writes=[gw_s])
            kb.op("pool", lambda e: e.memset(buf[:, 0:OFF_C], 0.0), writes=[buf])
            kb.op("pool", lambda e: e.memset(buf[:, OFF_C + CTXN:OFF_L], 0.0), writes=[buf])
            kb.op("pool", lambda e: e.memset(buf[:, OFF_L + NS - CTXN:BUFW], 0.0), writes=[buf])
            kb.dma("sp", buf[:, OFF_C:OFF_C + CTXN], rec[:, blk, LATC:LATC + CTXN], reads=[rec], writes=[buf])
            if col_major:
                kb.dma("sp", a[:, 0:LATC], rec[:, blk, 0:LATC], reads=[rec], writes=[a])
                kb.op("pool", lambda e: e.tensor_copy(out=buf[:, OFF_L:OFF_L + LATC].rearrange("p (c r) -> p c r", c=64, r=128), in_=a[:, 0:LATC].rearrange("p (r c) -> p c r", r=128, c=64)), reads=[a], writes=[buf])
            else:
                kb.dma("sp", buf[:, OFF_L:OFF_L + LATC], rec[:, blk, 0:LATC], reads=[rec], writes=[buf])
            for (s0, sl, off) in segs:
                kb.op("act", lambda e: e.activation(out=x[:, s0:s0 + sl], in_=buf[:, off - 2:off - 2 + sl], func=AF.Identity, scale=cw_s[:, blk, 0:1], bias=cb_s[:, blk:blk + 1]),
                      reads=[buf, cw_s, cb_s], writes=[x])
                for j in range(1, 4):
                    kb.op("dve", lambda e: e.scalar_tensor_tensor(out=x[:, s0:s0 + sl], in0=buf[:, off - 2 + j:off - 2 + j + sl], scalar=cw_s[:, blk, j:j + 1], in1=x[:, s0:s0 + sl], op0=ALU.mult, op1=ALU.add),
                          reads=[buf, cw_s, x], writes=[x])
            for d in range(2):
                for (t0, TT) in ttiles:
                    pr = pss[c % 2]; pi = pss[2 + c % 2]; r_ = tr[c % 2]; i_ = ti[c % 2]; q_ = tq[c % 2]; c += 1
                    kb.op("pe", lambda e: e.matmul(pr[:, :TT], gw_s[:, d, 0, :], x[:, t0:t0 + TT], start=True, stop=True), reads=[gw_s, x], writes=[pr])
                    kb.op("pe", lambda e: e.matmul(pi[:, :TT], gw_s[:, d, 1, :], x[:, t0:t0 + TT], start=True, stop=True), reads=[gw_s, x], writes=[pi])
                    kb.op("act", lambda e: e.activation(out=r_[:, :TT], in_=pr[:, :TT], func=AF.Sigmoid, bias=gb_s[:, blk, d, 0:1]), reads=[pr, gb_s], writes=[r_])
                    kb.op("act", lambda e: e.activation(out=i_[:, :TT], in_=pi[:, :TT], func=AF.Sigmoid, bias=gb_s[:, blk, d, 1:2]), reads=[pi, gb_s], writes=[i_])
                    kb.op("act", lambda e: e.activation(out=a[:, t0:t0 + TT], in_=r_[:, :TT], func=AF.Exp, scale=cl[:, blk, d:d + 1]), reads=[r_, cl], writes=[a])
                    kb.op("act", lambda e: e.activation(out=q_[:, :TT], in_=a[:, t0:t0 + TT], func=AF.Square), reads=[a], writes=[q_])
                    kb.op("dve", lambda e: e.tensor_scalar(out=q_[:, :TT], in0=q_[:, :TT], scalar1=-1.0, scalar2=1.0, op0=ALU.mult, op1=ALU.add), reads=[q_], writes=[q_])
                    kb.op("act", lambda e: e.activation(out=q_[:, :TT], in_=q_[:, :TT], func=AF.Sqrt), reads=[q_], writes=[q_])
                    kb.op("dve", lambda e: e.tensor_tensor(out=i_[:, :TT], in0=i_[:, :TT], in1=x[:, t0:t0 + TT], op=ALU.mult), reads=[i_, x], writes=[i_])
                    kb.op("dve", lambda e: e.tensor_tensor(out=bt[:, t0:t0 + TT], in0=i_[:, :TT], in1=q_[:, :TT], op=ALU.mult), reads=[i_, q_], writes=[bt])
                if d == 0:
                    prev = None
                    for (p0, pl) in pieces:
                        init = 0.0 if prev is None else o0[:, prev - 1:prev]
                        kb.op("dve", lambda e: e.tensor_tensor_scan(out=o0[:, p0:p0 + pl], data0=a[:, p0:p0 + pl], data1=bt[:, p0:p0 + pl], initial=init, op0=ALU.mult, op1=ALU.add),
                              reads=[a, bt, o0], writes=[o0])
                        prev = p0 + pl
                else:
                    prev = None
                    for (p0, pl) in [pieces[0]] + pieces[:0:-1]:
                        init = 0.0 if prev is None else o1[:, prev:prev + 1]
                        kb.op("dve", lambda e: e.tensor_tensor_scan(out=o1[:, p0:p0 + pl][:, ::-1], data0=a[:, p0:p0 + pl][:, ::-1], data1=bt[:, p0:p0 + pl][:, ::-1], initial=init, op0=ALU.mult, op1=ALU.add),
                              reads=[a, bt, o1], writes=[o1])
                        prev = p0
            for (p0, pl) in pieces:
                kb.op("pool", lambda e: e.tensor_tensor(out=o0[:, p0:p0 + pl], in0=o0[:, p0:p0 + pl], in1=o1[:, p0:p0 + pl], op=ALU.add), reads=[o0, o1], writes=[o0])
            kb.dma("sp", hsum[:, blk, LATC:LATC + CTXN], o0[:, 0:CTXN], reads=[o0], writes=[hsum])
            if col_major:
                kb.op("pool", lambda e: e.tensor_copy(out=x[:, 0:LATC].rearrange("p (r c) -> p c r", r=128, c=64), in_=o0[:, CTXN:NS].rearrange("p (c r) -> p c r", c=64, r=128)), reads=[o0], writes=[x])
                kb.dma("sp", hsum[:, blk, 0:LATC], x[:, 0:LATC], reads=[x], writes=[hsum])
            else:
                kb.dma("sp", hsum[:, blk, 0:LATC], o0[:, CTXN:NS], reads=[o0], writes=[hsum])


def emit_gla_a(kb, h, mods, ng, w, w1, br, qkv, sr, gl, mods_off=0):
    pm = PreMod(kb, h, mods, ng, mods_off=mods_off)
    br_s = kb.sb([128, 16]); w1_s = kb.sb([128, 16, 32], BF16)
    kb.dma("sp", br_s[:], br[:], reads=[br], writes=[br_s])
    kb.dma("pool", w1_s[:].rearrange("p k n -> p (k n)"), w1[:], reads=[w1], writes=[w1_s])
    wt = [kb.sb([128, 16, 128], BF16) for _ in range(2)]
    pss = [kb.ps() for _ in range(2)]
    o = [kb.sb([128, 512]) for _ in range(3)]
    i = 0
    for (t0, TT, ms) in TILES_A:
        u = pm.tile(t0, TT, ms)
        ps = pss[i % 2]; oo = o[i % 3]; i += 1
        for kc in range(16):
            kb.op("pe", lambda e: e.matmul(ps[0:32, :TT], w1_s[:, kc, :], u[:, kc, :TT], start=(kc == 0), stop=(kc == 15)), reads=[w1_s, u], writes=[ps])
        kb.op("act", lambda e: e.activation(out=oo[0:32, :TT], in_=ps[0:32, :TT], func=AF.Identity), reads=[ps], writes=[oo])
        kb.dma("sp", gl[:, t0:t0 + TT], oo[0:32, :TT], reads=[oo], writes=[gl])
        for n in range(48):
            w_ = wt[i % 2]; ps = pss[i % 2]; oo = o[i % 3]; i += 1
            kb.dma("pool", w_[:].rearrange("p k n -> p (k n)"), w[n], reads=[w], writes=[w_])
            for kc in range(16):
                kb.op("pe", lambda e: e.matmul(ps[:, :TT], w_[:, kc, :], u[:, kc, :TT], start=(kc == 0), stop=(kc == 15)), reads=[w_, u], writes=[ps])
            if n < 32:
                sc = 1.0 / 16.0 if n < 8 else 1.0
                kb.op("act", lambda e: e.activation(out=oo[:, :TT], in_=ps[:, :TT], func=AF.Identity, scale=sc), reads=[ps], writes=[oo])
                kb.dma("sp", qkv[:, n, t0:t0 + TT], oo[:, :TT], reads=[oo], writes=[qkv])
            else:
                kb.op("act", lambda e: e.activation(out=oo[:, :TT], in_=ps[:, :TT], func=AF.Silu, bias=br_s[:, n - 32:n - 31]), reads=[ps, br_s], writes=[oo])
                kb.dma("sp", sr[:, n - 32, t0:t0 + TT], oo[:, :TT], reads=[oo], writes=[sr])


NCH = NS // 128


def emit_gla_b(kb, hd, qk_s, ktok_s, vtok_s, glT, w2, gbF, gbT, ngT, masks, rmask, ident, o0, pT_s):
    ident_s = kb.sb([128, 128]); otT = [kb.sb([128, 4, 128]) for _ in range(2)]; ps_x = kb.ps()
    kb.dma("sp", ident_s[:], ident[:], reads=[ident], writes=[ident_s])
    w2_s = kb.sb([16, 2, 256]); gbF_s = kb.sb([128, 2, 2]); gbT_s = kb.sb([128, 2, 256]); ngT_s = kb.sb([128, 512])
    masks_s = kb.sb([128, 2, 2, 128]); rmask_s = kb.sb([128, 512])
    kb.dma("sp", w2_s[:], w2[:, :, hd * 256:(hd + 1) * 256], reads=[w2], writes=[w2_s])
    kb.dma("sp", gbF_s[:], gbF[:, :, hd * 2:hd * 2 + 2], reads=[gbF], writes=[gbF_s])
    kb.dma("sp", gbT_s[:], gbT[:, :, hd * 256:(hd + 1) * 256], reads=[gbT], writes=[gbT_s])
    kb.dma("sp", ngT_s[:], ngT[:, hd * 512:(hd + 1) * 512], reads=[ngT], writes=[ngT_s])
    kb.dma("sp", masks_s[:], masks[:], reads=[masks], writes=[masks_s])
    kb.dma("sp", rmask_s[:], rmask[:], reads=[rmask], writes=[rmask_s])
    S = [kb.sb([128, 512]) for _ in range(2)]; Sb = [kb.sb([128, 512], BF16) for _ in range(2)]
    q_g = kb.sb([128, 2, 512]); k_g = kb.sb([128, 2, 512]); gl_g = kb.sb([16, 512])
    gF = kb.sb([128, 2, 512]); bc = kb.sb([128, 2, 512]); Eq = kb.sb([128, 2, 512]); Ek = kb.sb([128, 2, 512])
    qe = kb.sb([128, 2, 512], BF16); ke = kb.sb([128, 2, 512], BF16)
    kt = [kb.sb([128, 256]) for _ in range(2)]; vt = [kb.sb([128, 512], BF16) for _ in range(2)]
    gt = [kb.sb([128, 256]) for _ in range(2)]; krem = [kb.sb([128, 256], BF16) for _ in range(2)]
    attb = [kb.sb([128, 128], BF16) for _ in range(2)]
    ot = [kb.sb([128, 512]) for _ in range(2)]; o0t = [kb.sb([128, 512]) for _ in range(2)]; junk = kb.sb([128, 512])
    ssq = [kb.sb([128, 1]) for _ in range(2)]
    ps_g = kb.ps(); ps_t = kb.ps(); ps_a = kb.ps(); ps_o = [kb.ps() for _ in range(2)]; ps_s = [kb.ps() for _ in range(2)]
    eps_t = eps_ap(kb, EPS)
    ci = 0
    for d in range(2):
        for kc in range(2):
            kb.op("dve", lambda e: e.memset(S[kc][:], 0.0), writes=[S[kc]])
            kb.op("dve", lambda e: e.memset(Sb[kc][:], 0.0), writes=[Sb[kc]])
        groups = [(0, 2)] + [(2 + 4 * g, 4) for g in range(16)]
        if d == 1:
            groups = [groups[0]] + groups[:0:-1]
        for (c0, nc_) in groups:
            t0 = c0 * 128; TT = nc_ * 128
            kb.dma("sp", q_g[:, :, :TT], qk_s[:, hd * 2:hd * 2 + 2, t0:t0 + TT], reads=[qk_s], writes=[q_g])
            kb.dma("sp", k_g[:, :, :TT], qk_s[:, 8 + hd * 2:8 + hd * 2 + 2, t0:t0 + TT], reads=[qk_s], writes=[k_g])
            kb.dma("sp", gl_g[:, :TT], glT[:, d, t0:t0 + TT], reads=[glT], writes=[gl_g])
            for kc in range(2):
                kb.op("pe", lambda e: e.matmul(ps_g[:, :TT], w2_s[:, d, kc * 128:(kc + 1) * 128], gl_g[:, :TT], start=True, stop=True), reads=[w2_s, gl_g], writes=[ps_g])
                kb.op("act", lambda e: e.activation(out=gF[:, kc, :TT], in_=ps_g[:, :TT], func=AF.Sigmoid, bias=gbF_s[:, d, kc:kc + 1]), reads=[ps_g, gbF_s], writes=[gF])
            kb.op("act", lambda e: e.activation(out=gF[:, :, :TT], in_=gF[:, :, :TT], func=AF.Ln), reads=[gF], writes=[gF])
            kb.op("dve", lambda e: e.tensor_scalar(out=gF[:, :, :TT], in0=gF[:, :, :TT], scalar1=1.0 / 16.0, scalar2=None, op0=ALU.mult), reads=[gF], writes=[gF])
            for kc in range(2):
                if d == 0:
                    kb.op("dve", lambda e: e.tensor_tensor_scan(out=bc[:, kc, :TT], data0=rmask_s[:, :TT], data1=gF[:, kc, :TT], initial=0.0, op0=ALU.mult, op1=ALU.add),
                          reads=[rmask_s, gF], writes=[bc])
                else:
                    kb.op("dve", lambda e: e.tensor_tensor_scan(out=bc[:, kc, :TT][:, ::-1], data0=rmask_s[:, :TT], data1=gF[:, kc, :TT][:, ::-1], initial=0.0, op0=ALU.mult, op1=ALU.add),
                          reads=[rmask_s, gF], writes=[bc])
            kb.op("act", lambda e: e.activation(out=Eq[:, :, :TT], in_=bc[:, :, :TT], func=AF.Exp), reads=[bc], writes=[Eq])
            kb.op("act", lambda e: e.activation(out=Ek[:, :, :TT], in_=bc[:, :, :TT], func=AF.Exp, scale=-1.0), reads=[bc], writes=[Ek])
            kb.op("dve", lambda e: e.tensor_tensor(out=qe[:, :, :TT], in0=q_g[:, :, :TT], in1=Eq[:, :, :TT], op=ALU.mult), reads=[q_g, Eq], writes=[qe])
            kb.op("dve", lambda e: e.tensor_tensor(out=ke[:, :, :TT], in0=k_g[:, :, :TT], in1=Ek[:, :, :TT], op=ALU.mult), reads=[k_g, Ek], writes=[ke])
            chunks = list(range(nc_)) if d == 0 else list(range(nc_ - 1, -1, -1))
            for cc in chunks:
                c = c0 + cc; a0 = cc * 128; tt0 = c * 128
                kt_ = kt[ci % 2]; vt_ = vt[ci % 2]; gt_ = gt[ci % 2]; kr_ = krem[ci % 2]; ab_ = attb[ci % 2]
                ot_ = ot[ci % 2]; o0_ = o0t[ci % 2]; pso = ps_o[ci % 2]; ss_ = ssq[ci % 2]; ci += 1
                kb.dma("sp", kt_[:], ktok_s[tt0:tt0 + 128, hd * 256:(hd + 1) * 256], reads=[ktok_s], writes=[kt_])
                kb.dma("pool", vt_[:], vtok_s[tt0:tt0 + 128, hd * 512:(hd + 1) * 512], reads=[vtok_s], writes=[vt_])
                kb.op("pe", lambda e: e.matmul(ps_t[:, 0:256], gl_g[:, a0:a0 + 128], w2_s[:, d, :], start=True, stop=True), reads=[gl_g, w2_s], writes=[ps_t])
                kb.op("dve", lambda e: e.tensor_tensor(out=gt_[:], in0=ps_t[:, 0:256], in1=gbT_s[:, d, :], op=ALU.add), reads=[ps_t, gbT_s], writes=[gt_])
                kb.op("act", lambda e: e.activation(out=gt_[:], in_=gt_[:], func=AF.Sigmoid), reads=[gt_], writes=[gt_])
                kb.op("act", lambda e: e.activation(out=gt_[:], in_=gt_[:], func=AF.Ln), reads=[gt_], writes=[gt_])
                kb.op("pe", lambda e: e.matmul(ps_t[:, 256:512], masks_s[:, d, 0, :], gt_[:], start=True, stop=True), reads=[masks_s, gt_], writes=[ps_t])
                kb.op("act", lambda e: e.activation(out=gt_[:], in_=ps_t[:, 256:512], func=AF.Exp), reads=[ps_t], writes=[gt_])
                kb.op("dve", lambda e: e.tensor_tensor(out=kr_[:], in0=kt_[:], in1=gt_[:], op=ALU.mult), reads=[kt_, gt_], writes=[kr_])
                for kc in range(2):
                    kb.op("pe", lambda e: e.matmul(ps_a[:, 0:128], ke[:, kc, a0:a0 + 128], qe[:, kc, a0:a0 + 128], start=(kc == 0), stop=(kc == 1)), reads=[ke, qe], writes=[ps_a])
                kb.op("dve", lambda e: e.tensor_tensor(out=ab_[:], in0=ps_a[:, 0:128], in1=masks_s[:, d, 1, :], op=ALU.mult), reads=[ps_a, masks_s], writes=[ab_])
                kb.op("pe", lambda e: e.matmul(pso[:], ab_[:], vt_[:], start=True, stop=False), reads=[ab_, vt_], writes=[pso])
                for kc in range(2):
                    kb.op("pe", lambda e: e.matmul(pso[:], qe[:, kc, a0:a0 + 128], Sb[kc][:], start=False, stop=(kc == 1)), reads=[qe, Sb[kc]], writes=[pso])
                if d == 0:
                    kb.op("act", lambda e: e.activation(out=ot_[:], in_=pso[:], func=AF.Identity), reads=[pso], writes=[ot_])
                    kb.dma("sp", o0[tt0:tt0 + 128, :], ot_[:], reads=[ot_], writes=[o0])
                else:
                    kb.dma("sp", o0_[:], o0[tt0:tt0 + 128, :], reads=[o0], writes=[o0_])
                    kb.op("dve", lambda e: e.tensor_tensor(out=ot_[:], in0=pso[:], in1=o0_[:], op=ALU.add), reads=[pso, o0_], writes=[ot_])
                    kb.op("act", lambda e: e.activation(out=junk[:], in_=ot_[:], func=AF.Square, accum_out=ss_[:]), reads=[ot_], writes=[junk, ss_])
                    kb.op("act", lambda e: e.activation(out=ss_[:], in_=ss_[:], func=AF.Sqrt, scale=1.0 / 512.0, bias=eps_t), reads=[ss_], writes=[ss_])
                    kb.op("dve", lambda e: e.reciprocal(out=ss_[:], in_=ss_[:]), reads=[ss_], writes=[ss_])
                    kb.op("dve", lambda e: e.scalar_tensor_tensor(out=ot_[:], in0=ot_[:], scalar=ss_[:, 0:1], in1=ngT_s[:], op0=ALU.mult, op1=ALU.mult), reads=[ot_, ss_, ngT_s], writes=[ot_])
                    oT_ = otT[ci % 2]
                    for jx in range(4):
                        kb.op("pe", lambda e: e.matmul(ps_x[:, jx * 128:(jx + 1) * 128], ot_[:, jx * 128:(jx + 1) * 128], ident_s[:], start=True, stop=True), reads=[ot_, ident_s], writes=[ps_x])
                    kb.op("act", lambda e: e.activation(out=oT_[:].rearrange("p a t -> p (a t)"), in_=ps_x[:, 0:512], func=AF.Identity), reads=[ps_x], writes=[oT_])
                    kb.dma("sp", pT_s[:, hd * 4:(hd + 1) * 4, tt0:tt0 + 128], oT_[:], reads=[oT_], writes=[pT_s])
                lastcol = a0 + 127 if d == 0 else a0
                for kc in range(2):
                    pss_ = ps_s[kc]
                    kb.op("pe", lambda e: e.matmul(pss_[:], kr_[:, kc * 128:(kc + 1) * 128], vt_[:], start=True, stop=True), reads=[kr_, vt_], writes=[pss_])
                    kb.op("dve", lambda e: e.scalar_tensor_tensor(out=S[kc][:], in0=S[kc][:], scalar=Eq[:, kc, lastcol:lastcol + 1], in1=pss_[:], op0=ALU.mult, op1=ALU.add),
                          reads=[S[kc], Eq, pss_], writes=[S[kc]])
                    kb.op("act", lambda e: e.activation(out=Sb[kc][:], in_=S[kc][:], func=AF.Identity), reads=[S[kc]], writes=[Sb[kc]])


def gla_b_consts():
    s = np.arange(128)[:, None]; t = np.arange(128)[None, :]
    masks = np.zeros((128, 2, 2, 128), np.float32)
    masks[:, 0, 0, :] = (s > t) / 16.0
    masks[:, 1, 0, :] = (s < t) / 16.0
    masks[:, 0, 1, :] = (s <= t)
    masks[:, 1, 1, :] = (s >= t)
    rmask = np.ones((128, 512), np.float32); rmask[:, ::128] = 0.0
    return masks, rmask


US_W = 8720
US_LAT = 1
US_CTX = 8195


def emit_rwkv_a(kb, h, mods, ng, mu, w, wl, g1, g2, rkv, lora, gout, us, mods_off=0):
    pm = PreMod(kb, h, mods, ng, out_dt=F32, hs_w=516, mods_off=mods_off)
    mu_s = kb.sb([128, 6, 16]); wl_s = kb.sb([128, 16, 384], BF16); g1_s = kb.sb([128, 16, 256], BF16); g2_s = kb.sb([128, 16, 2, 128], BF16)
    kb.dma("sp", mu_s[:], mu[:], reads=[mu], writes=[mu_s])
    kb.dma("pool", wl_s[:].rearrange("p k n -> p (k n)"), wl[:], reads=[wl], writes=[wl_s])
    kb.dma("pool", g1_s[:].rearrange("p k n -> p (k n)"), g1[:], reads=[g1], writes=[g1_s])
    for n in range(16):
        kb.dma("pool", g2_s[:, n].rearrange("p k n -> p (k n)"), g2[n], reads=[g2], writes=[g2_s])
    zt = pm.hs
    kb.op("dve", lambda e: e.memset(zt[:], 0.0), writes=[zt])
    c0 = 0
    while c0 < US_W:
        wd = min(516, US_W - c0)
        kb.dma("sp", us[:, :, c0:c0 + wd], zt[:, :, :wd], reads=[zt], writes=[us])
        c0 += wd
    for (t0, TT, ms) in TILES_A:
        u32 = pm.tile(t0, TT, ms)
        if ms == 1:
            kb.dma("sp", us[:, :, US_LAT + t0:US_LAT + t0 + TT], u32[:, :, :TT], reads=[u32], writes=[us])
        else:
            kb.dma("sp", us[:, :, US_CTX:US_CTX + CTXC], u32[:, :, :CTXC], reads=[u32], writes=[us])
    uc = pm.hs; dx = pm.u
    xm = [kb.sb([128, 16, 512], BF16) for _ in range(2)]
    wt = [kb.sb([128, 16, 128], BF16) for _ in range(2)]
    pss = [kb.ps() for _ in range(2)]
    o = [kb.sb([128, 512]) for _ in range(3)]
    glb = kb.sb([128, 2, 512], BF16)
    i = 0; xi = 0
    for (t0, TT, ms) in TILES_A:
        off = (US_LAT + t0) if ms == 1 else US_CTX
        kb.dma("sp", uc[:, :, 0:TT + 2], us[:, :, off - 1:off + TT + 1], reads=[us], writes=[uc])
        kb.op("dve", lambda e: e.tensor_tensor(out=dx[:, :, :TT], in0=uc[:, :, 0:TT], in1=uc[:, :, 2:TT + 2], op=ALU.add), reads=[uc], writes=[dx])
        kb.op("dve", lambda e: e.scalar_tensor_tensor(out=dx[:, :, :TT], in0=dx[:, :, :TT], scalar=0.5, in1=uc[:, :, 1:TT + 1], op0=ALU.mult, op1=ALU.subtract), reads=[dx, uc], writes=[dx])
        for m in (0, 2, 3, 1, 4, 5):
            x_ = xm[xi % 2]; xi += 1
            for kc in range(16):
                kb.op("dve", lambda e: e.scalar_tensor_tensor(out=x_[:, kc, :TT], in0=dx[:, kc, :TT], scalar=mu_s[:, m, kc:kc + 1], in1=uc[:, kc, 1:TT + 1], op0=ALU.mult, op1=ALU.add),
                      reads=[dx, mu_s, uc], writes=[x_])
            if m in (0, 2, 3):
                base = {0: 0, 2: 16, 3: 32}[m]
                for n in range(16):
                    w_ = wt[i % 2]; ps = pss[i % 2]; oo = o[i % 3]; i += 1
                    kb.dma("pool", w_[:].rearrange("p k n -> p (k n)"), w[base + n], reads=[w], writes=[w_])
                    for kc in range(16):
                        kb.op("pe", lambda e: e.matmul(ps[:, :TT], w_[:, kc, :], x_[:, kc, :TT], start=(kc == 0), stop=(kc == 15)), reads=[w_, x_], writes=[ps])
                    kb.op("act", lambda e: e.activation(out=oo[:, :TT], in_=ps[:, :TT], func=AF.Identity), reads=[ps], writes=[oo])
                    kb.dma("sp", rkv[:, base + n, t0:t0 + TT], oo[:, :TT], reads=[oo], writes=[rkv])
            elif m in (1, 4):
                for d in range(2):
                    col = (0 if m == 1 else 192) + d * 96
                    ps = pss[i % 2]; oo = o[i % 3]; i += 1
                    for kc in range(16):
                        kb.op("pe", lambda e: e.matmul(ps[0:96, :TT], wl_s[:, kc, col:col + 96], x_[:, kc, :TT], start=(kc == 0), stop=(kc == 15)), reads=[wl_s, x_], writes=[ps])
                    kb.op("act", lambda e: e.activation(out=oo[0:96, :TT], in_=ps[0:96, :TT], func=(AF.Tanh if m == 1 else AF.Identity)), reads=[ps], writes=[oo])
                    kb.dma("sp", lora[:, (0 if m == 1 else 2) + d, t0:t0 + TT], oo[0:96, :TT], reads=[oo], writes=[lora])
            else:
                for c2 in range(2):
                    ps = pss[i % 2]; i += 1
                    for kc in range(16):
                        kb.op("pe", lambda e: e.matmul(ps[:, :TT], g1_s[:, kc, c2 * 128:(c2 + 1) * 128], x_[:, kc, :TT], start=(kc == 0), stop=(kc == 15)), reads=[g1_s, x_], writes=[ps])
                    kb.op("act", lambda e: e.activation(out=glb[:, c2, :TT], in_=ps[:, :TT], func=AF.Sigmoid), reads=[ps], writes=[glb])
                for n in range(16):
                    ps = pss[i % 2]; oo = o[i % 3]; i += 1
                    for kc in range(2):
                        kb.op("pe", lambda e: e.matmul(ps[:, :TT], g2_s[:, n, kc, :], glb[:, kc, :TT], start=(kc == 0), stop=(kc == 1)), reads=[g2_s, glb], writes=[ps])
                    kb.op("act", lambda e: e.activation(out=oo[:, :TT], in_=ps[:, :TT], func=AF.Identity), reads=[ps], writes=[oo])
                    kb.dma("sp", gout[:, n, t0:t0 + TT], oo[:, :TT], reads=[oo], writes=[gout])


NCHK = NS // 128
GN_EPS = 64e-5
P_KK, P_KA, P_RK, P_LNW, P_LNB, P_W0, P_A0 = 0, 1, 2, 3, 4, 5, 7


def emit_rwkv_b(kb, hq, rkv, lo, w2, a2, prm, cst, msk, rmk, y0, bv0, p1):
    stage = 9
    NCHK = NS // 128

    def col(c):
        return LATC + c * 128 if c < 2 else (c - 2) * 128
    w2_s = kb.sb([96, 2, 512]); a2_s = kb.sb([96, 2, 512]); prm_s = kb.sb([128, 9, 4]); cst_s = kb.sb([128, 2, 128])
    msk_s = kb.sb([128, 2, 640]); rmk_s = kb.sb([128, 512])
    kb.dma("sp", w2_s[:], w2[:, :, hq * 512:(hq + 1) * 512], reads=[w2], writes=[w2_s])
    kb.dma("sp", a2_s[:], a2[:, :, hq * 512:(hq + 1) * 512], reads=[a2], writes=[a2_s])
    kb.dma("sp", prm_s[:], prm[:, :, hq * 4:(hq + 1) * 4], reads=[prm], writes=[prm_s])
    for s, d in ((cst_s, cst), (msk_s, msk), (rmk_s, rmk)):
        kb.dma("sp", s[:], d[:], reads=[d], writes=[s])
    bones = cst_s[:, 0, :]; ident = cst_s[:, 1, :]

    def pb(idx):
        return prm_s[:, idx, :].unsqueeze(2).broadcast_to([128, 4, 128])

    X3 = [128, 4, 128]
    R = kb.sb(X3); K = kb.sb(X3); V = kb.sb(X3); LO = kb.sb([96, 4, 128])
    LW = kb.sb(X3); AS = kb.sb(X3); KK = kb.sb(X3); KD = kb.sb(X3); B_ = kb.sb(X3); CUM = kb.sb(X3)
    G = kb.sb(X3); GI = kb.sb(X3); GP = kb.sb(X3); t1 = kb.sb(X3); t2 = kb.sb(X3); BV = kb.sb(X3)
    AR = kb.sb([128, 4, 2, 128]); BK = kb.sb([128, 4, 2, 128])
    YT = kb.sb(X3); Y0 = kb.sb(X3); BV0 = kb.sb(X3)
    Hz = [kb.sb([128, 128]) for _ in range(4)]
    BZ = [kb.sb([128, 128]) for _ in range(2)]; KZ = [kb.sb([128, 128]) for _ in range(2)]
    VZ = [kb.sb([128, 128]) for _ in range(2)]; UZ = [kb.sb([128, 128]) for _ in range(2)]
    for t in BZ + KZ + VZ + UZ:
        kb.op("dve", lambda e: e.memset(t[:], 0.0), writes=[t])
    SC = [kb.sb([128, 512]) for _ in range(2)]
    XX = [kb.sb([128, 2, 2, 128]) for _ in range(2)]
    TT = kb.sb([128, 2, 128]); RHS = [kb.sb([128, 64]) for _ in range(2)]
    ps_z = kb.ps(); ps_tr = kb.ps(); ps_sc = kb.ps(); ps_p = kb.ps(); ps_t = kb.ps(); ps_r = kb.ps(); ps_y = kb.ps(); ps_h = kb.ps()
    eps_gn = eps_ap(kb, GN_EPS)

    def bmm(out_ps, src, src_t):
        kb.op("pe", lambda e: e.matmul(out_ps[:, 0:512], bones, src[:].rearrange("p a t -> p (a t)"), start=True, stop=True), reads=[cst_s, src_t], writes=[out_ps])

    for d in range(2):
        for hz in Hz:
            kb.op("dve", lambda e: e.memset(hz[:], 0.0), writes=[hz])
        order = list(range(NCHK)) if d == 0 else [1, 0] + list(range(NCHK - 1, 1, -1))
        last = 127 if d == 0 else 0
        for c in order:
            t0 = c * 128
            sc0 = col(c)
            kb.dma("sp", R[:], rkv[:, hq * 4:hq * 4 + 4, sc0:sc0 + 128], reads=[rkv], writes=[R])
            kb.dma("sp", K[:], rkv[:, 16 + hq * 4:16 + hq * 4 + 4, sc0:sc0 + 128], reads=[rkv], writes=[K])
            kb.dma("sp", V[:], rkv[:, 32 + hq * 4:32 + hq * 4 + 4, sc0:sc0 + 128], reads=[rkv], writes=[V])
            kb.dma("sp", LO[:], lo[:, :, sc0:sc0 + 128], reads=[lo], writes=[LO])
            for hp in range(4):
                kb.op("pe", lambda e: e.matmul(ps_z[:, hp * 128:(hp + 1) * 128], w2_s[:, d, hp * 128:(hp + 1) * 128], LO[:, d, :], start=True, stop=True), reads=[w2_s, LO], writes=[ps_z])
            kb.op("dve", lambda e: e.tensor_tensor(out=t1[:], in0=ps_z[:, 0:512].rearrange("p (a t) -> p a t", a=4), in1=pb(P_W0 + d), op=ALU.add), reads=[ps_z, prm_s], writes=[t1])
            kb.op("act", lambda e: e.activation(out=t1[:], in_=t1[:], func=AF.Sigmoid), reads=[t1], writes=[t1])
            kb.op("dve", lambda e: e.tensor_scalar(out=LW[:], in0=t1[:], scalar1=-0.6065306597126334, scalar2=None, op0=ALU.mult), reads=[t1], writes=[LW])
            for hp in range(4):
                kb.op("pe", lambda e: e.matmul(ps_z[:, hp * 128:(hp + 1) * 128], a2_s[:, d, hp * 128:(hp + 1) * 128], LO[:, 2 + d, :], start=True, stop=True), reads=[a2_s, LO], writes=[ps_z])
            kb.op("dve", lambda e: e.tensor_tensor(out=AS[:], in0=ps_z[:, 0:512].rearrange("p (a t) -> p a t", a=4), in1=pb(P_A0 + d), op=ALU.add), reads=[ps_z, prm_s], writes=[AS])
            kb.op("act", lambda e: e.activation(out=AS[:], in_=AS[:], func=AF.Sigmoid), reads=[AS], writes=[AS])
            kb.op("dve", lambda e: e.tensor_tensor(out=KK[:], in0=K[:], in1=pb(P_KK), op=ALU.mult), reads=[K, prm_s], writes=[KK])
            kb.op("dve", lambda e: e.tensor_tensor(out=t2[:], in0=KK[:], in1=KK[:], op=ALU.mult), reads=[KK], writes=[t2])
            bmm(ps_z, t2, t2)
            kb.op("act", lambda e: e.activation(out=t2[:], in_=ps_z[:, 0:512].rearrange("p (a t) -> p a t", a=4), func=AF.Sqrt), reads=[ps_z], writes=[t2])
            kb.op("dve", lambda e: e.tensor_scalar(out=t2[:], in0=t2[:], scalar1=1e-12, scalar2=None, op0=ALU.max), reads=[t2], writes=[t2])
            kb.op("dve", lambda e: e.reciprocal(out=t2[:], in_=t2[:]), reads=[t2], writes=[t2])
            kb.op("dve", lambda e: e.tensor_tensor(out=KK[:], in0=KK[:], in1=t2[:], op=ALU.mult), reads=[KK, t2], writes=[KK])
            kb.op("dve", lambda e: e.scalar_tensor_tensor(out=t2[:], in0=AS[:], scalar=-1.0, in1=pb(P_KA), op0=ALU.add, op1=ALU.mult), reads=[AS, prm_s], writes=[t2])
            kb.op("dve", lambda e: e.scalar_tensor_tensor(out=KD[:], in0=t2[:], scalar=1.0, in1=K[:], op0=ALU.add, op1=ALU.mult), reads=[t2, K], writes=[KD])
            kb.op("dve", lambda e: e.tensor_tensor(out=B_[:], in0=KK[:], in1=AS[:], op=ALU.mult), reads=[KK, AS], writes=[B_])
            lwf = LW[:].rearrange("p a t -> p (a t)"); cumf = CUM[:].rearrange("p a t -> p (a t)")
            if d == 0:
                kb.op("dve", lambda e: e.tensor_tensor_scan(out=cumf, data0=rmk_s[:], data1=lwf, initial=0.0, op0=ALU.mult, op1=ALU.add), reads=[rmk_s, LW], writes=[CUM])
            else:
                kb.op("dve", lambda e: e.tensor_tensor_scan(out=cumf[:, ::-1], data0=rmk_s[:], data1=lwf[:, ::-1], initial=0.0, op0=ALU.mult, op1=ALU.add), reads=[rmk_s, LW], writes=[CUM])
            kb.op("act", lambda e: e.activation(out=G[:], in_=CUM[:], func=AF.Exp), reads=[CUM], writes=[G])
            kb.op("act", lambda e: e.activation(out=GI[:], in_=CUM[:], func=AF.Exp, scale=-1.0), reads=[CUM], writes=[GI])
            kb.op("dve", lambda e: e.tensor_tensor(out=t2[:], in0=CUM[:], in1=LW[:], op=ALU.subtract), reads=[CUM, LW], writes=[t2])
            kb.op("act", lambda e: e.activation(out=GP[:], in_=t2[:], func=AF.Exp), reads=[t2], writes=[GP])
            kb.op("dve", lambda e: e.scalar_tensor_tensor(out=AR[:, :, 0, :], in0=KK[:], scalar=-1.0, in1=GP[:], op0=ALU.mult, op1=ALU.mult), reads=[KK, GP], writes=[AR])
            kb.op("dve", lambda e: e.tensor_tensor(out=AR[:, :, 1, :], in0=R[:], in1=G[:], op=ALU.mult), reads=[R, G], writes=[AR])
            kb.op("dve", lambda e: e.tensor_tensor(out=BK[:, :, 0, :], in0=B_[:], in1=GI[:], op=ALU.mult), reads=[B_, GI], writes=[BK])
            kb.op("dve", lambda e: e.tensor_tensor(out=BK[:, :, 1, :], in0=KD[:], in1=GI[:], op=ALU.mult), reads=[KD, GI], writes=[BK])
            kb.op("dve", lambda e: e.tensor_tensor(out=t2[:], in0=R[:], in1=KD[:], op=ALU.mult), reads=[R, KD], writes=[t2])
            kb.op("dve", lambda e: e.tensor_tensor(out=t2[:], in0=t2[:], in1=pb(P_RK), op=ALU.mult), reads=[t2, prm_s], writes=[t2])
            bmm(ps_z, t2, t2)
            kb.op("dve", lambda e: e.tensor_tensor(out=BV[:], in0=ps_z[:, 0:512].rearrange("p (a t) -> p a t", a=4), in1=V[:], op=ALU.mult), reads=[ps_z, V], writes=[BV])
            for hp in range(4 if stage >= 1 else 0):
                hz = Hz[hp]
                kb.op("pe", lambda e: e.matmul(ps_tr[:, 0:128], BK[:, hp, 0, :], ident, start=True, stop=True), reads=[BK, cst_s], writes=[ps_tr])
                kb.op("pe", lambda e: e.matmul(ps_tr[:, 128:256], BK[:, hp, 1, :], ident, start=True, stop=True), reads=[BK, cst_s], writes=[ps_tr])
                kb.op("pe", lambda e: e.matmul(ps_tr[:, 256:384], V[:, hp, :], ident, start=True, stop=True), reads=[V, cst_s], writes=[ps_tr])
                for h2 in range(2):
                    cs = slice(64 * h2, 64 * h2 + 64)
                    kb.op("act", lambda e: e.activation(out=BZ[h2][:, cs], in_=ps_tr[:, 64 * h2:64 * h2 + 64], func=AF.Identity), reads=[ps_tr], writes=[BZ[h2]])
                    kb.op("dve", lambda e: e.tensor_copy(out=KZ[h2][:, cs], in_=ps_tr[:, 128 + 64 * h2:128 + 64 * h2 + 64]), reads=[ps_tr], writes=[KZ[h2]])
                    kb.op("act", lambda e: e.activation(out=VZ[h2][:, cs], in_=ps_tr[:, 256 + 64 * h2:256 + 64 * h2 + 64], func=AF.Identity), reads=[ps_tr], writes=[VZ[h2]])
                if stage < 2:
                    continue
                xx = XX[0]
                for h2 in range(2):
                    P = slice(64 * h2, 64 * h2 + 64)
                    arf = AR[P, hp, :, :].rearrange("p a t -> p (a t)")
                    kb.op("pe", lambda e: e.matmul(ps_sc[:, 0:256], BK[P, hp, 0, :], arf, start=True, stop=True), reads=[BK, AR], writes=[ps_sc])
                    kb.op("pe", lambda e: e.matmul(ps_sc[:, 256:512], BK[P, hp, 1, :], arf, start=True, stop=True), reads=[BK, AR], writes=[ps_sc])
                    kb.op("pe", lambda e: e.matmul(ps_t[:, 256:384], AR[P, hp, 0, :], BK[P, hp, 0, :], start=True, stop=True), reads=[BK, AR], writes=[ps_t])
                    kb.op("dve", lambda e: e.tensor_tensor(out=SC[h2][:], in0=ps_sc[:, 0:512], in1=msk_s[:, d, 0:512], op=ALU.mult), reads=[ps_sc, msk_s], writes=[SC[h2]])
                    kb.op("act", lambda e: e.activation(out=xx[:, h2, 0, :], in_=SC[h2][:, 0:128], func=AF.Identity), reads=[SC[h2]], writes=[xx])
                    kb.op("dve", lambda e: e.tensor_tensor(out=xx[:, h2, 1, :], in0=ps_t[:, 256:384], in1=msk_s[:, d, 512:640], op=ALU.mult), reads=[ps_t, msk_s], writes=[xx])
                    kb.op("dve", lambda e: e.tensor_tensor(out=TT[:, h2, :], in0=SC[h2][:, 0:128], in1=ident, op=ALU.add), reads=[SC[h2], cst_s], writes=[TT])
                if stage < 3:
                    continue
                cur = 0
                for lvl in range(6):
                    xa = XX[cur]; xb = XX[1 - cur]
                    for h2 in range(2):
                        kb.op("pe", lambda e: e.matmul(ps_p[:, h2 * 256:h2 * 256 + 128], xa[:, h2, 1, :], xa[:, h2, 0, :], start=True, stop=True), reads=[xa], writes=[ps_p])
                        kb.op("pe", lambda e: e.matmul(ps_p[:, h2 * 256 + 128:h2 * 256 + 256], xa[:, h2, 0, :], xa[:, h2, 1, :], start=True, stop=True), reads=[xa], writes=[ps_p])
                    kb.op("act", lambda e: e.activation(out=xb[:].rearrange("p h a t -> p (h a t)"), in_=ps_p[:, 0:512], func=AF.Identity), reads=[ps_p], writes=[xb])
                    for h2 in range(2):
                        kb.op("pe", lambda e: e.matmul(ps_t[:, h2 * 128:(h2 + 1) * 128], xb[:, h2, 1, :], TT[:, h2, :], start=True, stop=True), reads=[xb, TT], writes=[ps_t])
                    kb.op("dve", lambda e: e.tensor_tensor(out=TT[:].rearrange("p h t -> p (h t)"), in0=TT[:].rearrange("p h t -> p (h t)"), in1=ps_t[:, 0:256], op=ALU.add), reads=[TT, ps_t], writes=[TT])
                    cur = 1 - cur
                if stage < 4:
                    continue
                for h2 in range(2):
                    P = slice(64 * h2, 64 * h2 + 64); cs = P
                    kb.op("pe", lambda e: e.matmul(ps_r[:, h2 * 64:h2 * 64 + 64], AR[P, hp, 0, :], hz[P, cs], start=True, stop=False), reads=[AR, hz], writes=[ps_r])
                    kb.op("pe", lambda e: e.matmul(ps_r[:, h2 * 64:h2 * 64 + 64], SC[h2][:, 256:384], VZ[h2][:, cs], start=False, stop=True), reads=[SC[h2], VZ[h2]], writes=[ps_r])
                    kb.op("act", lambda e: e.activation(out=RHS[h2][:], in_=ps_r[:, h2 * 64:h2 * 64 + 64], func=AF.Identity), reads=[ps_r], writes=[RHS[h2]])
                    kb.op("pe", lambda e: e.matmul(ps_r[:, 128 + h2 * 64:128 + h2 * 64 + 64], TT[:, h2, :], RHS[h2][:], start=True, stop=True), reads=[TT, RHS[h2]], writes=[ps_r])
                    kb.op("dve", lambda e: e.tensor_copy(out=UZ[h2][:, cs], in_=ps_r[:, 128 + h2 * 64:128 + h2 * 64 + 64]), reads=[ps_r], writes=[UZ[h2]])
                if stage < 5:
                    continue
                for h2 in range(2):
                    P = slice(64 * h2, 64 * h2 + 64)
                    kb.op("pe", lambda e: e.matmul(ps_y[:, 0:128], hz[P, :], AR[P, hp, 1, :], start=(h2 == 0), stop=False), reads=[hz, AR], writes=[ps_y])
                    kb.op("pe", lambda e: e.matmul(ps_y[:, 0:128], UZ[h2][:], SC[h2][:, 128:256], start=False, stop=False), reads=[UZ[h2], SC[h2]], writes=[ps_y])
                    kb.op("pe", lambda e: e.matmul(ps_y[:, 0:128], VZ[h2][:], SC[h2][:, 384:512], start=False, stop=(h2 == 1)), reads=[VZ[h2], SC[h2]], writes=[ps_y])
                kb.op("act", lambda e: e.activation(out=YT[:, hp, :], in_=ps_y[:, 0:128], func=AF.Identity), reads=[ps_y], writes=[YT])
                if stage < 6:
                    continue
                for h2 in range(2):
                    cs = slice(64 * h2, 64 * h2 + 64)
                    kb.op("pe", lambda e: e.matmul(ps_h[:, 0:64], BZ[h2][:], UZ[h2][:, cs], start=(h2 == 0), stop=False), reads=[BZ[h2], UZ[h2]], writes=[ps_h])
                    kb.op("pe", lambda e: e.matmul(ps_h[:, 0:64], KZ[h2][:], VZ[h2][:, cs], start=False, stop=(h2 == 1)), reads=[KZ[h2], VZ[h2]], writes=[ps_h])
                for h2 in range(2):
                    P = slice(64 * h2, 64 * h2 + 64)
                    kb.op("dve", lambda e: e.tensor_tensor(out=hz[P, P], in0=hz[P, P], in1=ps_h[P, 0:64], op=ALU.add), reads=[hz, ps_h], writes=[hz])
                    kb.op("dve", lambda e: e.tensor_scalar(out=hz[P, P], in0=hz[P, P], scalar1=G[P, hp, last:last + 1], scalar2=None, op0=ALU.mult), reads=[hz, G], writes=[hz])
            if d == 0:
                kb.dma("sp", y0[:, :, t0:t0 + 128], YT[:], reads=[YT], writes=[y0])
                kb.dma("sp", bv0[:, :, t0:t0 + 128], BV[:], reads=[BV], writes=[bv0])
            else:
                kb.dma("sp", Y0[:], y0[:, :, t0:t0 + 128], reads=[y0], writes=[Y0])
                kb.dma("sp", BV0[:], bv0[:, :, t0:t0 + 128], reads=[bv0], writes=[BV0])
                kb.op("dve", lambda e: e.tensor_tensor(out=YT[:], in0=YT[:], in1=Y0[:], op=ALU.add), reads=[YT, Y0], writes=[YT])
                bmm(ps_z, YT, YT)
                kb.op("dve", lambda e: e.scalar_tensor_tensor(out=YT[:], in0=ps_z[:, 0:512].rearrange("p (a t) -> p a t", a=4), scalar=-1.0 / 64.0, in1=YT[:], op0=ALU.mult, op1=ALU.add), reads=[ps_z, YT], writes=[YT])
                kb.op("dve", lambda e: e.tensor_tensor(out=t2[:], in0=YT[:], in1=YT[:], op=ALU.mult), reads=[YT], writes=[t2])
                bmm(ps_z, t2, t2)
                kb.op("act", lambda e: e.activation(out=t2[:], in_=ps_z[:, 0:512].rearrange("p (a t) -> p a t", a=4), func=AF.Sqrt, scale=1.0 / 64.0, bias=eps_gn), reads=[ps_z], writes=[t2])
                kb.op("dve", lambda e: e.reciprocal(out=t2[:], in_=t2[:]), reads=[t2], writes=[t2])
                kb.op("dve", lambda e: e.tensor_tensor(out=YT[:], in0=YT[:], in1=t2[:], op=ALU.mult), reads=[YT, t2], writes=[YT])
                kb.op("dve", lambda e: e.tensor_tensor(out=YT[:], in0=YT[:], in1=pb(P_LNW), op=ALU.mult), reads=[YT, prm_s], writes=[YT])
                kb.op("dve", lambda e: e.tensor_tensor(out=YT[:], in0=YT[:], in1=pb(P_LNB), op=ALU.add), reads=[YT, prm_s], writes=[YT])
                kb.op("dve", lambda e: e.tensor_tensor(out=BV[:], in0=BV[:], in1=BV0[:], op=ALU.add), reads=[BV, BV0], writes=[BV])
                kb.op("dve", lambda e: e.tensor_tensor(out=YT[:], in0=YT[:], in1=BV[:], op=ALU.add), reads=[YT, BV], writes=[YT])
                kb.dma("sp", p1[:, hq * 4:hq * 4 + 4, sc0:sc0 + 128], YT[:], reads=[YT], writes=[p1])


def rwkv_b_consts():
    p = np.arange(128)[:, None]; q = np.arange(128)[None, :]
    cst = np.zeros((128, 2, 128), np.float32)
    cst[:, 0, :] = (p // 64 == q // 64)
    cst[:, 1, :] = (p == q)
    msk = np.zeros((128, 2, 640), np.float32)
    for d, (strict, incl) in enumerate((((p < q), (p <= q)), ((p > q), (p >= q)))):
        msk[:, d, 0:128] = strict; msk[:, d, 128:256] = incl; msk[:, d, 256:384] = strict; msk[:, d, 384:512] = incl
        msk[:, d, 512:640] = strict.T
    rmk = np.ones((128, 512), np.float32); rmk[:, ::128] = 0.0
    return cst, msk, rmk


def emit_ada(kb, cT, w, bia, mods_all):
    NCHG = 96
    c32 = kb.sb([128, 16, 2]); cb = kb.sb([128, 16, 2], BF16)
    bs = kb.sb([128, 384]); osb = kb.sb([128, 384, 2])
    wts = [kb.sb([128, 4, 16, 128], BF16) for _ in range(3)]
    pss = [kb.ps([128, 512]) for _ in range(2)]
    kb.dma("sp", c32[:], cT[:], reads=[cT], writes=[c32])
    kb.dma("sp", bs[:], bia[:], reads=[bia], writes=[bs])
    kb.op("act", lambda e: e.activation(out=cb[:], in_=c32[:], func=AF.Silu), reads=[c32], writes=[cb])
    for g in range(NCHG):
        wt = wts[g % 3]
        kb.dma("pool", wt[:].rearrange("p a k n -> p (a k n)"), w[g], reads=[w], writes=[wt])
        for a in range(4):
            j = g * 4 + a
            ps = pss[j % 2]
            for kc in range(16):
                kb.op("pe", lambda e: e.matmul(ps[:, 0:2], wt[:, a, kc, :], cb[:, kc, :], start=(kc == 0), stop=(kc == 15)), reads=[wt, cb], writes=[ps])
            kb.op("act", lambda e: e.activation(out=osb[:, j, :], in_=ps[:, 0:2], func=AF.Identity, bias=bs[:, j:j + 1]), reads=[ps, bs], writes=[osb])
    kb.dma("sp", mods_all[:], osb[:], reads=[osb], writes=[mods_all])


def perm_fwd(kb, eng, dst_t, dst_ap, src_t, src_ap):
    kb.op(eng, lambda e: e.tensor_copy(out=dst_ap.rearrange("p (c r) -> p c r", c=64, r=128), in_=src_ap.rearrange("p (r c) -> p c r", r=128, c=64)), reads=[src_t], writes=[dst_t])


def perm_inv(kb, eng, dst_t, dst_ap, src_t, src_ap):
    kb.op(eng, lambda e: e.tensor_copy(out=dst_ap.rearrange("p (r c) -> p c r", r=128, c=64), in_=src_ap.rearrange("p (c r) -> p c r", c=64, r=128)), reads=[src_t], writes=[dst_t])


def emit_gla_prep(kb, qkv_d, gl_d, ident, qk_s, ktok_s, vtok_s, glT_s, col_major):
    R = kb.sb([128, NS]); S = kb.sb([128, NS]); ident_s = kb.sb([128, 128])
    ev = [kb.sb([128, 512]) for _ in range(2)]; pst = [kb.ps() for _ in range(2)]
    kb.dma("sp", ident_s[:], ident[:], reads=[ident], writes=[ident_s])
    n = 0
    for f in range(32):
        kb.dma("sp", R[:], qkv_d[:, f, 0:NS], reads=[qkv_d], writes=[R])
        kb.op("act", lambda e: e.activation(out=S[:, 0:CTXC], in_=R[:, LATC:LATC + CTXC], func=AF.Identity), reads=[R], writes=[S])
        if col_major:
            perm_fwd(kb, "pool", S, S[:, CTXC:NS], R, R[:, 0:LATC])
        else:
            kb.op("pool", lambda e: e.tensor_copy(out=S[:, CTXC:NS], in_=R[:, 0:LATC]), reads=[R], writes=[S])
        if f < 16:
            kb.dma("sp", qk_s[:, f, :], S[:], reads=[S], writes=[qk_s])
        if f >= 8:
            tok, fc = (ktok_s, f - 8) if f < 16 else (vtok_s, f - 16)
            for b0 in range(0, NS // 128, 4):
                nb = min(4, NS // 128 - b0)
                ps = pst[n % 2]; e_ = ev[n % 2]; n += 1
                for j in range(nb):
                    kb.op("pe", lambda e: e.matmul(ps[:, j * 128:(j + 1) * 128], S[:, (b0 + j) * 128:(b0 + j + 1) * 128], ident_s[:], start=True, stop=True), reads=[S, ident_s], writes=[ps])
                kb.op("dve", lambda e: e.tensor_copy(out=e_[:, :nb * 128], in_=ps[:, :nb * 128]), reads=[ps], writes=[e_])
                kb.dma("sp", tok[b0 * 128:(b0 + nb) * 128, fc * 128:(fc + 1) * 128].rearrange("(j p) c -> p j c", p=128), e_[:, :nb * 128].rearrange("p (j c) -> p j c", c=128), reads=[e_], writes=[tok])
    for d in range(2):
        kb.dma("sp", R[0:16, :], gl_d[d * 16:(d + 1) * 16, 0:NS], reads=[gl_d], writes=[R])
        kb.op("act", lambda e: e.activation(out=S[0:16, 0:CTXC], in_=R[0:16, LATC:LATC + CTXC], func=AF.Identity), reads=[R], writes=[S])
        if col_major:
            perm_fwd(kb, "pool", S, S[0:16, CTXC:NS], R, R[0:16, 0:LATC])
        else:
            kb.op("pool", lambda e: e.tensor_copy(out=S[0:16, CTXC:NS], in_=R[0:16, 0:LATC]), reads=[R], writes=[S])
        kb.dma("sp", glT_s[:, d, :], S[0:16, :], reads=[S], writes=[glT_s])


def emit_gla_post(kb, pT_s, p1_d, col_major):
    R = kb.sb([128, NS]); S = kb.sb([128, NS])
    for f in range(16):
        kb.dma("sp", S[:], pT_s[:, f, :], reads=[pT_s], writes=[S])
        kb.op("act", lambda e: e.activation(out=R[:, LATC:LATC + CTXC], in_=S[:, 0:CTXC], func=AF.Identity), reads=[S], writes=[R])
        if col_major:
            perm_inv(kb, "pool", R, R[:, 0:LATC], S, S[:, CTXC:NS])
        else:
            kb.op("pool", lambda e: e.tensor_copy(out=R[:, 0:LATC], in_=S[:, CTXC:NS]), reads=[S], writes=[R])
        kb.dma("sp", p1_d[:, f, 0:NS], R[:], reads=[R], writes=[p1_d])


def build_fused():
    kb = KB()
    I = lambda name, shape: kb.dram(name, shape, kind="ExternalInput")
    N = lambda name, shape: kb.dram(name, shape)
    h = [I("h0", [128, 16, TCP]), N("h1", [128, 16, TCP]), N("h2", [128, 16, TCP]), N("h3", [128, 16, TCP]), kb.dram("hout", [128, 16, TCP], kind="ExternalOutput")]
    cT = I("cT", [128, 16, 2]); ada_w = I("ada_w", [96, 128, 8192]); ada_b = I("ada_b", [128, 384])
    mods_all = N("mods_all", [128, 384, 2])
    ng = [I(f"ng{l}", [128, 4, 16]) for l in range(4)]
    CW = [dict(wout=I(f"wout{l}", [16, 128, 2048]), wg=I(f"wg{l}", [44, 128, 2048]), wu=I(f"wu{l}", [44, 128, 2048]), wo=I(f"wo{l}", [16, 2, 128, 2816]), bout=I(f"bout{l}", [128, 16])) for l in range(4)]
    LR = [dict(w=I(f"lw{j}", [32, 128, 2048]), bin_=I(f"lb{j}", [128, 32]), cw=I(f"lcw{j}", [128, 16, 4]), cb=I(f"lcb{j}", [128, 16]), gw=I(f"lgw{j}", [128, 16, 2, 2, 128]),
               gb=I(f"lgb{j}", [128, 16, 2, 2]), ll=I(f"lll{j}", [128, 16, 2])) for j in range(2)]
    GL = dict(w=I("gw_in", [48, 128, 2048]), w1=I("gw1", [128, 512]), br=I("gbr", [128, 16]), w2=I("gw2", [16, 2, 1024]), gbF=I("ggbF", [128, 2, 8]), gbT=I("ggbT", [128, 2, 1024]),
              ngT=I("gngT", [128, 2048]), masks=I("gmasks", [128, 2, 2, 128]), rmask=I("grmask", [128, 512]))
    ident = I("ident", [128, 128])
    RW = dict(mu=I("rmu", [128, 6, 16]), w=I("rw", [48, 128, 2048]), wl=I("rwl", [128, 6144]), g1=I("rg1", [128, 4096]), g2=I("rg2", [16, 128, 256]), w2=I("rw2", [96, 2, 2048]),
              a2=I("ra2", [96, 2, 2048]), prm=I("rprm", [128, 9, 16]), cst=I("rcst", [128, 2, 128]), msk=I("rmsk", [128, 2, 640]), rmk=I("rrmk", [128, 512]))
    t_p1 = N("t_p1", [128, 16, TCP]); t_p2 = N("t_p2", [128, 16, TCP]); t_rec = N("t_rec", [128, 16, TCP])
    qkv_d = N("qkv_d", [128, 32, TCP]); gl_d = N("gl_d", [32, TCP]); qk_s = N("qk_s", [128, 16, NS]); ktok_s = N("ktok_s", [NS, 1024]); vtok_s = N("vtok_s", [NS, 2048])
    glT_s = N("glT_s", [16, 2, NS]); o0 = N("o0", [NS, 512]); pT_s = N("pT_s", [128, 16, NS])
    rkv_d = N("rkv_d", [128, 48, TCP]); lora_d = N("lora_d", [96, 4, TCP]); us = N("us", [128, 16, US_W]); y0 = N("y0", [128, 4, NS]); bv0 = N("bv0", [128, 4, NS])

    def phase(fn):
        kb.begin_phase(); fn(); kb.end_phase()

    phase(lambda: emit_ada(kb, cT, ada_w, ada_b, mods_all))
    for l in range(4):
        kind, j = l % 3, l // 3
        col_major = l % 2 == 1
        mo = l * 96
        if kind == 0:
            P = LR[j]
            phase(lambda: emit_lru_a(kb, h[l], mods_all, ng[l], P["w"], P["bin_"], t_p2, t_rec, mods_off=mo))
            phase(lambda: emit_lru_b(kb, t_rec, P["cw"], P["cb"], P["gw"], P["gb"], P["ll"], t_p1, col_major))
        elif kind == 1:
            phase(lambda: emit_gla_a(kb, h[l], mods_all, ng[l], GL["w"], GL["w1"], GL["br"], qkv_d, t_p2, gl_d, mods_off=mo))
            phase(lambda: emit_gla_prep(kb, qkv_d, gl_d, ident, qk_s, ktok_s, vtok_s, glT_s, col_major))
            for hd in range(4):
                phase(lambda: emit_gla_b(kb, hd, qk_s, ktok_s, vtok_s, glT_s, GL["w2"], GL["gbF"], GL["gbT"], GL["ngT"], GL["masks"], GL["rmask"], ident, o0, pT_s))
            phase(lambda: emit_gla_post(kb, pT_s, t_p1, col_major))
        else:
            phase(lambda: emit_rwkv_a(kb, h[l], mods_all, ng[l], RW["mu"], RW["w"], RW["wl"], RW["g1"], RW["g2"], rkv_d, lora_d, t_p2, us, mods_off=mo))
            for hq in range(4):
                phase(lambda: emit_rwkv_b(kb, hq, rkv_d, lora_d, RW["w2"], RW["a2"], RW["prm"], RW["cst"], RW["msk"], RW["rmk"], y0, bv0, t_p1))
        C = CW[l]
        phase(lambda: emit_c(kb, h[l], t_p1, t_p2, mods_all, ng[l], C["bout"], C["wout"], C["wg"], C["wu"], C["wo"], h[l + 1], mods_off=mo))
    kb.finish([h[4]])
    kb.close()
    return kb


def fused_inputs(inp):
    f32 = np.float32
    A = lambda x: np.ascontiguousarray(x, dtype=f32)
    aw = inp["ada_w"].reshape(4, 16, 128, 96, 128).transpose(0, 3, 2, 1, 4).reshape(384, 128, 2048)
    ada_w = A(aw.reshape(96, 4, 128, 2048).transpose(0, 2, 1, 3).reshape(96, 128, 8192))
    ada_b = A(inp["ada_b"].reshape(384, 128).T)
    shared = dict(ada_w=ada_w, ada_b=ada_b)
    for l in range(4):
        shared[f"ng{l}"] = A(inp["norm_g"][l].reshape(4, 16, 128).transpose(2, 0, 1))
    def cw(l, w_out, b_out):
        W = prep_c_weights(w_out, inp["ffn_w_in"][l], inp["ffn_w_out"][l])
        shared[f"wout{l}"] = W["wout"]; shared[f"wg{l}"] = W["wg"]; shared[f"wu{l}"] = W["wu"]; shared[f"wo{l}"] = W["wo"]
        shared[f"bout{l}"] = A(b_out.reshape(16, 128).T)
    z = np.zeros(2048, f32)
    cw(0, inp["lru_w_out"][0], inp["lru_b_out"][0]); cw(1, inp["gla_w_out"][0], z); cw(2, inp["rwkv_w_out"][0], z); cw(3, inp["lru_w_out"][1], inp["lru_b_out"][1])
    for j in range(2):
        shared[f"lw{j}"] = A(inp["lru_w_in"][j].reshape(16, 128, 32, 128).transpose(2, 1, 0, 3).reshape(32, 128, 2048))
        shared[f"lb{j}"] = A(inp["lru_b_in"][j].reshape(32, 128).T)
        shared[f"lcw{j}"] = A(inp["lru_conv_w"][j].reshape(4, 16, 128).transpose(2, 1, 0))
        shared[f"lcb{j}"] = A(inp["lru_conv_b"][j].reshape(16, 128).T)
        shared[f"lgw{j}"] = A(inp["lru_gate_w"][j].transpose(3, 2, 0, 1, 4))
        shared[f"lgb{j}"] = A(inp["lru_gate_b"][j].reshape(2, 2, 16, 128).transpose(3, 2, 0, 1))
        shared[f"lll{j}"] = A(inp["lru_log_lambda"][j].reshape(2, 16, 128).transpose(2, 1, 0))
    shared["gw_in"] = A(inp["gla_w_in"][0].reshape(16, 128, 48, 128).transpose(2, 1, 0, 3).reshape(48, 128, 2048))
    shared["gw1"] = A(inp["gla_gate_w1"][0].transpose(1, 0, 2).reshape(16, 128, 32).transpose(1, 0, 2).reshape(128, 512))
    shared["gbr"] = A(inp["gla_b_r"][0].reshape(16, 128).T)
    shared["gw2"] = A(inp["gla_gate_w2"][0].transpose(1, 0, 2))
    gb = inp["gla_gate_b"][0]
    shared["ggbF"] = A(gb.reshape(2, 8, 128).transpose(2, 0, 1))
    shared["ggbT"] = A(np.broadcast_to(gb[None], (128, 2, 1024)))
    shared["gngT"] = A(np.broadcast_to(inp["gla_norm_g"][0][None], (128, 2048)))
    masks, rmask = gla_b_consts()
    shared["gmasks"] = masks; shared["grmask"] = rmask
    shared["ident"] = np.eye(128, dtype=f32)
    shared["rmu"] = A(inp["rwkv_mu"][0].reshape(6, 16, 128).transpose(2, 0, 1))
    shared["rw"] = A(inp["rwkv_w_rkv"][0].reshape(3, 16, 128, 16, 128).transpose(0, 3, 2, 1, 4).reshape(48, 128, 2048))
    wl = np.concatenate([inp["rwkv_w1"][0][0], inp["rwkv_w1"][0][1], inp["rwkv_a1"][0][0], inp["rwkv_a1"][0][1]], axis=1)
    shared["rwl"] = A(wl.reshape(16, 128, 384).transpose(1, 0, 2).reshape(128, 6144))
    shared["rg1"] = A(inp["rwkv_g1"][0].reshape(16, 128, 256).transpose(1, 0, 2).reshape(128, 4096))
    shared["rg2"] = A(inp["rwkv_g2"][0].reshape(2, 128, 16, 128).transpose(2, 1, 0, 3).reshape(16, 128, 256))
    shared["rw2"] = A(inp["rwkv_w2"][0].transpose(1, 0, 2)); shared["ra2"] = A(inp["rwkv_a2"][0].transpose(1, 0, 2))
    def pp(v):
        return v.reshape(16, 128).T
    shared["rprm"] = A(np.stack([pp(inp["rwkv_k_k"][0]), pp(inp["rwkv_k_a"][0]), pp(inp["rwkv_r_k"][0].reshape(-1)), pp(inp["rwkv_ln_w"][0]), pp(inp["rwkv_ln_b"][0]),
                                 pp(inp["rwkv_w0"][0][0]), pp(inp["rwkv_w0"][0][1]), pp(inp["rwkv_a0"][0][0]), pp(inp["rwkv_a0"][0][1])], axis=1))
    cst, msk, rmk = rwkv_b_consts()
    shared["rcst"] = cst; shared["rmsk"] = msk; shared["rrmk"] = rmk
    maps = []
    for b in range(2):
        a = np.zeros((2048, TCP), f32)
        a[:, 0:LATC] = inp["x"][b].T; a[:, LATC:LATC + CTXC] = inp["ctx"][b].T
        cT = np.stack([inp["c_ctx"], inp["c"][b]], axis=1)
        maps.append(dict(h0=fm(a), cT=A(cT.reshape(16, 128, 2).transpose(1, 0, 2)), **shared))
    return maps

def kernel(**inp):
    inp = {k: np.asarray(v) for k, v in inp.items()}
    kb = build_fused()
    maps = fused_inputs(inp)
    res = run(kb, maps)
    out = np.stack([res[b]["hout"].transpose(1, 0, 2).reshape(2048, TCP)[:, 0:LATC].T for b in range(2)])
    return np.ascontiguousarray(out.astype(np.float32))
```
